# Optimizing a Trainium2 kernel written in Bass

```python
import math
import jax, jax.numpy as jnp
from jax import lax
import numpy as np

D_MODEL = 2048
BATCH = 4
SEQ = 4096
DEPTH = 1

CHUNK = 64
Q_BLOCK = 128

ML_HEADS = 4
ML_DQK = 128
ML_DV = 256
ML_CONV = 4
GATE_CAP = 15.0

DA_HEADS = 8
DA_DH = 64
DA_DV = 2 * DA_DH
ROPE_DIM = DA_DH // 4
ROPE_THETA = 500000.0

ML_WIDTH = ML_HEADS * ML_DV
DA_WIDTH = DA_HEADS * DA_DV
MIX_WIDTH = ML_WIDTH + DA_WIDTH

IN_SPLITS = (ML_HEADS * ML_DQK, ML_HEADS * ML_DQK, ML_WIDTH, ML_WIDTH, ML_HEADS, ML_HEADS,
             DA_HEADS * 2 * DA_DH, DA_HEADS * 2 * DA_DH, DA_WIDTH)
N_IN = (4 * ML_HEADS * ML_DQK // 2) + 2 * ML_WIDTH + 2 * ML_HEADS + 4 * DA_HEADS * DA_DH + DA_WIDTH

D_FF = ((8 * D_MODEL + 3 * 256 - 1) // (3 * 256)) * 256

kernel_name = "hybrid_mlstm_diffattn_chunk_causal_block"


def rms_norm(x, w, eps=1e-6):
    xf = x.astype(jnp.float32)
    y = xf * lax.rsqrt(jnp.mean(xf * xf, axis=-1, keepdims=True) + eps)
    return (y * w.astype(jnp.float32)).astype(x.dtype)


def causal_dwconv(x, w, b):
    k = w.shape[0]
    y = lax.conv_general_dilated(x, w[:, None, :].astype(x.dtype), window_strides=(1,),
                                 padding=[(k - 1, 0)], dimension_numbers=('NWC', 'WIO', 'NWC'),
                                 feature_group_count=x.shape[-1])
    return y + b.astype(x.dtype)


def partial_rope(x, positions):
    half = ROPE_DIM // 2
    inv_freq = ROPE_THETA ** (-jnp.arange(0, ROPE_DIM, 2, dtype=jnp.float32) / ROPE_DIM)
    ang = positions.astype(jnp.float32)[..., None] * inv_freq
    cos = jnp.cos(ang)[:, :, None, None, :]
    sin = jnp.sin(ang)[:, :, None, None, :]
    xf = x.astype(jnp.float32)
    x1 = xf[..., :half]
    x2 = xf[..., half:ROPE_DIM]
    rot = jnp.concatenate([x1 * cos - x2 * sin, x2 * cos + x1 * sin, xf[..., ROPE_DIM:]], axis=-1)
    return rot.astype(x.dtype)


def mlstm_chunkwise(q, k, v, li, lf):
    B, H, S, _ = q.shape
    nc = S // CHUNK
    f32 = jnp.float32
    q = q.astype(f32).reshape(B, H, nc, CHUNK, ML_DQK) * (ML_DQK ** -0.5)
    k = k.astype(f32).reshape(B, H, nc, CHUNK, ML_DQK)
    v = v.astype(f32).reshape(B, H, nc, CHUNK, ML_DV)
    li = li.reshape(B, H, nc, CHUNK)
    lf = lf.reshape(B, H, nc, CHUNK)
    b = jnp.cumsum(lf, axis=-1)
    b_end = b[..., -1]
    a = b_end[..., None] - b + li
    m_loc = jnp.max(a, axis=-1)
    w = jnp.exp(a - m_loc[..., None])
    C_loc = jnp.einsum('bhclv,bhclk->bhcvk', v * w[..., None], k)
    n_loc = jnp.einsum('bhcl,bhclk->bhck', w, k)

    def step(carry, inp):
        C, n, m = carry
        Cl, nl, ml, be = inp
        m_new = jnp.maximum(be + m, ml)
        f_old = jnp.exp(be + m - m_new)
        f_loc = jnp.exp(ml - m_new)
        C_new = f_old[..., None, None] * C + f_loc[..., None, None] * Cl
        n_new = f_old[..., None] * n + f_loc[..., None] * nl
        return (C_new, n_new, m_new), (C, n, m)

    init = (jnp.zeros((B, H, ML_DV, ML_DQK), f32), jnp.zeros((B, H, ML_DQK), f32), jnp.zeros((B, H), f32))
    xs = (jnp.moveaxis(C_loc, 2, 0), jnp.moveaxis(n_loc, 2, 0), jnp.moveaxis(m_loc, 2, 0), jnp.moveaxis(b_end, 2, 0))
    _, (C_prev, n_prev, m_prev) = lax.scan(step, init, xs)
    C_prev = jnp.moveaxis(C_prev, 0, 2)
    n_prev = jnp.moveaxis(n_prev, 0, 2)
    m_prev = jnp.moveaxis(m_prev, 0, 2)

    causal = jnp.tril(jnp.ones((CHUNK, CHUNK), dtype=bool))
    log_d = jnp.where(causal, b[..., :, None] - b[..., None, :] + li[..., None, :], -jnp.inf)
    g = b + m_prev[..., None]
    m_t = jnp.maximum(g, jnp.max(log_d, axis=-1))
    d = jnp.exp(log_d - m_t[..., None])
    inter = jnp.exp(g - m_t)
    qk = jnp.einsum('bhctd,bhcsd->bhcts', q, k) * d
    num = inter[..., None] * jnp.einsum('bhcvk,bhctk->bhctv', C_prev, q) + jnp.einsum('bhcts,bhcsv->bhctv', qk, v)
    den = inter * jnp.einsum('bhck,bhctk->bhct', n_prev, q) + jnp.sum(qk, axis=-1)
    h = num / jnp.maximum(jnp.abs(den), jnp.exp(-m_t))[..., None]
    return h.reshape(B, H, S, ML_DV)


def diff_attention(q, k, v, lam):
    S = q.shape[3]
    scale = DA_DH ** -0.5
    outs = []
    for j in range(S // Q_BLOCK):
        q0 = j * Q_BLOCK
        kend = q0 + Q_BLOCK
        qb = q[:, :, :, q0:kend]
        kb = k[:, :, :, :kend]
        vb = v[:, :, :kend]
        s = jnp.einsum('bhmqd,bhmkd->bhmqk', qb, kb).astype(jnp.float32) * scale
        q_chunk = (q0 + jnp.arange(Q_BLOCK)) // CHUNK
        k_chunk = jnp.arange(kend) // CHUNK
        allowed = k_chunk[None, :] <= q_chunk[:, None]
        p = jax.nn.softmax(jnp.where(allowed, s, -jnp.inf), axis=-1)
        a = p[:, :, 0] - lam * p[:, :, 1]
        outs.append(jnp.einsum('bhqk,bhkd->bhqd', a.astype(v.dtype), vb))
    return jnp.concatenate(outs, axis=2)


def setup_inputs(seed: int = 0) -> dict:
    key = jax.random.key(seed)
    ks = jax.random.split(key, 24)
    f32 = jnp.float32
    nrm = lambda k, shape, s: jax.random.normal(k, shape, f32) * s
    x = jax.random.normal(ks[0], (BATCH, SEQ, D_MODEL), f32)
    c = jax.random.normal(ks[1], (BATCH, D_MODEL), f32)
    offset = jax.random.randint(ks[2], (BATCH, 1), 0, 4096, dtype=jnp.int32)
    positions = offset + jnp.arange(SEQ, dtype=jnp.int32)[None, :]
    gate_b = jnp.concatenate([nrm(ks[10], (DEPTH, ML_HEADS), 0.1),
                              3.0 + nrm(ks[11], (DEPTH, ML_HEADS), 0.5)], axis=-1)
    return {
        "x": x,
        "c": c,
        "positions": positions,
        "norm1_w": 1.0 + nrm(ks[3], (DEPTH, D_MODEL), 0.02),
        "norm2_w": 1.0 + nrm(ks[4], (DEPTH, D_MODEL), 0.02),
        "w_ada": nrm(ks[5], (DEPTH, D_MODEL, 6 * D_MODEL), 0.5 * D_MODEL ** -0.5),
        "b_ada": nrm(ks[6], (DEPTH, 6 * D_MODEL), 0.02),
        "w_in": nrm(ks[7], (DEPTH, D_MODEL, N_IN), D_MODEL ** -0.5),
        "mlstm_conv_w": nrm(ks[8], (DEPTH, ML_CONV, 2 * ML_HEADS * ML_DQK), 0.5),
        "mlstm_conv_b": nrm(ks[9], (DEPTH, 2 * ML_HEADS * ML_DQK), 0.02),
        "mlstm_gate_b": gate_b,
        "mlstm_norm_w": 1.0 + nrm(ks[12], (DEPTH, ML_HEADS, ML_DV), 0.02),
        "q_norm_w": 1.0 + nrm(ks[13], (DEPTH, DA_DH), 0.02),
        "k_norm_w": 1.0 + nrm(ks[14], (DEPTH, DA_DH), 0.02),
        "lambda_q1": nrm(ks[15], (DEPTH, DA_DH), 0.1),
        "lambda_k1": nrm(ks[16], (DEPTH, DA_DH), 0.1),
        "lambda_q2": nrm(ks[17], (DEPTH, DA_DH), 0.1),
        "lambda_k2": nrm(ks[18], (DEPTH, DA_DH), 0.1),
        "subln_w": 1.0 + nrm(ks[19], (DEPTH, DA_DV), 0.02),
        "w_out": nrm(ks[20], (DEPTH, MIX_WIDTH, D_MODEL), MIX_WIDTH ** -0.5),
        "w_gate_up": nrm(ks[21], (DEPTH, D_MODEL, 2 * D_FF), D_MODEL ** -0.5),
        "w_down": nrm(ks[22], (DEPTH, D_FF, D_MODEL), D_FF ** -0.5),
    }


def reference(x, c, positions, norm1_w, norm2_w, w_ada, b_ada, w_in, mlstm_conv_w, mlstm_conv_b,
              mlstm_gate_b, mlstm_norm_w, q_norm_w, k_norm_w, lambda_q1, lambda_k1, lambda_q2,
              lambda_k2, subln_w, w_out, w_gate_up, w_down):
    B, S, _ = x.shape
    split_pts = np.cumsum(IN_SPLITS)[:-1].tolist()
    qk_w = ML_HEADS * ML_DQK
    for layer in range(DEPTH):
        lambda_init = 0.8 - 0.6 * math.exp(-0.3 * layer)
        mod = jnp.einsum('bd,de->be', jax.nn.silu(c), w_ada[layer]) + b_ada[layer]
        shift1, scale1, gate1, shift2, scale2, gate2 = jnp.split(mod[:, None, :], 6, axis=-1)

        h = rms_norm(x, norm1_w[layer]) * (1 + scale1) + shift1
        proj = h @ w_in[layer]
        mq, mk, mv, mo, mi, mf, dq, dk, dv = jnp.split(proj, split_pts, axis=-1)

        mqk = jax.nn.silu(causal_dwconv(jnp.concatenate([mq, mk], axis=-1), mlstm_conv_w[layer], mlstm_conv_b[layer]))
        mq, mk = mqk[..., :qk_w], mqk[..., qk_w:]
        gates = jnp.concatenate([mi, mf], axis=-1).astype(jnp.float32) + mlstm_gate_b[layer].astype(jnp.float32)
        gates = GATE_CAP * jnp.tanh(gates / GATE_CAP)
        li = jnp.transpose(gates[..., :ML_HEADS], (0, 2, 1))
        lf = jnp.transpose(jax.nn.log_sigmoid(gates[..., ML_HEADS:]), (0, 2, 1))
        to_heads = lambda t, d: jnp.transpose(t.reshape(B, S, ML_HEADS, d), (0, 2, 1, 3))
        h_ml = mlstm_chunkwise(to_heads(mq, ML_DQK), to_heads(mk, ML_DQK), to_heads(mv, ML_DV), li, lf)
        h_ml = rms_norm(jnp.transpose(h_ml, (0, 2, 1, 3)), mlstm_norm_w[layer])
        h_ml = h_ml.reshape(B, S, ML_WIDTH).astype(x.dtype) * jax.nn.sigmoid(mo)

        dq = partial_rope(rms_norm(dq.reshape(B, S, DA_HEADS, 2, DA_DH), q_norm_w[layer]), positions)
        dk = partial_rope(rms_norm(dk.reshape(B, S, DA_HEADS, 2, DA_DH), k_norm_w[layer]), positions)
        dq = jnp.transpose(dq, (0, 2, 3, 1, 4))
        dk = jnp.transpose(dk, (0, 2, 3, 1, 4))
        dv = jnp.transpose(dv.reshape(B, S, DA_HEADS, DA_DV), (0, 2, 1, 3))
        lam = (jnp.exp(jnp.sum(lambda_q1[layer].astype(jnp.float32) * lambda_k1[layer].astype(jnp.float32)))
               - jnp.exp(jnp.sum(lambda_q2[layer].astype(jnp.float32) * lambda_k2[layer].astype(jnp.float32)))
               + lambda_init)
        o_da = diff_attention(dq, dk, dv, lam)
        o_da = rms_norm(o_da, subln_w[layer]) * (1.0 - lambda_init)
        o_da = jnp.transpose(o_da, (0, 2, 1, 3)).reshape(B, S, DA_WIDTH)

        mix = jnp.concatenate([h_ml, o_da], axis=-1) @ w_out[layer]
        x = x + gate1 * mix

        h = rms_norm(x, norm2_w[layer]) * (1 + scale2) + shift2
        g, u = jnp.split(h @ w_gate_up[layer], 2, axis=-1)
        x = x + gate2 * ((jax.nn.silu(g) * u) @ w_down[layer])
    return x
```

```python
import math
import os
import numpy as np
import concourse.bass as bass
import concourse.mybir as mybir
from concourse.bass_utils import run_bass_kernel_spmd

F32 = mybir.dt.float32
BF16 = mybir.dt.bfloat16
I32 = mybir.dt.int32
AF = mybir.ActivationFunctionType
ALU = mybir.AluOpType
AX = mybir.AxisListType

SEM_EPOCH = 16000
import os as _os
SAME_ENGINE_SYNC = _os.environ.get("K_SES", "1") == "1"


class Buf:
    __slots__ = ("name", "w", "rd", "dsem", "dcnt", "excl")

    def __init__(self, name, excl=False):
        self.name = name
        self.excl = excl
        self.w = None
        self.rd = {}
        self.dsem = None
        self.dcnt = 0


class Op:
    __slots__ = ("eng", "fn", "waits", "need_sig", "sigidx", "dma_sem")

    def __init__(self, eng, fn):
        self.eng = eng
        self.fn = fn
        self.waits = []
        self.need_sig = False
        self.sigidx = None
        self.dma_sem = None


class Prog:
    ENGS = ("pe", "act", "dve", "pool", "sp")

    def __init__(self, nc):
        self.nc = nc
        self.ops = {e: [] for e in self.ENGS}
        self.nsem = 0
        self.bufs = []

    def buf(self, name, excl=False):
        b = Buf(name, excl)
        self.bufs.append(b)
        return b

    def _newsem(self, name):
        self.nsem += 1
        return self.nc.alloc_semaphore(f"s{self.nsem}_{name}")

    def _collect(self, op, reads, writes, relaxed):
        toks = []
        for b in reads:
            if b.w is not None:
                toks.append((b.w, False))
            if b.excl:
                for k, t in b.rd.items():
                    if k != op.eng:
                        toks.append((t, False))
        for b in writes:
            if b.w is not None:
                toks.append((b.w, True))
            for t in b.rd.values():
                toks.append((t, False))
        for t, waw in toks:
            if t[0] == "op":
                p = t[1]
                if p is op:
                    continue
                if p.eng == op.eng and (p.eng == "pe" or not SAME_ENGINE_SYNC or (waw and relaxed)):
                    continue
                p.need_sig = True
            op.waits.append(t)

    def op(self, eng, fn, reads=(), writes=(), relaxed=False):
        o = Op(eng, fn)
        self._collect(o, reads, writes, relaxed)
        tok = ("op", o)
        for b in reads:
            b.rd[eng] = tok
        for b in writes:
            b.w = tok
            b.rd = {}
        self.ops[eng].append(o)
        return o

    def dma(self, eng, out_ap, in_ap, reads=(), writes=(), sem_buf=None):
        sb = sem_buf or (writes[0] if writes else reads[0])
        if sb.dsem is None:
            sb.dsem = self._newsem(sb.name)
        sb.dcnt += 16
        sem, cnt = sb.dsem, sb.dcnt

        def fn(e, out_ap=out_ap, in_ap=in_ap):
            return e.dma_start(out=out_ap, in_=in_ap)

        o = Op(eng, fn)
        o.dma_sem = sem
        self._collect(o, reads, writes, False)
        tok = ("sem", sem, cnt)
        for b in reads:
            b.rd[("d", id(sem))] = tok
        for b in writes:
            b.w = tok
            b.rd = {}
        self.ops[eng].append(o)
        return o

    def barrier(self):
        toks = []
        for b in self.bufs:
            if b.w is not None:
                toks.append(b.w)
            toks.extend(b.rd.values())
        for e in self.ENGS:
            for o in reversed(self.ops[e]):
                if o.fn is not None and o.dma_sem is None:
                    toks.append(("op", o))
                    break
        for e in self.ENGS:
            o = Op(e, None)
            for t in toks:
                if t[0] == "op":
                    if t[1].eng == e:
                        continue
                    t[1].need_sig = True
                o.waits.append(t)
            self.ops[e].append(o)
        for b in self.bufs:
            b.w = None
            b.rd = {}

    def final_wait(self, eng, bufs):
        o = Op(eng, None)
        for b in bufs:
            if b.w is not None:
                o.waits.append(b.w)
                if b.w[0] == "op":
                    b.w[1].need_sig = True
        self.ops[eng].append(o)

    def emit(self):
        nc = self.nc
        eng_sems = {}
        for e in self.ENGS:
            n = 0
            for o in self.ops[e]:
                if o.need_sig and o.dma_sem is None and o.fn is not None:
                    o.sigidx = n
                    n += 1
            eng_sems[e] = [self._newsem(f"{e}{i}") for i in range((n + SEM_EPOCH - 1) // SEM_EPOCH)]
        self.stats = {e: len(self.ops[e]) for e in self.ENGS}
        self.stats["nsem"] = self.nsem

        def resolve(t):
            if t[0] == "sem":
                return t[1], t[2]
            p = t[1]
            assert p.sigidx is not None, "waiting on op without signal"
            return eng_sems[p.eng][p.sigidx // SEM_EPOCH], p.sigidx % SEM_EPOCH + 1

        def run(e, h):
            waited = {}
            for o in self.ops[e]:
                need = {}
                for t in o.waits:
                    s, v = resolve(t)
                    k = id(s)
                    if waited.get(k, 0) >= v:
                        continue
                    if k not in need or need[k][1] < v:
                        need[k] = (s, v)
                for k, (s, v) in need.items():
                    h.wait_ge(s, v)
                    waited[k] = v
                if o.fn is None:
                    continue
                ins = o.fn(h)
                if o.dma_sem is not None:
                    ins.then_inc(o.dma_sem, 16)
                elif o.need_sig:
                    ins.then_inc(eng_sems[e][o.sigidx // SEM_EPOCH], 1)

        with nc.Block() as block:
            @block.tensor
            def _(h):
                run("pe", h)

            @block.scalar
            def _(h):
                run("act", h)

            @block.vector
            def _(h):
                run("dve", h)

            @block.gpsimd
            def _(h):
                run("pool", h)

            @block.sync
            def _(h):
                run("sp", h)


def MM(out, lhsT, rhs, start=True, stop=True):
    return lambda e: e.matmul(out, lhsT=lhsT, rhs=rhs, start=start, stop=stop)


def TR(out, in_, ident):
    return lambda e: e.transpose(out=out, in_=in_, identity=ident)


def ACTF(out, in_, func, scale=None, bias=None, accum=None):
    kw = {}
    if scale is not None:
        kw["scale"] = scale
    if bias is not None:
        kw["bias"] = bias
    if accum is not None:
        kw["accum_out"] = accum
    return lambda e: e.activation(out=out, in_=in_, func=func, **kw)


def TS(out, in0, s1, s2=None, op0=ALU.mult, op1=None):
    if op1 is None:
        return lambda e: e.tensor_scalar(out=out, in0=in0, scalar1=s1, scalar2=None, op0=op0)
    return lambda e: e.tensor_scalar(out=out, in0=in0, scalar1=s1, scalar2=s2, op0=op0, op1=op1)


def TT(out, in0, in1, op):
    return lambda e: e.tensor_tensor(out=out, in0=in0, in1=in1, op=op)


def STT(out, in0, scalar, in1, op0, op1):
    return lambda e: e.scalar_tensor_tensor(out=out, in0=in0, scalar=scalar, in1=in1, op0=op0, op1=op1)


def CP(out, in_):
    return lambda e: e.tensor_copy(out=out, in_=in_)


def MSET(ap, v):
    return lambda e: e.memset(ap, v)


def RCP(out, in_):
    return lambda e: e.reciprocal(out=out, in_=in_)


D = 2048
KC = 16
TOK = 2048
NTILE = 32
NBLK = 8
DFF = 5632
NFF = 44
N_IN = 6152
EPS = 1e-6
LAMBDA_INIT = 0.8 - 0.6 * math.exp(-0.3 * 0)
ML_W = 770
DA_W = 768
ARENA_BYTES = 211968

_CST = [("c_l", 16), ("b_l", 96), ("n1w", 16), ("n2w", 16), ("cw", 32), ("cb", 8), ("gb", 8), ("nw", 1024),
        ("wqk", 512), ("lamv", 256), ("subw", 1), ("flag", 1), ("pbias", 1), ("invf", 8), ("ident", 128),
        ("tri", 128)]
CO = {}
_o = 0
for _n, _s in _CST:
    CO[_n] = (_o, _s)
    _o += _s
NCST = _o


class Arena:
    def __init__(self, nc):
        self.t = nc.alloc_sbuf_tensor("arena", [128, ARENA_BYTES // 4], F32)
        self.off = 0

    def alloc(self, shape, dtype, parts=128, p0=0):
        n = int(np.prod(shape))
        esz = 2 if dtype == BF16 else 4
        nb = (n * esz + 31) // 32 * 32
        o = self.off
        self.off += nb
        assert self.off <= ARENA_BYTES, f"SBUF arena overflow {self.off}"
        v = self.t[p0:p0 + parts, o // 4:(o + nb) // 4]
        if dtype != F32:
            v = v.bitcast(dtype)
        v = v[:, 0:n]
        if len(shape) > 1:
            names = [f"a{i}" for i in range(len(shape))]
            kw = {nm: int(s) for nm, s in zip(names, shape)}
            v = v.rearrange("p (" + " ".join(names) + ") -> p " + " ".join(names), **kw)
        return v


def build(dbg=False, phases="0ABC"):
    nc = bass.Bass("TRN2", target_bir_lowering=False)
    P = Prog(nc)
    A = Arena(nc)

    def din(name, shape, dt=F32):
        return nc.dram_tensor(name, list(shape), dt, kind="ExternalInput").ap()

    def dscr(name, shape, dt):
        if dbg:
            return nc.dram_tensor(name, list(shape), dt, kind="ExternalOutput").ap()
        return nc.dram_tensor(name, list(shape), dt).ap()

    x_d = din("x", [2 * TOK, D])
    pos_d = din("pos", [128, NTILE], I32)
    cst_d = din("cst", [128, NCST])
    wada_d = din("w_ada_l", [KC, 128, 6 * D]).rearrange("k p n -> p k n")
    win_d = din("w_in_l", [128, KC, N_IN])
    wo_d = din("wo_l", [16, 128, KC, 128])
    wgu_d = din("wgu_l", [NFF, 128, KC, 256])
    wd_d = din("wd_l", [16, 128, NFF, 128])
    out_d = nc.dram_tensor("out", [TOK, D], F32, kind="ExternalOutput").ap()
    hs_d = dscr("hs", [NBLK, 128, KC, 512], BF16)
    ms_d = dscr("ms", [16, 128, TOK], BF16)
    modd = dscr("modd", [1, 6 * D], F32)

    ps = nc.alloc_psum_tensor("ps", [128, 8, 512], F32)
    bps = [P.buf(f"ps{i}", excl=True) for i in range(8)]

    def psbf(bank):
        return ps[:, bank, :].bitcast(BF16).rearrange("p (a b) -> p a b", a=8)

    CST = A.alloc([NCST], F32)
    bC = P.buf("cst")
    bK = P.buf("derived")
    P.dma("sp", CST, cst_d, writes=[bC])

    def cs(name):
        o, n = CO[name]
        return CST[:, o:o + n]

    identf = cs("ident")
    tri = cs("tri")
    flag = cs("flag")
    pbias = cs("pbias")
    posi = A.alloc([NTILE], I32)
    P.dma("sp", posi, pos_d, writes=[bK])
    identb = A.alloc([128], BF16)
    onesf = A.alloc([128], F32)
    onesb = A.alloc([128], BF16)
    mod = A.alloc([96], F32)
    A1 = A.alloc([16], F32)
    A2 = A.alloc([16], F32)
    Sh1 = mod[:, 0:16]
    Sh2 = mod[:, 48:64]
    gate1 = mod[:, 32:48]
    gate2 = mod[:, 80:96]
    cosT = A.alloc([NTILE, 8], F32)
    sinT = A.alloc([NTILE, 8], F32)
    stats = A.alloc([128], F32)
    neglam = A.alloc([1], F32)
    subw_s = A.alloc([1], F32)
    sc = A.alloc([16], F32)
    P.op("dve", CP(identb, identf), reads=[bC], writes=[bK])
    P.op("dve", MSET(onesf, 1.0), writes=[bK])
    P.op("dve", MSET(onesb, 1.0), writes=[bK])
    P.op("dve", MSET(stats, 0.0), writes=[bK])
    rp = A.off
    posf = A.alloc([NTILE], F32)
    ang = A.alloc([NTILE, 8], F32)
    ang2 = A.alloc([NTILE, 8], F32)
    ki = A.alloc([NTILE, 8], I32)
    kf = A.alloc([NTILE, 8], F32)
    P.op("dve", CP(posf, posi), reads=[bK], writes=[bK])
    P.op("dve", TT(ang, posf[:, :, None].to_broadcast([128, NTILE, 8]),
                   cs("invf")[:, None, :].to_broadcast([128, NTILE, 8]), ALU.mult), reads=[bK, bC], writes=[bK])
    for (dst, shift) in ((sinT, 0.0), (cosT, math.pi / 2)):
        P.op("dve", TS(ang2, ang, shift, None, ALU.add), reads=[bK], writes=[bK])
        P.op("dve", TS(ki, ang2, 1.0 / (2 * math.pi), None, ALU.mult), reads=[bK], writes=[bK])
        P.op("dve", CP(kf, ki), reads=[bK], writes=[bK])
        P.op("dve", STT(ang2, kf, -2 * math.pi, ang2, ALU.mult, ALU.add), reads=[bK], writes=[bK])
        P.op("dve", TS(ang2, ang2, -3.1415925, 3.1415925, ALU.max, ALU.min), reads=[bK], writes=[bK])
        P.op("act", ACTF(dst, ang2, AF.Sin), reads=[bK], writes=[bK])
    lamv = cs("lamv").rearrange("p (a b) -> p a b", a=4)
    lt = A.alloc([2, 64], F32)
    ls = A.alloc([2], F32)
    P.op("dve", TT(lt[:, 0, :], lamv[:, 0, :], lamv[:, 1, :], ALU.mult), reads=[bC], writes=[bK])
    P.op("dve", TT(lt[:, 1, :], lamv[:, 2, :], lamv[:, 3, :], ALU.mult), reads=[bC, bK], writes=[bK])
    P.op("dve", lambda e: e.tensor_reduce(out=ls, in_=lt, axis=AX.X, op=ALU.add), reads=[bK], writes=[bK])
    P.op("act", ACTF(ls, ls, AF.Exp), reads=[bK], writes=[bK])
    P.op("dve", STT(neglam, ls[:, 1:2], -LAMBDA_INIT, ls[:, 0:1], ALU.add, ALU.subtract), reads=[bK], writes=[bK])
    P.op("dve", TS(subw_s, cs("subw"), 1.0 - LAMBDA_INIT, None, ALU.mult), reads=[bC], writes=[bK])
    P.op("act", ACTF(sc, cs("c_l"), AF.Silu), reads=[bC], writes=[bK])
    base_off = A.off

    norm_uid = [0]

    def norm_s1(xsrc, bx, xn, bxn, junk, bjunk):
        uid = norm_uid[0]
        norm_uid[0] += 1
        st = stats[:, 2 * uid:2 * uid + 2]
        bst = P.buf(f"st{uid}")
        k = uid % 2
        P.op("act", ACTF(junk, xsrc, AF.Square, accum=st[:, 0:1]), reads=[bx, bK], writes=[bjunk, bst], relaxed=True)
        P.op("act", ACTF(st[:, 1:2], st[:, 0:1], AF.Ln, scale=1.0 / D, bias=EPS), reads=[bst], writes=[bst])
        P.op("act", ACTF(st[:, 1:2], st[:, 1:2], AF.Exp, scale=-0.5), reads=[bst], writes=[bst])
        P.op("dve", TS(xn[k], xsrc, st[:, 1:2], None, ALU.mult), reads=[bx, bst], writes=[bxn[k]])
        return k

    def norm_s2(k, Av, Shv, dst_of_kc, bdst_act, bdst_dve, xn, bxn):
        for hf in range(2):
            bank = 4 + 2 * k + hf
            pv = psbf(bank)
            for q in range(8):
                kc = hf * 8 + q
                P.op("pe", TR(pv[:, q, :], xn[k][:, kc * 128:(kc + 1) * 128], identb), reads=[bxn[k], bK], writes=[bps[bank]])
            for q in range(8):
                kc = hf * 8 + q
                if hf == 0 and q < 6:
                    P.op("act", ACTF(dst_of_kc(kc), pv[:, q, :], AF.Identity, scale=Av[:, kc:kc + 1], bias=Shv[:, kc:kc + 1]),
                         reads=[bps[bank], bK], writes=[bdst_act], relaxed=True)
                else:
                    P.op("dve", TS(dst_of_kc(kc), pv[:, q, :], Av[:, kc:kc + 1], Shv[:, kc:kc + 1], ALU.mult, ALU.add),
                         reads=[bps[bank], bK], writes=[bdst_dve], relaxed=True)

    def norm_pipeline(tiles, xn, bxn, junk, bjunk, after=None):
        ks = {}
        ks[0] = norm_s1(tiles[0][0], tiles[0][1], xn, bxn, junk, bjunk)
        for t, tl in enumerate(tiles):
            if t + 1 < len(tiles):
                ks[t + 1] = norm_s1(tiles[t + 1][0], tiles[t + 1][1], xn, bxn, junk, bjunk)
            norm_s2(ks[t], tl[2], tl[3], tl[4], tl[5], tl[6], xn, bxn)
            if after is not None:
                after(t)

    xv = x_d.rearrange("(t p) d -> p t d", p=128)
    bmd = P.buf("modd")
    bhs1 = P.buf("hs")
    bhs = [bhs1] * NBLK

    if "A" in phases:
        mrow = [A.alloc([512], F32, parts=1) for _ in range(2)]
        bmr = [P.buf(f"mrow{i}") for i in range(2)]
        wab = [A.alloc([KC, 512], F32) for _ in range(2)]
        bwa = [P.buf(f"wab{i}") for i in range(2)]
        modT = A.alloc([128], F32, parts=96)
        bmT = P.buf("modT")

        def ada_cols(cb):
            i = cb % 2
            P.dma("sp", wab[i], wada_d[:, :, cb * 512:(cb + 1) * 512], writes=[bwa[i]])
            for kc in range(KC):
                P.op("pe", MM(ps[0:1, i, :], sc[:, kc:kc + 1], wab[i][:, kc, :], kc == 0, kc == KC - 1),
                     reads=[bwa[i], bK], writes=[bps[i]])
            P.op("act", lambda e, i=i: e.copy(out=mrow[i], in_=ps[0:1, i, :]), reads=[bps[i]], writes=[bmr[i]])
            P.dma("pool", modd[:, cb * 512:(cb + 1) * 512], mrow[i], reads=[bmr[i]], writes=[bmd])

        def ada_finish(j0, j1):
            n = j1 - j0
            P.dma("sp", modT[0:n, :], modd[:, j0 * 128:j1 * 128].rearrange("o (j p) -> (o j) p", p=128), reads=[bmd], writes=[bmT])
            P.op("pe", TR(ps[:, 2, 0:n], modT[0:n, :], identf[0:n, 0:n]), reads=[bmT, bC], writes=[bps[2]])
            P.op("dve", TT(mod[:, j0:j1], ps[:, 2, 0:n], cs("b_l")[:, j0:j1], ALU.add), reads=[bps[2], bC], writes=[bK])

        for cb in range(8):
            ada_cols(cb)
        ada_finish(0, 32)
        P.op("dve", STT(A1, mod[:, 16:32], 1.0, cs("n1w"), ALU.add, ALU.mult), reads=[bK, bC], writes=[bK])

        xb = [A.alloc([4, D], F32) for _ in range(2)]
        bxb = [P.buf(f"xb{i}") for i in range(2)]
        xn = [A.alloc([D], BF16) for _ in range(2)]
        bxn = [P.buf(f"xn{i}") for i in range(2)]
        junk = A.alloc([D], BF16)
        bjunk = P.buf("junk")
        hTb = [A.alloc([KC, 512], BF16) for _ in range(2)]
        bhTa = [P.buf(f"hTa{i}") for i in range(2)]
        bhTd = [P.buf(f"hTd{i}") for i in range(2)]
        P.dma("sp", xb[0], xv[:, 0:4, :], writes=[bxb[0]])
        P.dma("sp", xb[1], xv[:, 4:8, :], writes=[bxb[1]])
        tiles = []
        for blk in range(NBLK):
            i = blk % 2
            for j in range(4):
                tiles.append((xb[i][:, j, :], bxb[i], A1, Sh1, (lambda kc, i=i, j=j: hTb[i][:, kc, j * 128:(j + 1) * 128]), bhTa[i], bhTd[i]))

        def after_tile(t):
            blk, j = divmod(t, 4)
            i = blk % 2
            if j == 3:
                P.dma("pool", hs_d[blk], hTb[i], reads=[bhTa[i], bhTd[i]], writes=[bhs[blk]])
                if blk + 2 < NBLK:
                    P.dma("sp", xb[i], xv[:, 4 * blk + 8:4 * blk + 12, :], writes=[bxb[i]])

        norm_pipeline(tiles, xn, bxn, junk, bjunk, after=after_tile)
        P.barrier()
    A.off = base_off

    bms = P.buf("ms")

    if "B" in phases:
        wbuf = [A.alloc([KC, ML_W], BF16) for _ in range(2)]
        bwb = [P.buf(f"wb{i}") for i in range(2)]
        hbuf = [A.alloc([KC, 512], BF16) for _ in range(2)]
        bhb = [P.buf(f"hb{i}") for i in range(2)]
        mob = [A.alloc([2, 512], BF16) for _ in range(2)]
        bmoa = [P.buf(f"moa{i}") for i in range(2)]
        un_b0 = A.off
        xc = [A.alloc([515], F32) for _ in range(2)]
        bxc = [P.buf(f"xc{i}") for i in range(2)]
        ycv = [A.alloc([512], F32) for _ in range(2)]
        bycv = [P.buf(f"ycv{i}") for i in range(2)]
        sgm = [A.alloc([512], F32) for _ in range(2)]
        bsgm = [P.buf(f"sgm{i}") for i in range(2)]
        qkT = [[A.alloc([512], BF16) for _ in range(2)] for _ in range(2)]
        bqkT = [[P.buf(f"qkT{p}_{i}") for i in range(2)] for p in range(2)]
        Gp = [A.alloc([96], F32) for _ in range(2)]
        bGp = [P.buf(f"G{p}") for p in range(2)]
        vaug = [A.alloc([257], BF16) for _ in range(2)]
        bva = [P.buf(f"vaug{i}") for i in range(2)]
        vw = [A.alloc([257], BF16) for _ in range(2)]
        bvw = [P.buf(f"vw{i}") for i in range(2)]
        ktm = [A.alloc([128], BF16) for _ in range(2)]
        bktm = [P.buf(f"ktm{i}") for i in range(2)]
        ATs = [A.alloc([128], BF16) for _ in range(2)]
        bATs = [P.buf(f"ATs{i}") for i in range(2)]
        C32 = A.alloc([257], F32)
        bC32 = P.buf("C32")
        Cb = [A.alloc([257], BF16) for _ in range(2)]
        bCb = [P.buf(f"Cb{i}") for i in range(2)]
        dds = A.alloc([64 * 8], F32)
        junk2 = A.alloc([256], BF16)
        bjunk2 = P.buf("junk2")
        hn = [A.alloc([256], F32) for _ in range(2)]
        bhn = [P.buf(f"hn{i}") for i in range(2)]
        sig = [A.alloc([256], F32) for _ in range(2)]
        bsig = [P.buf(f"sig{i}") for i in range(2)]
        hm = [A.alloc([256], BF16) for _ in range(2)]
        bhm = [P.buf(f"hm{i}") for i in range(2)]
        un_ml_end = A.off
        A.off = un_b0
        KT = [A.alloc([NTILE * 128], BF16) for _ in range(2)]
        bKT = [P.buf(f"KT{i}") for i in range(2)]
        Vh = A.alloc([2, NTILE, 128], BF16)
        bVh = P.buf("Vh")
        QT = [A.alloc([512], BF16) for _ in range(2)]
        bQT = [P.buf(f"QT{i}") for i in range(2)]
        sqb = [A.alloc([8, 64], F32) for _ in range(2)]
        bsqb = [P.buf(f"sqb{i}") for i in range(2)]
        rs = [A.alloc([8], F32) for _ in range(2)]
        brs = [P.buf(f"rs{i}") for i in range(2)]
        t1 = [A.alloc([8, 64], F32) for _ in range(2)]
        bt1 = [P.buf(f"t1{i}") for i in range(2)]
        qkb = [A.alloc([8, 64], BF16) for _ in range(2)]
        bqkb = [P.buf(f"qkb{i}") for i in range(2)]
        rt = [A.alloc([4, 8, 8], F32) for _ in range(2)]
        brt = [P.buf(f"rt{i}") for i in range(2)]
        PT = [A.alloc([2, 512], BF16) for _ in range(3)]
        bPT = [P.buf(f"PT{i}") for i in range(3)]
        Pacc = [A.alloc([512], F32) for _ in range(2)]
        bPacc = [P.buf(f"Pacc{i}") for i in range(2)]
        fz = [A.alloc([512], F32) for _ in range(5)]
        bfz = [P.buf(f"fz{i}") for i in range(5)]
        A.off = max(A.off, un_ml_end)
        wbg = [A.alloc([KC, 128], F32) for _ in range(2)]
        bwbg = [P.buf(f"wbg{i}") for i in range(2)]
        mrb = [A.alloc([128], F32, parts=1) for _ in range(2)]
        bmrb = [P.buf(f"mrb{i}") for i in range(2)]
        accb = A.alloc([128], F32)
        baccb = P.buf("accb")
        tmpb = [A.alloc([128], F32) for _ in range(2)]
        btmpb = [P.buf(f"tmpb{i}") for i in range(2)]
        modT2 = A.alloc([128], F32, parts=64)
        bmT2 = P.buf("modT2")

        def ada_bg():
            for c in range(32, 96):
                i = c % 2
                P.dma("sp", wbg[i], wada_d[:, :, c * 128:(c + 1) * 128], writes=[bwbg[i]])
                yield
                P.op("pool", TS(accb, wbg[i][:, 0, :], sc[:, 0:1], None, ALU.mult), reads=[bwbg[i], bK], writes=[baccb])
                for kc in range(1, KC):
                    P.op("pool", TS(tmpb[kc % 2], wbg[i][:, kc, :], sc[:, kc:kc + 1], None, ALU.mult), reads=[bwbg[i], bK], writes=[btmpb[kc % 2]])
                    P.op("pool", TT(accb, accb, tmpb[kc % 2], ALU.add), reads=[btmpb[kc % 2], baccb], writes=[baccb])
                P.op("pe", MM(ps[0:1, 4, 384:512], onesf[:, 0:1], accb), reads=[baccb, bK], writes=[bps[4]])
                P.op("act", lambda e, i=i: e.copy(out=mrb[i], in_=ps[0:1, 4, 384:512]), reads=[bps[4]], writes=[bmrb[i]])
                P.dma("pool", modd[:, c * 128:(c + 1) * 128], mrb[i], reads=[bmrb[i]], writes=[bmd])
                yield

        g_ada = ada_bg()

        def ada_step(n=2):
            for _ in range(n):
                try:
                    next(g_ada)
                except StopIteration:
                    return

        def interleave(g1, g2):
            gens = [g for g in (g1, g2) if g is not None]
            while gens:
                for g in list(gens):
                    try:
                        next(g)
                    except StopIteration:
                        gens.remove(g)

        P.op("dve", MSET(dds, 0.0), writes=[bK])
        for i in range(2):
            P.op("dve", MSET(vaug[i][:, 256:257], 1.0), writes=[bva[i]])

        wml = cs("nw").rearrange("p (h v) -> p h v", h=4)
        cw = cs("cw").rearrange("p (h q t) -> p h q t", h=4, q=2)
        cbv = cs("cb").rearrange("p (h q) -> p h q", h=4)
        gbv = cs("gb").rearrange("p (h g) -> p h g", h=4)
        wqk = cs("wqk").rearrange("p (g d) -> p g d", g=8)
        LNS = math.log(128.0 ** -0.5)

        npass = 8
        pass_cols = [(hh * ML_W, ML_W) for hh in range(4)] + [(4 * ML_W + hp * DA_W, DA_W) for hp in range(4)]

        def load_w(pi):
            o, n = pass_cols[pi]
            P.dma("pool", wbuf[pi % 2][:, :, 0:n], win_d[:, :, o:o + n], writes=[bwb[pi % 2]])

        hcount = [0]

        def load_h(blk):
            i = hcount[0] % 2
            hcount[0] += 1
            P.dma("sp", hbuf[i], hs_d[blk], reads=[bhs[blk]], writes=[bhb[i]])
            return i

        load_w(0)
        cur_tile_head = [0]
        mo_cnt = [0]
        KB = os.environ.get("KDBG_B", "")
        for pi in range(npass):
            if pi + 1 < npass:
                load_w(pi + 1)
            if (KB == "ml" and pi >= 4) or (KB == "da" and pi < 4) or (KB == "ml1" and pi >= 1) or (KB == "da1" and pi != 4):
                continue
            W = wbuf[pi % 2]
            bW = bwb[pi % 2]
            if pi == 4:
                P.barrier()
            nxt = load_h(0)
            if pi < 4:
                hh = pi
                P.op("dve", MSET(C32, 0.0), writes=[bC32])
                hmap = {}

                def ml_prep(blk, hi):
                    H = hbuf[hi]
                    bH = bhb[hi]
                    cur = blk >= 4
                    pb = blk % 2
                    for qk in (1, 0):
                        if qk == 0 and blk < 3:
                            continue
                        for kc in range(KC):
                            P.op("pe", MM(ps[:, qk, :], W[:, kc, qk * 128:(qk + 1) * 128], H[:, kc, :], kc == 0, kc == KC - 1),
                                 reads=[bW, bH], writes=[bps[qk]])
                        yield
                        first = (blk == 0) if qk == 1 else (blk == 3)
                        if first:
                            P.op("dve", MSET(xc[qk][:, 0:3], 0.0), writes=[bxc[qk]])
                        elif blk == 4:
                            P.op("dve", TS(xc[qk][:, 0:3], xc[qk][:, 512:515], flag[:, 0:1], None, ALU.mult),
                                 reads=[bxc[qk], bC], writes=[bxc[qk]])
                        else:
                            P.op("dve", CP(xc[qk][:, 0:3], xc[qk][:, 512:515]), reads=[bxc[qk]], writes=[bxc[qk]])
                        P.op("act", lambda e, qk=qk: e.copy(out=xc[qk][:, 3:515], in_=ps[:, qk, :]), reads=[bps[qk]], writes=[bxc[qk]])
                        yield
                        if qk == 0 and blk < 4:
                            continue
                        P.op("dve", TS(ycv[qk], xc[qk][:, 0:512], cw[:, hh, qk, 0:1], cbv[:, hh, qk:qk + 1], ALU.mult, ALU.add),
                             reads=[bxc[qk], bC], writes=[bycv[qk]])
                        yield
                        for tp in range(1, 4):
                            P.op("dve", STT(ycv[qk], xc[qk][:, tp:tp + 512], cw[:, hh, qk, tp:tp + 1], ycv[qk], ALU.mult, ALU.add),
                                 reads=[bxc[qk], bC, bycv[qk]], writes=[bycv[qk]])
                            yield
                        P.op("act", ACTF(sgm[qk], ycv[qk], AF.Exp, scale=-1.0), reads=[bycv[qk]], writes=[bsgm[qk]])
                        P.op("act", ACTF(sgm[qk], sgm[qk], AF.Ln, bias=1.0), reads=[bsgm[qk]], writes=[bsgm[qk]])
                        yield
                        P.op("act", ACTF(sgm[qk], sgm[qk], AF.Exp, scale=-1.0), reads=[bsgm[qk]], writes=[bsgm[qk]])
                        P.op("dve", TT(qkT[pb][qk], ycv[qk], sgm[qk], ALU.mult), reads=[bycv[qk], bsgm[qk]], writes=[bqkT[pb][qk]])
                        yield
                    for j in range(4):
                        for kc in range(KC):
                            P.op("pe", MM(ps[:, 5, 384 + 2 * j:386 + 2 * j], H[:, kc, j * 128:(j + 1) * 128], W[:, kc, 768:770], kc == 0, kc == KC - 1),
                                 reads=[bW, bH], writes=[bps[5]])
                        yield
                    G = Gp[pb]
                    bG = bGp[pb]
                    gsb = G[:, 0:8].rearrange("p (j g) -> p j g", j=4)
                    th = G[:, 8:16].rearrange("p (j g) -> p j g", j=4)
                    e1, l1, li, nBe, tq, argw, argf = (G[:, 16 + 4 * i:20 + 4 * i] for i in range(7))
                    wv, fpv, eB = (G[:, 48 + 4 * i:52 + 4 * i] for i in range(3))
                    P.op("dve", TT(gsb, ps[:, 5, 384:392].rearrange("p (j g) -> p j g", j=4),
                                   gbv[:, hh:hh + 1, :].to_broadcast([128, 4, 2]), ALU.add), reads=[bps[5], bC], writes=[bG])
                    P.op("act", ACTF(th, gsb, AF.Exp, scale=2.0 / 15.0), reads=[bG], writes=[bG])
                    yield
                    P.op("act", ACTF(th, th, AF.Ln, bias=1.0), reads=[bG], writes=[bG])
                    P.op("act", ACTF(th, th, AF.Exp, scale=-1.0), reads=[bG], writes=[bG])
                    yield
                    P.op("dve", TS(th, th, -2.0, 1.0, ALU.mult, ALU.add), reads=[bG], writes=[bG])
                    P.op("act", ACTF(e1, th[:, :, 1], AF.Exp, scale=-15.0), reads=[bG], writes=[bG])
                    yield
                    P.op("act", ACTF(l1, e1, AF.Ln, bias=1.0), reads=[bG], writes=[bG])
                    P.op("dve", TS(li, th[:, :, 0], 15.0, None, ALU.mult), reads=[bG], writes=[bG])
                    yield
                    P.op("pe", MM(ps[:, 5, 400:404], tri, l1), reads=[bG, bC], writes=[bps[5]])
                    P.op("pe", MM(ps[:, 5, 416:420], onesf, l1), reads=[bG, bK], writes=[bps[5]])
                    yield
                    P.op("dve", CP(nBe, ps[:, 5, 416:420]), reads=[bps[5]], writes=[bG])
                    P.op("dve", TT(tq, li, nBe, ALU.subtract), reads=[bG], writes=[bG])
                    yield
                    P.op("dve", TT(argw, tq, ps[:, 5, 400:404], ALU.add), reads=[bG, bps[5]], writes=[bG])
                    P.op("dve", TT(argf, nBe, ps[:, 5, 400:404], ALU.subtract), reads=[bG, bps[5]], writes=[bG])
                    yield
                    P.op("act", ACTF(wv, argw, AF.Exp), reads=[bG], writes=[bG])
                    P.op("act", ACTF(fpv, argf, AF.Exp, bias=LNS), reads=[bG], writes=[bG])
                    P.op("act", ACTF(eB, nBe, AF.Exp, scale=-1.0), reads=[bG], writes=[bG])
                    yield

                def ml_tiles(blk, hi):
                    H = hbuf[hi]
                    bH = bhb[hi]
                    cur = blk >= 4
                    pb = blk % 2
                    G = Gp[pb]
                    bG = bGp[pb]
                    wv, fpv, eB = (G[:, 48 + 4 * i:52 + 4 * i] for i in range(3))
                    qT = qkT[pb][0]
                    kT = qkT[pb][1]
                    bqT = bqkT[pb][0]
                    bkT = bqkT[pb][1]
                    nv = 512 if cur else 256
                    mi = mo_cnt[0] % 2

                    def emit_vo(j):
                        vb_ = 2 + (j % 2)
                        for kc in range(KC):
                            P.op("pe", MM(ps[:, vb_, 0:nv], H[:, kc, j * 128:(j + 1) * 128], W[:, kc, 256:256 + nv], kc == 0, kc == KC - 1),
                                 reads=[bW, bH], writes=[bps[vb_]])

                    def tile(j):
                        t = blk * 4 + j
                        vb = 2 + (j % 2)
                        nb = 6 + (j % 2)
                        a = t % 2
                        P.op("dve", TS(vw[a][:, 0:256], ps[:, vb, 0:256], wv[:, j:j + 1], None, ALU.mult), reads=[bps[vb], bG], writes=[bvw[a]])
                        P.op("dve", CP(vw[a][:, 256:257], wv[:, j:j + 1]), reads=[bG], writes=[bvw[a]])
                        P.op("pe", TR(psbf(5)[:, 2 + a, :], kT[:, j * 128:(j + 1) * 128], identb), reads=[bkT, bK], writes=[bps[5]])
                        P.op("act", lambda e, a=a: e.copy(out=ktm[a], in_=psbf(5)[:, 2 + a, :]), reads=[bps[5]], writes=[bktm[a]])
                        yield
                        if cur:
                            u = cur_tile_head[0]
                            cur_tile_head[0] += 1
                            dd = dds[:, 8 * u:8 * u + 8]
                            bdd = P.buf(f"dd{u}")
                            P.op("act", lambda e, a=a, vb=vb: e.copy(out=vaug[a][:, 0:256], in_=ps[:, vb, 0:256]), reads=[bps[vb]], writes=[bva[a]])
                            P.op("act", ACTF(sig[a], ps[:, vb, 256:512], AF.Exp, scale=-1.0), reads=[bps[vb]], writes=[bsig[a]])
                            P.op("pe", MM(ps[:, 5, 0:128], kT[:, j * 128:(j + 1) * 128], qT[:, j * 128:(j + 1) * 128]),
                                 reads=[bqT, bkT], writes=[bps[5]])
                            P.op("dve", STT(ATs[a], ps[:, 5, 0:128], wv[:, j:j + 1], tri, ALU.mult, ALU.mult), reads=[bps[5], bG, bC], writes=[bATs[a]])
                            yield
                        if blk == 4 and j == 0:
                            P.op("dve", TS(C32, C32, flag[:, 0:1], None, ALU.mult), reads=[bC32, bC], writes=[bC32])
                        if cur:
                            P.op("dve", TS(Cb[a], C32, eB[:, j:j + 1], None, ALU.mult), reads=[bC32, bG], writes=[bCb[a]])
                            P.op("pe", MM(ps[:, nb, 0:257], ATs[a], vaug[a], True, False), reads=[bATs[a], bva[a]], writes=[bps[nb]])
                            P.op("pe", MM(ps[:, nb, 0:257], qT[:, j * 128:(j + 1) * 128], Cb[a], False, True), reads=[bqT, bCb[a]], writes=[bps[nb]])
                        P.op("pe", MM(ps[:, 4, 0:257], ktm[a], vw[a]), reads=[bktm[a], bvw[a]], writes=[bps[4]])
                        P.op("dve", STT(C32, C32, eB[:, j:j + 1], ps[:, 4, 0:257], ALU.mult, ALU.add), reads=[bC32, bG, bps[4]], writes=[bC32])
                        yield
                        if not cur:
                            return
                        P.op("dve", TT(dd[:, 0:1], ps[:, nb, 256:257], fpv[:, j:j + 1], ALU.mult), reads=[bps[nb], bG], writes=[bdd])
                        P.op("dve", TS(dd[:, 1:2], dd[:, 0:1], -1.0, 1.0, ALU.mult, ALU.max), reads=[bdd], writes=[bdd])
                        yield
                        P.op("dve", TT(dd[:, 1:2], dd[:, 1:2], dd[:, 0:1], ALU.max), reads=[bdd], writes=[bdd])
                        P.op("dve", RCP(dd[:, 1:2], dd[:, 1:2]), reads=[bdd], writes=[bdd])
                        yield
                        P.op("dve", TT(dd[:, 2:3], fpv[:, j:j + 1], dd[:, 1:2], ALU.mult), reads=[bdd, bG], writes=[bdd])
                        P.op("act", ACTF(junk2, ps[:, nb, 0:256], AF.Square, scale=dd[:, 2:3], accum=dd[:, 3:4]), reads=[bps[nb], bdd], writes=[bjunk2, bdd])
                        yield
                        P.op("act", ACTF(dd[:, 4:5], dd[:, 3:4], AF.Ln, scale=1.0 / 256.0, bias=EPS), reads=[bdd], writes=[bdd])
                        P.op("act", ACTF(dd[:, 4:5], dd[:, 4:5], AF.Exp, scale=-0.5), reads=[bdd], writes=[bdd])
                        yield
                        P.op("dve", TT(dd[:, 5:6], dd[:, 2:3], dd[:, 4:5], ALU.mult), reads=[bdd], writes=[bdd])
                        yield
                        P.op("dve", STT(hn[a], ps[:, nb, 0:256], dd[:, 5:6], wml[:, hh, :], ALU.mult, ALU.mult), reads=[bps[nb], bdd, bC], writes=[bhn[a]])
                        P.op("act", ACTF(sig[a], sig[a], AF.Ln, bias=1.0), reads=[bsig[a]], writes=[bsig[a]])
                        yield
                        P.op("act", ACTF(sig[a], sig[a], AF.Exp, scale=-1.0), reads=[bsig[a]], writes=[bsig[a]])
                        yield
                        P.op("dve", TT(hm[a], hn[a], sig[a], ALU.mult), reads=[bhn[a], bsig[a]], writes=[bhm[a]])
                        yield
                        for i2 in range(2):
                            P.op("pe", TR(psbf(5)[:, 4 + i2, :], hm[a][:, i2 * 128:(i2 + 1) * 128], identb), reads=[bhm[a], bK], writes=[bps[5]])
                        P.op("act", lambda e, mi=mi, j=j: e.copy(out=mob[mi][:, :, j * 128:(j + 1) * 128], in_=psbf(5)[:, 4:6, :]),
                             reads=[bps[5]], writes=[bmoa[mi]], relaxed=True)
                        yield

                    emit_vo(0)
                    yield
                    act_t = []
                    for j in range(4):
                        if j + 1 < 4:
                            emit_vo(j + 1)
                            yield
                        act_t.append(tile(j))
                        if len(act_t) == 2:
                            done0 = False
                            while not done0:
                                for g in list(act_t):
                                    try:
                                        next(g)
                                    except StopIteration:
                                        if g is act_t[0]:
                                            done0 = True
                                        act_t.remove(g)
                                yield
                        else:
                            for _ in range(3):
                                try:
                                    next(act_t[0])
                                except StopIteration:
                                    act_t.pop()
                                    break
                            yield
                    for g in act_t:
                        for _ in g:
                            yield
                    if cur:
                        cb4 = blk - 4
                        P.dma("sp", ms_d[2 * hh:2 * hh + 2, :, cb4 * 512:(cb4 + 1) * 512].rearrange("c p n -> p c n"), mob[mi],
                              reads=[bmoa[mi]], writes=[bms])
                        mo_cnt[0] += 1

                hmap[0] = nxt
                for _ in ml_prep(0, hmap[0]):
                    pass
                for blk in range(NBLK):
                    ada_step()
                    g2 = None
                    if blk + 1 < NBLK:
                        hmap[blk + 1] = load_h(blk + 1)
                        g2 = ml_prep(blk + 1, hmap[blk + 1])
                    interleave(ml_tiles(blk, hmap[blk]), g2)
            else:
                hp = pi - 4
                for blk in range(NBLK):
                    ada_step()
                    hi = nxt
                    if blk + 1 < NBLK:
                        nxt = load_h(blk + 1)
                    H = hbuf[hi]
                    bH = bhb[hi]
                    cur = blk >= 4
                    c0 = 0 if cur else 128

                    def proj_mm(j):
                        b0 = 2 * (j % 2)
                        for hd in range(2):
                            for kc in range(KC):
                                P.op("pe", MM(ps[:, b0 + hd, c0:384], H[:, kc, j * 128:(j + 1) * 128], W[:, kc, hd * 384 + c0:hd * 384 + 384],
                                              kc == 0, kc == KC - 1), reads=[bW, bH], writes=[bps[b0 + hd]])

                    def tile_gen(j):
                        t = blk * 4 + j
                        b0 = 2 * (j % 2)
                        bb = [bps[b0], bps[b0 + 1]]
                        sl = t % 2
                        P.op("act", lambda e, t=t, b0=b0: e.copy(out=Vh[:, :, t, :], in_=ps[:, b0:b0 + 2, 256:384]), reads=bb, writes=[bVh], relaxed=True)
                        pq = ps[:, b0:b0 + 2, 0:256].rearrange("p h (g d) -> p h g d", g=4)
                        sq4 = sqb[sl].rearrange("p (h g) d -> p h g d", h=2)
                        t14 = t1[sl].rearrange("p (h g) d -> p h g d", h=2)
                        P.op("act", ACTF(sq4, pq, AF.Square), reads=bb, writes=[bsqb[sl]])
                        yield
                        P.op("dve", lambda e, sl=sl: e.tensor_reduce(out=rs[sl], in_=sqb[sl], axis=AX.X, op=ALU.add), reads=[bsqb[sl]], writes=[brs[sl]])
                        yield
                        P.op("act", ACTF(rs[sl], rs[sl], AF.Ln, scale=1.0 / 64.0, bias=EPS), reads=[brs[sl]], writes=[brs[sl]])
                        P.op("act", ACTF(rs[sl], rs[sl], AF.Exp, scale=-0.5), reads=[brs[sl]], writes=[brs[sl]])
                        yield
                        P.op("dve", TT(t14, pq, rs[sl].rearrange("p (h g) -> p h g", h=2)[:, :, :, None].to_broadcast([128, 2, 4, 64]), ALU.mult),
                             reads=bb + [brs[sl]], writes=[bt1[sl]])
                        yield
                        P.op("dve", TT(t1[sl], t1[sl], wqk, ALU.mult), reads=[bt1[sl], bC], writes=[bt1[sl]])
                        yield
                        P.op("act", lambda e, sl=sl: e.copy(out=qkb[sl], in_=t1[sl]), reads=[bt1[sl]], writes=[bqkb[sl]])
                        cosb = cosT[:, t:t + 1, :].to_broadcast([128, 8, 8])
                        sinb = sinT[:, t:t + 1, :].to_broadcast([128, 8, 8])
                        x1 = t1[sl][:, :, 0:8]
                        x2 = t1[sl][:, :, 8:16]
                        R_ = rt[sl]
                        P.op("dve", TT(R_[:, 0], x1, cosb, ALU.mult), reads=[bt1[sl], bK], writes=[brt[sl]])
                        P.op("dve", TT(R_[:, 1], x2, sinb, ALU.mult), reads=[bt1[sl], bK], writes=[brt[sl]], relaxed=True)
                        yield
                        P.op("dve", TT(R_[:, 2], x2, cosb, ALU.mult), reads=[bt1[sl], bK], writes=[brt[sl]], relaxed=True)
                        P.op("dve", TT(R_[:, 3], x1, sinb, ALU.mult), reads=[bt1[sl], bK], writes=[brt[sl]], relaxed=True)
                        yield
                        P.op("dve", TT(qkb[sl][:, :, 0:8], R_[:, 0], R_[:, 1], ALU.subtract), reads=[brt[sl]], writes=[bqkb[sl]])
                        P.op("dve", TT(qkb[sl][:, :, 8:16], R_[:, 2], R_[:, 3], ALU.add), reads=[brt[sl]], writes=[bqkb[sl]], relaxed=True)
                        yield
                        qk2 = qkb[sl].rearrange("p g d -> p (g d)")
                        for hd in range(2):
                            P.op("pe", TR(psbf(4)[:, 2 * hd, :], qk2[:, hd * 256 + 128:hd * 256 + 256], identb), reads=[bqkb[sl], bK], writes=[bps[4]])
                            if cur:
                                P.op("pe", TR(psbf(4)[:, 2 * hd + 1, :], qk2[:, hd * 256:hd * 256 + 128], identb), reads=[bqkb[sl], bK], writes=[bps[4]])
                        yield
                        for hd in range(2):
                            P.op("act", lambda e, hd=hd, t=t: e.copy(out=KT[hd][:, t * 128:(t + 1) * 128], in_=psbf(4)[:, 2 * hd, :]),
                                 reads=[bps[4]], writes=[bKT[hd]], relaxed=True)
                            if cur:
                                P.op("act", lambda e, hd=hd, j=j: e.copy(out=QT[hd][:, j * 128:(j + 1) * 128], in_=psbf(4)[:, 2 * hd + 1, :]),
                                     reads=[bps[4]], writes=[bQT[hd]], relaxed=True)
                        yield

                    proj_mm(0)
                    act_g = []
                    for j in range(4):
                        if j + 1 < 4:
                            proj_mm(j + 1)
                        act_g.append(tile_gen(j))
                        if len(act_g) == 2:
                            done0 = False
                            nstep = 0
                            while not done0:
                                for g in list(act_g):
                                    try:
                                        next(g)
                                    except StopIteration:
                                        if g is act_g[0]:
                                            done0 = True
                                        act_g.remove(g)
                                nstep += 1
                        elif j == 0:
                            for _ in range(4):
                                next(act_g[0])
                    for g in act_g:
                        for _ in g:
                            pass
                    if not cur:
                        continue
                    cb4 = blk - 4
                    ndone = 16 + 4 * cb4
                    for hd in range(2):
                        steps = [(c, 0, None) for c in range(ndone)] + [(ndone + jd, jd * 128, jd) for jd in (3, 2, 1, 0)]
                        nst = len(steps)

                        def emit_S(n):
                            c, q0, jd = steps[n]
                            sb0 = 4 + 2 * (n % 2)
                            for m in range(2):
                                P.op("pe", MM(ps[:, sb0 + m, q0:512], KT[hd][m * 64:(m + 1) * 64, c * 128:(c + 1) * 128], QT[hd][m * 64:(m + 1) * 64, q0:512]),
                                     reads=[bKT[hd], bQT[hd]], writes=[bps[sb0 + m]])

                        def emit_E(n):
                            c, q0, jd = steps[n]
                            sb0 = 4 + 2 * (n % 2)
                            pt = n % 3
                            P.op("act", ACTF(PT[pt][:, :, q0:512], ps[:, sb0:sb0 + 2, q0:512], AF.Exp, scale=0.125, bias=(pbias[:, 0:1] if c < 16 else None)),
                                 reads=[bps[sb0], bps[sb0 + 1], bC], writes=[bPT[pt]])
                            if jd is not None:
                                P.op("dve", MSET(PT[pt][64:128, :, q0:q0 + 64], 0.0), reads=[bPT[pt]], writes=[bPT[pt]])

                        def emit_PV(n):
                            c, q0, jd = steps[n]
                            pt = n % 3
                            for m in range(2):
                                P.op("pe", MM(ps[:, m, q0:512], Vh[:, hd, c, :], PT[pt][:, m, q0:512], c == 0, jd == 0),
                                     reads=[bVh, bPT[pt]], writes=[bps[m]])
                            if c == 0:
                                P.op("dve", CP(Pacc[0], PT[pt][:, 0, :]), reads=[bPT[pt]], writes=[bPacc[0]])
                            else:
                                P.op("dve", TT(Pacc[0][:, q0:512], Pacc[0][:, q0:512], PT[pt][:, 0, q0:512], ALU.add), reads=[bPT[pt], bPacc[0]], writes=[bPacc[0]])
                            P.op("pe", MM(ps[:, 3, q0:512], onesb, PT[pt][:, 1, q0:512], c == 0, jd == 0), reads=[bPT[pt], bK], writes=[bps[3]])

                        emit_S(0)
                        emit_S(1)
                        for n in range(nst):
                            emit_E(n)
                            if n + 2 < nst:
                                emit_S(n + 2)
                            emit_PV(n)
                        P.op("pe", MM(ps[:, 2, :], onesf, Pacc[0]), reads=[bPacc[0], bK], writes=[bps[2]])
                        mi = mo_cnt[0] % 2
                        mo_cnt[0] += 1
                        P.op("dve", RCP(fz[0], ps[:, 2, :]), reads=[bps[2]], writes=[bfz[0]])
                        P.op("dve", TT(fz[1], ps[:, 0, :], fz[0], ALU.mult), reads=[bps[0], bfz[0]], writes=[bfz[1]])
                        P.op("dve", RCP(fz[2], ps[:, 3, :]), reads=[bps[3]], writes=[bfz[2]])
                        P.op("dve", TT(fz[3], ps[:, 1, :], fz[2], ALU.mult), reads=[bps[1], bfz[2]], writes=[bfz[3]])
                        P.op("dve", STT(fz[1], fz[3], neglam[:, 0:1], fz[1], ALU.mult, ALU.add), reads=[bfz[3], bfz[1], bK], writes=[bfz[1]])
                        P.op("act", ACTF(fz[4], fz[1], AF.Square), reads=[bfz[1]], writes=[bfz[4]])
                        P.op("pe", MM(ps[:, 2, :], onesf, fz[4]), reads=[bfz[4], bK], writes=[bps[2]])
                        P.op("act", ACTF(fz[0], ps[:, 2, :], AF.Ln, scale=1.0 / 128.0, bias=EPS), reads=[bps[2]], writes=[bfz[0]])
                        P.op("act", ACTF(fz[0], fz[0], AF.Exp, scale=-0.5), reads=[bfz[0]], writes=[bfz[0]])
                        P.op("dve", STT(mob[mi][:, 0, :], fz[1], subw_s[:, 0:1], fz[0], ALU.mult, ALU.mult), reads=[bfz[1], bfz[0], bK], writes=[bmoa[mi]])
                        ch = 8 + 2 * hp + hd
                        P.dma("sp", ms_d[ch, :, cb4 * 512:(cb4 + 1) * 512], mob[mi][:, 0, :], reads=[bmoa[mi]], writes=[bms])
        for _ in g_ada:
            pass
        P.dma("sp", modT2, modd[:, 32 * 128:96 * 128].rearrange("o (j p) -> (o j) p", p=128), reads=[bmd], writes=[bmT2])
        P.op("pe", TR(ps[:, 2, 0:64], modT2, identf[0:64, 0:64]), reads=[bmT2, bC], writes=[bps[2]])
        P.op("dve", TT(mod[:, 32:96], ps[:, 2, 0:64], cs("b_l")[:, 32:96], ALU.add), reads=[bps[2], bC], writes=[bK])
        P.op("dve", STT(A2, mod[:, 64:80], 1.0, cs("n2w"), ALU.add, ALU.mult), reads=[bK, bC], writes=[bK])
        P.barrier()
    A.off = base_off

    bout = P.buf("out")
    if "C" in phases:
        h2T = A.alloc([KC, 1024], BF16)
        bh2a, bh2d = P.buf("h2a"), P.buf("h2d")
        wgu = [A.alloc([KC, 256], BF16) for _ in range(2)]
        bwgu = [P.buf(f"wgu{i}") for i in range(2)]
        wdb = [A.alloc([NFF, 128], BF16) for _ in range(2)]
        bwd = [P.buf(f"wd{i}") for i in range(2)]
        tmpf = [A.alloc([512], F32) for _ in range(2)]
        btmp = [P.buf(f"tmp{i}") for i in range(2)]
        xs = [A.alloc([8, 128], F32) for _ in range(2)]
        bxs = [P.buf(f"xs{i}") for i in range(2)]
        xn = [A.alloc([D], BF16) for _ in range(2)]
        bxn = [P.buf(f"xn{i}") for i in range(2)]
        junk = A.alloc([D], BF16)
        bjunk = P.buf("junk")
        sg = [A.alloc([512], F32) for _ in range(2)]
        bsg = [P.buf(f"sg{i}") for i in range(2)]
        un0 = A.off
        mxb = [A.alloc([KC, 512], BF16) for _ in range(2)]
        bmx = [P.buf(f"mx{i}") for i in range(2)]
        xsb = A.alloc([4, D], F32)
        bxsb = P.buf("xsb")
        A.off = un0
        actT = A.alloc([NFF, 1024], BF16)
        bact = P.buf("actT")
        wob = [wgu[i][:, :, 0:128] for i in range(2)]
        out_v = out_d.rearrange("(t p) d -> p t d", p=128)
        tcount = [0]
        wcount = [0]
        for TB in range(2):
            for sb in range(2):
                g512 = TB * 2 + sb
                mi = g512 % 2
                P.dma("sp", mxb[mi], ms_d[:, :, g512 * 512:(g512 + 1) * 512].rearrange("c p n -> p c n"),
                      reads=[bms], writes=[bmx[mi]])
                P.dma("sp", xsb, xv[:, 16 + 4 * g512:16 + 4 * g512 + 4, :], writes=[bxsb])
                for m in range(16):
                    wi = wcount[0] % 2
                    wcount[0] += 1
                    P.dma("pool", wob[wi], wo_d[m], writes=[bwgu[wi]])
                    ob = m % 2
                    for kc in range(KC):
                        P.op("pe", MM(ps[:, ob, :], wob[wi][:, kc, :], mxb[mi][:, kc, :], kc == 0, kc == KC - 1),
                             reads=[bwgu[wi], bmx[mi]], writes=[bps[ob]])
                    ti = tcount[0] % 2
                    tcount[0] += 1
                    P.op("act", ACTF(tmpf[ti], ps[:, ob, :], AF.Copy, scale=gate1[:, m:m + 1]), reads=[bps[ob], bK], writes=[btmp[ti]])
                    tb = 2 + ti
                    for j in range(4):
                        P.op("pe", TR(ps[:, tb, j * 128:(j + 1) * 128], tmpf[ti][:, j * 128:(j + 1) * 128], identf), reads=[btmp[ti], bC], writes=[bps[tb]])
                    xslice = xsb[:, :, m * 128:(m + 1) * 128]
                    P.op("dve", TT(xslice, ps[:, tb, :].rearrange("p (j f) -> p j f", j=4), xslice, ALU.add), reads=[bps[tb], bxsb], writes=[bxsb])
                ntiles = []
                for j in range(4):
                    col = sb * 512 + j * 128
                    ntiles.append((xsb[:, j, :], bxsb, A2, Sh2, (lambda kc, col=col: h2T[:, kc, col:col + 128]), bh2a, bh2d))
                norm_pipeline(ntiles, xn, bxn, junk, bjunk)
                P.dma("sp", out_v[:, 4 * g512:4 * g512 + 4, :], xsb, reads=[bxsb], writes=[bout])
            P.barrier()
            for jf in range(NFF):
                wi = wcount[0] % 2
                wcount[0] += 1
                P.dma("pool", wgu[wi], wgu_d[jf], writes=[bwgu[wi]])
                for sb in range(2):
                    for gu in range(2):
                        ob = 2 * (sb % 2) + gu + (4 if jf % 2 else 0)
                        for kc in range(KC):
                            P.op("pe", MM(ps[:, ob, :], wgu[wi][:, kc, gu * 128:(gu + 1) * 128], h2T[:, kc, sb * 512:(sb + 1) * 512], kc == 0, kc == KC - 1),
                                 reads=[bwgu[wi], bh2a, bh2d], writes=[bps[ob]])
                    ob = 2 * (sb % 2) + (4 if jf % 2 else 0)
                    si = tcount[0] % 2
                    tcount[0] += 1
                    P.op("act", ACTF(sg[si], ps[:, ob, :], AF.Silu), reads=[bps[ob]], writes=[bsg[si]])
                    P.op("dve", TT(actT[:, jf, sb * 512:(sb + 1) * 512], sg[si], ps[:, ob + 1, :], ALU.mult), reads=[bsg[si], bps[ob + 1]], writes=[bact], relaxed=True)
            for m in range(16):
                wi = m % 2
                P.dma("pool", wdb[wi], wd_d[m], writes=[bwd[wi]])
                xi = m % 2
                P.dma("sp", xs[xi], out_v[:, 8 * TB:8 * TB + 8, m * 128:(m + 1) * 128], reads=[bout], writes=[bxs[xi]], sem_buf=bxs[xi])
                for sb in range(2):
                    ob = sb
                    for fc in range(NFF):
                        P.op("pe", MM(ps[:, ob, :], wdb[wi][:, fc, :], actT[:, fc, sb * 512:(sb + 1) * 512], fc == 0, fc == NFF - 1),
                             reads=[bwd[wi], bact], writes=[bps[ob]])
                    ti = tcount[0] % 2
                    tcount[0] += 1
                    P.op("act", ACTF(tmpf[ti], ps[:, ob, :], AF.Copy, scale=gate2[:, m:m + 1]), reads=[bps[ob], bK], writes=[btmp[ti]])
                    tb = 2 + ti
                    for j in range(4):
                        P.op("pe", TR(ps[:, tb, j * 128:(j + 1) * 128], tmpf[ti][:, j * 128:(j + 1) * 128], identf), reads=[btmp[ti], bC], writes=[bps[tb]])
                    xslice = xs[xi][:, sb * 4:(sb + 1) * 4, :]
                    P.op("dve", TT(xslice, ps[:, tb, :].rearrange("p (j f) -> p j f", j=4), xslice, ALU.add), reads=[bps[tb], bxs[xi]], writes=[bxs[xi]])
                bo2 = P.buf(f"o2_{TB}_{m}")
                P.dma("sp", out_v[:, 8 * TB:8 * TB + 8, m * 128:(m + 1) * 128], xs[xi], reads=[bxs[xi]], writes=[bout, bo2], sem_buf=bo2)
            P.barrier()
    P.barrier()
    P.emit()
    return nc, P


def _prep_shared(inp):
    f = np.float32
    w_in = np.asarray(inp["w_in"], f)[0]
    cols = []
    for hh in range(4):
        cols += list(range(hh * 128, hh * 128 + 128)) + list(range(512 + hh * 128, 512 + hh * 128 + 128))
        cols += list(range(1024 + hh * 256, 1024 + hh * 256 + 256)) + list(range(2048 + hh * 256, 2048 + hh * 256 + 256))
        cols += [3072 + hh, 3076 + hh]
    for hp in range(4):
        for hd in range(2):
            h = 2 * hp + hd
            cols += list(range(3080 + h * 128, 3080 + h * 128 + 128)) + list(range(4104 + h * 128, 4104 + h * 128 + 128))
            cols += list(range(5128 + h * 128, 5128 + h * 128 + 128))
    cols = np.asarray(cols)
    sh = {}
    sh["w_in_l"] = np.ascontiguousarray(w_in.reshape(KC, 128, N_IN).transpose(1, 0, 2)[:, :, cols])
    sh["w_ada_l"] = np.ascontiguousarray(np.asarray(inp["w_ada"], f)[0].reshape(KC, 128, 6 * D))
    sh["wo_l"] = np.ascontiguousarray(np.asarray(inp["w_out"], f)[0].reshape(KC, 128, 16, 128).transpose(2, 1, 0, 3))
    wgu = np.asarray(inp["w_gate_up"], f)[0].reshape(KC, 128, 2, NFF, 128)
    sh["wgu_l"] = np.ascontiguousarray(wgu.transpose(3, 1, 0, 2, 4).reshape(NFF, 128, KC, 256))
    sh["wd_l"] = np.ascontiguousarray(np.asarray(inp["w_down"], f)[0].reshape(NFF, 128, 16, 128).transpose(2, 1, 0, 3))
    return sh


def _prep_cst(inp, b, half):
    f = np.float32
    c = np.zeros((128, NCST), f)

    def put(name, arr):
        o, n = CO[name]
        c[:, o:o + n] = np.asarray(arr, f).reshape(128, n)

    rep = lambda v: np.broadcast_to(np.asarray(v, f).reshape(1, -1), (128, np.asarray(v).size))
    put("c_l", np.asarray(inp["c"], f)[b].reshape(KC, 128).T)
    put("b_l", np.asarray(inp["b_ada"], f)[0].reshape(96, 128).T)
    put("n1w", np.asarray(inp["norm1_w"], f)[0].reshape(KC, 128).T)
    put("n2w", np.asarray(inp["norm2_w"], f)[0].reshape(KC, 128).T)
    cw = np.asarray(inp["mlstm_conv_w"], f)[0].reshape(4, 2, 4, 128)
    put("cw", cw.transpose(3, 2, 1, 0).reshape(128, 32))
    cb = np.asarray(inp["mlstm_conv_b"], f)[0].reshape(2, 4, 128)
    put("cb", cb.transpose(2, 1, 0).reshape(128, 8))
    gb = np.asarray(inp["mlstm_gate_b"], f)[0].reshape(2, 4)
    put("gb", rep(gb.T.reshape(-1)))
    put("nw", rep(np.asarray(inp["mlstm_norm_w"], f)[0].reshape(-1)))
    qw = np.asarray(inp["q_norm_w"], f)[0]
    kw = np.asarray(inp["k_norm_w"], f)[0]
    put("wqk", rep(np.concatenate([qw, qw, kw, kw, qw, qw, kw, kw])))
    put("lamv", rep(np.concatenate([np.asarray(inp[k], f)[0] for k in ("lambda_q1", "lambda_k1", "lambda_q2", "lambda_k2")])))
    put("subw", np.asarray(inp["subln_w"], f)[0].reshape(128, 1))
    put("flag", np.full((128, 1), float(half), f))
    put("pbias", np.full((128, 1), 0.0 if half else -30000.0, f))
    invf = (np.float32(500000.0) ** (-np.arange(0, 16, 2, dtype=np.float32) / np.float32(16))).astype(f)
    put("invf", rep(invf))
    put("ident", np.eye(128, dtype=f))
    put("tri", np.triu(np.ones((128, 128), f)))
    return c


def _in_maps(inp):
    sh = _prep_shared(inp)
    x = np.asarray(inp["x"], np.float32)
    pos = np.asarray(inp["positions"], np.int32)
    maps = []
    for core in range(8):
        b, half = core // 2, core % 2
        m = dict(sh)
        m["x"] = np.ascontiguousarray(np.concatenate([x[b, 0:TOK], x[b, half * TOK:(half + 1) * TOK]], axis=0))
        pp = np.concatenate([pos[b, 0:TOK], pos[b, half * TOK:(half + 1) * TOK]])
        m["pos"] = np.ascontiguousarray(pp.reshape(NTILE, 128).T)
        m["cst"] = _prep_cst(inp, b, half)
        maps.append(m)
    return maps


_NC = None


def kernel(**inputs):
    global _NC
    if _NC is None:
        _NC = build()[0]
    maps = _in_maps(inputs)
    res = run_bass_kernel_spmd(_NC, maps, core_ids=list(range(8)))
    out = np.empty((4, 2 * TOK, D), np.float32)
    for core in range(8):
        b, half = core // 2, core % 2
        out[b, half * TOK:(half + 1) * TOK] = res.results[core]["out"]
    return out
```

```python
import math
import os
import numpy as np
import concourse.bass as bass
import concourse.mybir as mybir
from concourse.bass_utils import run_bass_kernel_spmd

F32 = mybir.dt.float32
BF16 = mybir.dt.bfloat16
I32 = mybir.dt.int32
AF = mybir.ActivationFunctionType
ALU = mybir.AluOpType
AX = mybir.AxisListType

SEM_EPOCH = 16000
import os as _os
SAME_ENGINE_SYNC = _os.environ.get("K_SES", "1") == "1"


class Buf:
    __slots__ = ("name", "w", "rd", "dsem", "dcnt", "excl")

    def __init__(self, name, excl=False):
        self.name = name
        self.excl = excl
        self.w = None
        self.rd = {}
        self.dsem = None
        self.dcnt = 0


class Op:
    __slots__ = ("eng", "fn", "waits", "need_sig", "sigidx", "dma_sem")

    def __init__(self, eng, fn):
        self.eng = eng
        self.fn = fn
        self.waits = []
        self.need_sig = False
        self.sigidx = None
        self.dma_sem = None


class Prog:
    ENGS = ("pe", "act", "dve", "pool", "sp")

    def __init__(self, nc):
        self.nc = nc
        self.ops = {e: [] for e in self.ENGS}
        self.nsem = 0
        self.bufs = []

    def buf(self, name, excl=False):
        b = Buf(name, excl)
        self.bufs.append(b)
        return b

    def _newsem(self, name):
        self.nsem += 1
        return self.nc.alloc_semaphore(f"s{self.nsem}_{name}")

    def _collect(self, op, reads, writes, relaxed):
        toks = []
        for b in reads:
            if b.w is not None:
                toks.append((b.w, False))
            if b.excl:
                for k, t in b.rd.items():
                    if k != op.eng:
                        toks.append((t, False))
        for b in writes:
            if b.w is not None:
                toks.append((b.w, True))
            for t in b.rd.values():
                toks.append((t, False))
        for t, waw in toks:
            if t[0] == "op":
                p = t[1]
                if p is op:
                    continue
                if p.eng == op.eng and (p.eng == "pe" or not SAME_ENGINE_SYNC or (waw and relaxed)):
                    continue
                p.need_sig = True
            op.waits.append(t)

    def op(self, eng, fn, reads=(), writes=(), relaxed=False):
        o = Op(eng, fn)
        self._collect(o, reads, writes, relaxed)
        tok = ("op", o)
        for b in reads:
            b.rd[eng] = tok
        for b in writes:
            b.w = tok
            b.rd = {}
        self.ops[eng].append(o)
        return o

    def dma(self, eng, out_ap, in_ap, reads=(), writes=(), sem_buf=None):
        sb = sem_buf or (writes[0] if writes else reads[0])
        if sb.dsem is None:
            sb.dsem = self._newsem(sb.name)
        sb.dcnt += 16
        sem, cnt = sb.dsem, sb.dcnt

        def fn(e, out_ap=out_ap, in_ap=in_ap):
            return e.dma_start(out=out_ap, in_=in_ap)

        o = Op(eng, fn)
        o.dma_sem = sem
        self._collect(o, reads, writes, False)
        tok = ("sem", sem, cnt)
        for b in reads:
            b.rd[("d", id(sem))] = tok
        for b in writes:
            b.w = tok
            b.rd = {}
        self.ops[eng].append(o)
        return o

    def barrier(self):
        toks = []
        for b in self.bufs:
            if b.w is not None:
                toks.append(b.w)
            toks.extend(b.rd.values())
        for e in self.ENGS:
            for o in reversed(self.ops[e]):
                if o.fn is not None and o.dma_sem is None:
                    toks.append(("op", o))
                    break
        for e in self.ENGS:
            o = Op(e, None)
            for t in toks:
                if t[0] == "op":
                    if t[1].eng == e:
                        continue
                    t[1].need_sig = True
                o.waits.append(t)
            self.ops[e].append(o)
        for b in self.bufs:
            b.w = None
            b.rd = {}

    def final_wait(self, eng, bufs):
        o = Op(eng, None)
        for b in bufs:
            if b.w is not None:
                o.waits.append(b.w)
                if b.w[0] == "op":
                    b.w[1].need_sig = True
        self.ops[eng].append(o)

    def emit(self):
        nc = self.nc
        eng_sems = {}
        for e in self.ENGS:
            n = 0
            for o in self.ops[e]:
                if o.need_sig and o.dma_sem is None and o.fn is not None:
                    o.sigidx = n
                    n += 1
            eng_sems[e] = [self._newsem(f"{e}{i}") for i in range((n + SEM_EPOCH - 1) // SEM_EPOCH)]
        self.stats = {e: len(self.ops[e]) for e in self.ENGS}
        self.stats["nsem"] = self.nsem

        def resolve(t):
            if t[0] == "sem":
                return t[1], t[2]
            p = t[1]
            assert p.sigidx is not None, "waiting on op without signal"
            return eng_sems[p.eng][p.sigidx // SEM_EPOCH], p.sigidx % SEM_EPOCH + 1

        def run(e, h):
            waited = {}
            for o in self.ops[e]:
                need = {}
                for t in o.waits:
                    s, v = resolve(t)
                    k = id(s)
                    if waited.get(k, 0) >= v:
                        continue
                    if k not in need or need[k][1] < v:
                        need[k] = (s, v)
                for k, (s, v) in need.items():
                    h.wait_ge(s, v)
                    waited[k] = v
                if o.fn is None:
                    continue
                ins = o.fn(h)
                if o.dma_sem is not None:
                    ins.then_inc(o.dma_sem, 16)
                elif o.need_sig:
                    ins.then_inc(eng_sems[e][o.sigidx // SEM_EPOCH], 1)

        with nc.Block() as block:
            @block.tensor
            def _(h):
                run("pe", h)

            @block.scalar
            def _(h):
                run("act", h)

            @block.vector
            def _(h):
                run("dve", h)

            @block.gpsimd
            def _(h):
                run("pool", h)

            @block.sync
            def _(h):
                run("sp", h)


def MM(out, lhsT, rhs, start=True, stop=True):
    return lambda e: e.matmul(out, lhsT=lhsT, rhs=rhs, start=start, stop=stop)


def TR(out, in_, ident):
    return lambda e: e.transpose(out=out, in_=in_, identity=ident)


def ACTF(out, in_, func, scale=None, bias=None, accum=None):
    kw = {}
    if scale is not None:
        kw["scale"] = scale
    if bias is not None:
        kw["bias"] = bias
    if accum is not None:
        kw["accum_out"] = accum
    return lambda e: e.activation(out=out, in_=in_, func=func, **kw)


def TS(out, in0, s1, s2=None, op0=ALU.mult, op1=None):
    if op1 is None:
        return lambda e: e.tensor_scalar(out=out, in0=in0, scalar1=s1, scalar2=None, op0=op0)
    return lambda e: e.tensor_scalar(out=out, in0=in0, scalar1=s1, scalar2=s2, op0=op0, op1=op1)


def TT(out, in0, in1, op):
    return lambda e: e.tensor_tensor(out=out, in0=in0, in1=in1, op=op)


def STT(out, in0, scalar, in1, op0, op1):
    return lambda e: e.scalar_tensor_tensor(out=out, in0=in0, scalar=scalar, in1=in1, op0=op0, op1=op1)


def CP(out, in_):
    return lambda e: e.tensor_copy(out=out, in_=in_)


def MSET(ap, v):
    return lambda e: e.memset(ap, v)


def RCP(out, in_):
    return lambda e: e.reciprocal(out=out, in_=in_)


D = 2048
KC = 16
TOK = 2048
NTILE = 32
NBLK = 8
DFF = 5632
NFF = 44
N_IN = 6152
EPS = 1e-6
LAMBDA_INIT = 0.8 - 0.6 * math.exp(-0.3 * 0)
ML_W = 770
DA_W = 768
ARENA_BYTES = 211968

_CST = [("c_l", 16), ("b_l", 96), ("n1w", 16), ("n2w", 16), ("cw", 32), ("cb", 8), ("gb", 8), ("nw", 1024),
        ("wqk", 512), ("lamv", 256), ("subw", 1), ("flag", 1), ("pbias", 1), ("invf", 8), ("ident", 128),
        ("tri", 128)]
CO = {}
_o = 0
for _n, _s in _CST:
    CO[_n] = (_o, _s)
    _o += _s
NCST = _o


class Arena:
    def __init__(self, nc):
        self.t = nc.alloc_sbuf_tensor("arena", [128, ARENA_BYTES // 4], F32)
        self.off = 0

    def alloc(self, shape, dtype, parts=128, p0=0):
        n = int(np.prod(shape))
        esz = 2 if dtype == BF16 else 4
        nb = (n * esz + 31) // 32 * 32
        o = self.off
        self.off += nb
        assert self.off <= ARENA_BYTES, f"SBUF arena overflow {self.off}"
        v = self.t[p0:p0 + parts, o // 4:(o + nb) // 4]
        if dtype != F32:
            v = v.bitcast(dtype)
        v = v[:, 0:n]
        if len(shape) > 1:
            names = [f"a{i}" for i in range(len(shape))]
            kw = {nm: int(s) for nm, s in zip(names, shape)}
            v = v.rearrange("p (" + " ".join(names) + ") -> p " + " ".join(names), **kw)
        return v


def build(dbg=False, phases="0ABC"):
    nc = bass.Bass("TRN2", target_bir_lowering=False)
    P = Prog(nc)
    A = Arena(nc)

    def din(name, shape, dt=F32):
        return nc.dram_tensor(name, list(shape), dt, kind="ExternalInput").ap()

    def dscr(name, shape, dt):
        if dbg:
            return nc.dram_tensor(name, list(shape), dt, kind="ExternalOutput").ap()
        return nc.dram_tensor(name, list(shape), dt).ap()

    x_d = din("x", [2 * TOK, D])
    pos_d = din("pos", [128, NTILE], I32)
    cst_d = din("cst", [128, NCST])
    wada_d = din("w_ada_l", [KC, 128, 6 * D]).rearrange("k p n -> p k n")
    win_d = din("w_in_l", [128, KC, N_IN])
    wo_d = din("wo_l", [16, 128, KC, 128])
    wgu_d = din("wgu_l", [NFF, 128, KC, 256])
    wd_d = din("wd_l", [16, 128, NFF, 128])
    out_d = nc.dram_tensor("out", [TOK, D], F32, kind="ExternalOutput").ap()
    hs_d = dscr("hs", [NBLK, 128, KC, 512], BF16)
    ms_d = dscr("ms", [16, 128, TOK], BF16)
    modd = dscr("modd", [1, 6 * D], F32)

    ps = nc.alloc_psum_tensor("ps", [128, 8, 512], F32)
    bps = [P.buf(f"ps{i}", excl=True) for i in range(8)]

    def psbf(bank):
        return ps[:, bank, :].bitcast(BF16).rearrange("p (a b) -> p a b", a=8)

    CST = A.alloc([NCST], F32)
    bC = P.buf("cst")
    bK = P.buf("derived")
    P.dma("sp", CST, cst_d, writes=[bC])

    def cs(name):
        o, n = CO[name]
        return CST[:, o:o + n]

    identf = cs("ident")
    tri = cs("tri")
    flag = cs("flag")
    pbias = cs("pbias")
    posi = A.alloc([NTILE], I32)
    P.dma("sp", posi, pos_d, writes=[bK])
    identb = A.alloc([128], BF16)
    onesf = A.alloc([128], F32)
    onesb = A.alloc([128], BF16)
    mod = A.alloc([96], F32)
    A1 = A.alloc([16], F32)
    A2 = A.alloc([16], F32)
    Sh1 = mod[:, 0:16]
    Sh2 = mod[:, 48:64]
    gate1 = mod[:, 32:48]
    gate2 = mod[:, 80:96]
    cosT = A.alloc([NTILE, 8], F32)
    sinT = A.alloc([NTILE, 8], F32)
    stats = A.alloc([128], F32)
    neglam = A.alloc([1], F32)
    subw_s = A.alloc([1], F32)
    sc = A.alloc([16], F32)
    P.op("dve", CP(identb, identf), reads=[bC], writes=[bK])
    P.op("dve", MSET(onesf, 1.0), writes=[bK])
    P.op("dve", MSET(onesb, 1.0), writes=[bK])
    P.op("dve", MSET(stats, 0.0), writes=[bK])
    rp = A.off
    posf = A.alloc([NTILE], F32)
    ang = A.alloc([NTILE, 8], F32)
    ang2 = A.alloc([NTILE, 8], F32)
    ki = A.alloc([NTILE, 8], I32)
    kf = A.alloc([NTILE, 8], F32)
    P.op("dve", CP(posf, posi), reads=[bK], writes=[bK])
    P.op("dve", TT(ang, posf[:, :, None].to_broadcast([128, NTILE, 8]),
                   cs("invf")[:, None, :].to_broadcast([128, NTILE, 8]), ALU.mult), reads=[bK, bC], writes=[bK])
    for (dst, shift) in ((sinT, 0.0), (cosT, math.pi / 2)):
        P.op("dve", TS(ang2, ang, shift, None, ALU.add), reads=[bK], writes=[bK])
        P.op("dve", TS(ki, ang2, 1.0 / (2 * math.pi), None, ALU.mult), reads=[bK], writes=[bK])
        P.op("dve", CP(kf, ki), reads=[bK], writes=[bK])
        P.op("dve", STT(ang2, kf, -2 * math.pi, ang2, ALU.mult, ALU.add), reads=[bK], writes=[bK])
        P.op("dve", TS(ang2, ang2, -3.1415925, 3.1415925, ALU.max, ALU.min), reads=[bK], writes=[bK])
        P.op("act", ACTF(dst, ang2, AF.Sin), reads=[bK], writes=[bK])
    lamv = cs("lamv").rearrange("p (a b) -> p a b", a=4)
    lt = A.alloc([2, 64], F32)
    ls = A.alloc([2], F32)
    P.op("dve", TT(lt[:, 0, :], lamv[:, 0, :], lamv[:, 1, :], ALU.mult), reads=[bC], writes=[bK])
    P.op("dve", TT(lt[:, 1, :], lamv[:, 2, :], lamv[:, 3, :], ALU.mult), reads=[bC, bK], writes=[bK])
    P.op("dve", lambda e: e.tensor_reduce(out=ls, in_=lt, axis=AX.X, op=ALU.add), reads=[bK], writes=[bK])
    P.op("act", ACTF(ls, ls, AF.Exp), reads=[bK], writes=[bK])
    P.op("dve", STT(neglam, ls[:, 1:2], -LAMBDA_INIT, ls[:, 0:1], ALU.add, ALU.subtract), reads=[bK], writes=[bK])
    P.op("dve", TS(subw_s, cs("subw"), 1.0 - LAMBDA_INIT, None, ALU.mult), reads=[bC], writes=[bK])
    P.op("act", ACTF(sc, cs("c_l"), AF.Silu), reads=[bC], writes=[bK])
    base_off = A.off

    norm_uid = [0]

    def norm_s1(xsrc, bx, xn, bxn, junk, bjunk):
        uid = norm_uid[0]
        norm_uid[0] += 1
        st = stats[:, 2 * uid:2 * uid + 2]
        bst = P.buf(f"st{uid}")
        k = uid % 2
        P.op("act", ACTF(junk, xsrc, AF.Square, accum=st[:, 0:1]), reads=[bx, bK], writes=[bjunk, bst], relaxed=True)
        P.op("act", ACTF(st[:, 1:2], st[:, 0:1], AF.Ln, scale=1.0 / D, bias=EPS), reads=[bst], writes=[bst])
        P.op("act", ACTF(st[:, 1:2], st[:, 1:2], AF.Exp, scale=-0.5), reads=[bst], writes=[bst])
        P.op("dve", TS(xn[k], xsrc, st[:, 1:2], None, ALU.mult), reads=[bx, bst], writes=[bxn[k]])
        return k

    def norm_s2(k, Av, Shv, dst_of_kc, bdst_act, bdst_dve, xn, bxn):
        for hf in range(2):
            bank = 4 + 2 * k + hf
            pv = psbf(bank)
            for q in range(8):
                kc = hf * 8 + q
                P.op("pe", TR(pv[:, q, :], xn[k][:, kc * 128:(kc + 1) * 128], identb), reads=[bxn[k], bK], writes=[bps[bank]])
            for q in range(8):
                kc = hf * 8 + q
                if hf == 0 and q < 6:
                    P.op("act", ACTF(dst_of_kc(kc), pv[:, q, :], AF.Identity, scale=Av[:, kc:kc + 1], bias=Shv[:, kc:kc + 1]),
                         reads=[bps[bank], bK], writes=[bdst_act], relaxed=True)
                else:
                    P.op("dve", TS(dst_of_kc(kc), pv[:, q, :], Av[:, kc:kc + 1], Shv[:, kc:kc + 1], ALU.mult, ALU.add),
                         reads=[bps[bank], bK], writes=[bdst_dve], relaxed=True)

    def norm_pipeline(tiles, xn, bxn, junk, bjunk, after=None):
        ks = {}
        ks[0] = norm_s1(tiles[0][0], tiles[0][1], xn, bxn, junk, bjunk)
        for t, tl in enumerate(tiles):
            if t + 1 < len(tiles):
                ks[t + 1] = norm_s1(tiles[t + 1][0], tiles[t + 1][1], xn, bxn, junk, bjunk)
            norm_s2(ks[t], tl[2], tl[3], tl[4], tl[5], tl[6], xn, bxn)
            if after is not None:
                after(t)

    xv = x_d.rearrange("(t p) d -> p t d", p=128)
    bhs1 = P.buf("hs")
    bhs = [bhs1] * NBLK

    if "A" in phases:
        mrow = [A.alloc([512], F32, parts=1) for _ in range(2)]
        bmr = [P.buf(f"mrow{i}") for i in range(2)]
        wab = [A.alloc([KC, 512], F32) for _ in range(2)]
        bwa = [P.buf(f"wab{i}") for i in range(2)]
        bmd = P.buf("modd")
        modT = A.alloc([128], F32, parts=96)
        bmT = P.buf("modT")

        def ada_cols(cb):
            i = cb % 2
            P.dma("sp", wab[i], wada_d[:, :, cb * 512:(cb + 1) * 512], writes=[bwa[i]])
            for kc in range(KC):
                P.op("pe", MM(ps[0:1, i, :], sc[:, kc:kc + 1], wab[i][:, kc, :], kc == 0, kc == KC - 1),
                     reads=[bwa[i], bK], writes=[bps[i]])
            P.op("act", lambda e, i=i: e.copy(out=mrow[i], in_=ps[0:1, i, :]), reads=[bps[i]], writes=[bmr[i]])
            P.dma("pool", modd[:, cb * 512:(cb + 1) * 512], mrow[i], reads=[bmr[i]], writes=[bmd])

        def ada_finish(j0, j1):
            n = j1 - j0
            P.dma("sp", modT[0:n, :], modd[:, j0 * 128:j1 * 128].rearrange("o (j p) -> (o j) p", p=128), reads=[bmd], writes=[bmT])
            P.op("pe", TR(ps[:, 2, 0:n], modT[0:n, :], identf[0:n, 0:n]), reads=[bmT, bC], writes=[bps[2]])
            P.op("dve", TT(mod[:, j0:j1], ps[:, 2, 0:n], cs("b_l")[:, j0:j1], ALU.add), reads=[bps[2], bC], writes=[bK])

        for cb in range(8):
            ada_cols(cb)
        ada_finish(0, 32)
        P.op("dve", STT(A1, mod[:, 16:32], 1.0, cs("n1w"), ALU.add, ALU.mult), reads=[bK, bC], writes=[bK])

        xb = [A.alloc([4, D], F32) for _ in range(2)]
        bxb = [P.buf(f"xb{i}") for i in range(2)]
        xn = [A.alloc([D], BF16) for _ in range(2)]
        bxn = [P.buf(f"xn{i}") for i in range(2)]
        junk = A.alloc([D], BF16)
        bjunk = P.buf("junk")
        hTb = [A.alloc([KC, 512], BF16) for _ in range(2)]
        bhTa = [P.buf(f"hTa{i}") for i in range(2)]
        bhTd = [P.buf(f"hTd{i}") for i in range(2)]
        P.dma("sp", xb[0], xv[:, 0:4, :], writes=[bxb[0]])
        P.dma("sp", xb[1], xv[:, 4:8, :], writes=[bxb[1]])
        tiles = []
        for blk in range(NBLK):
            i = blk % 2
            for j in range(4):
                tiles.append((xb[i][:, j, :], bxb[i], A1, Sh1, (lambda kc, i=i, j=j: hTb[i][:, kc, j * 128:(j + 1) * 128]), bhTa[i], bhTd[i]))

        def after_tile(t):
            blk, j = divmod(t, 4)
            i = blk % 2
            if j == 3:
                P.dma("pool", hs_d[blk], hTb[i], reads=[bhTa[i], bhTd[i]], writes=[bhs[blk]])
                if blk + 2 < NBLK:
                    P.dma("sp", xb[i], xv[:, 4 * blk + 8:4 * blk + 12, :], writes=[bxb[i]])
            if j in (1, 3):
                cb = 8 + 2 * blk + (j // 2)
                ada_cols(cb)

        norm_pipeline(tiles, xn, bxn, junk, bjunk, after=after_tile)
        ada_finish(32, 96)
        P.op("dve", STT(A2, mod[:, 64:80], 1.0, cs("n2w"), ALU.add, ALU.mult), reads=[bK, bC], writes=[bK])
        P.barrier()
    A.off = base_off

    bms = P.buf("ms")

    if "B" in phases:
        wbuf = [A.alloc([KC, ML_W], BF16) for _ in range(2)]
        bwb = [P.buf(f"wb{i}") for i in range(2)]
        hbuf = [A.alloc([KC, 512], BF16) for _ in range(2)]
        bhb = [P.buf(f"hb{i}") for i in range(2)]
        mob = [A.alloc([2, 512], BF16) for _ in range(2)]
        bmoa = [P.buf(f"moa{i}") for i in range(2)]
        xc = [A.alloc([515], F32) for _ in range(2)]
        bxc = [P.buf(f"xc{i}") for i in range(2)]
        ycv = [A.alloc([512], F32) for _ in range(2)]
        bycv = [P.buf(f"ycv{i}") for i in range(2)]
        sgm = [A.alloc([512], F32) for _ in range(2)]
        bsgm = [P.buf(f"sgm{i}") for i in range(2)]
        qkT = [[A.alloc([512], BF16) for _ in range(2)] for _ in range(2)]
        bqkT = [[P.buf(f"qkT{p}_{i}") for i in range(2)] for p in range(2)]
        Gp = [A.alloc([96], F32) for _ in range(2)]
        bGp = [P.buf(f"G{p}") for p in range(2)]
        vaug = [A.alloc([257], BF16) for _ in range(2)]
        bva = [P.buf(f"vaug{i}") for i in range(2)]
        vw = [A.alloc([257], BF16) for _ in range(2)]
        bvw = [P.buf(f"vw{i}") for i in range(2)]
        ktm = [A.alloc([128], BF16) for _ in range(2)]
        bktm = [P.buf(f"ktm{i}") for i in range(2)]
        ATs = [A.alloc([128], BF16) for _ in range(2)]
        bATs = [P.buf(f"ATs{i}") for i in range(2)]
        C32 = A.alloc([257], F32)
        bC32 = P.buf("C32")
        Cb = [A.alloc([257], BF16) for _ in range(2)]
        bCb = [P.buf(f"Cb{i}") for i in range(2)]
        dds = A.alloc([64 * 8], F32)
        junk2 = A.alloc([256], BF16)
        bjunk2 = P.buf("junk2")
        hn = [A.alloc([256], F32) for _ in range(2)]
        bhn = [P.buf(f"hn{i}") for i in range(2)]
        sig = [A.alloc([256], F32) for _ in range(2)]
        bsig = [P.buf(f"sig{i}") for i in range(2)]
        hm = [A.alloc([256], BF16) for _ in range(2)]
        bhm = [P.buf(f"hm{i}") for i in range(2)]
        KT = [A.alloc([NTILE * 128], BF16) for _ in range(2)]
        bKT = [P.buf(f"KT{i}") for i in range(2)]
        Vh = A.alloc([2, NTILE, 128], BF16)
        bVh = P.buf("Vh")
        QT = [A.alloc([512], BF16) for _ in range(2)]
        bQT = [P.buf(f"QT{i}") for i in range(2)]
        sqb = [A.alloc([8, 64], F32) for _ in range(2)]
        bsqb = [P.buf(f"sqb{i}") for i in range(2)]
        rs = [A.alloc([8], F32) for _ in range(2)]
        brs = [P.buf(f"rs{i}") for i in range(2)]
        t1 = [A.alloc([8, 64], F32) for _ in range(2)]
        bt1 = [P.buf(f"t1{i}") for i in range(2)]
        qkb = [A.alloc([8, 64], BF16) for _ in range(2)]
        bqkb = [P.buf(f"qkb{i}") for i in range(2)]
        rt = [A.alloc([4, 8, 8], F32) for _ in range(2)]
        brt = [P.buf(f"rt{i}") for i in range(2)]
        PT = [A.alloc([2, 512], BF16) for _ in range(3)]
        bPT = [P.buf(f"PT{i}") for i in range(3)]
        Pacc = [A.alloc([512], F32) for _ in range(2)]
        bPacc = [P.buf(f"Pacc{i}") for i in range(2)]
        fz = [A.alloc([512], F32) for _ in range(5)]
        bfz = [P.buf(f"fz{i}") for i in range(5)]
        def interleave(g1, g2):
            gens = [g for g in (g1, g2) if g is not None]
            while gens:
                for g in list(gens):
                    try:
                        next(g)
                    except StopIteration:
                        gens.remove(g)

        P.op("dve", MSET(dds, 0.0), writes=[bK])
        for i in range(2):
            P.op("dve", MSET(vaug[i][:, 256:257], 1.0), writes=[bva[i]])

        wml = cs("nw").rearrange("p (h v) -> p h v", h=4)
        cw = cs("cw").rearrange("p (h q t) -> p h q t", h=4, q=2)
        cbv = cs("cb").rearrange("p (h q) -> p h q", h=4)
        gbv = cs("gb").rearrange("p (h g) -> p h g", h=4)
        wqk = cs("wqk").rearrange("p (g d) -> p g d", g=8)
        LNS = math.log(128.0 ** -0.5)

        npass = 8
        pass_cols = [(hh * ML_W, ML_W) for hh in range(4)] + [(4 * ML_W + hp * DA_W, DA_W) for hp in range(4)]

        def load_w(pi):
            o, n = pass_cols[pi]
            P.dma("pool", wbuf[pi % 2][:, :, 0:n], win_d[:, :, o:o + n], writes=[bwb[pi % 2]])

        hcount = [0]

        def load_h(blk):
            i = hcount[0] % 2
            hcount[0] += 1
            P.dma("sp", hbuf[i], hs_d[blk], reads=[bhs[blk]], writes=[bhb[i]])
            return i

        load_w(0)
        cur_tile_head = [0]
        mo_cnt = [0]
        KB = os.environ.get("KDBG_B", "")
        for pi in range(npass):
            if pi + 1 < npass:
                load_w(pi + 1)
            if (KB == "ml" and pi >= 4) or (KB == "da" and pi < 4) or (KB == "ml1" and pi >= 1) or (KB == "da1" and pi != 4):
                continue
            W = wbuf[pi % 2]
            bW = bwb[pi % 2]
            nxt = load_h(0)
            if pi < 4:
                hh = pi
                P.op("dve", MSET(C32, 0.0), writes=[bC32])
                hmap = {}

                def ml_prep(blk, hi):
                    H = hbuf[hi]
                    bH = bhb[hi]
                    cur = blk >= 4
                    pb = blk % 2
                    for qk in (1, 0):
                        if qk == 0 and blk < 3:
                            continue
                        for kc in range(KC):
                            P.op("pe", MM(ps[:, qk, :], W[:, kc, qk * 128:(qk + 1) * 128], H[:, kc, :], kc == 0, kc == KC - 1),
                                 reads=[bW, bH], writes=[bps[qk]])
                        yield
                        first = (blk == 0) if qk == 1 else (blk == 3)
                        if first:
                            P.op("dve", MSET(xc[qk][:, 0:3], 0.0), writes=[bxc[qk]])
                        elif blk == 4:
                            P.op("dve", TS(xc[qk][:, 0:3], xc[qk][:, 512:515], flag[:, 0:1], None, ALU.mult),
                                 reads=[bxc[qk], bC], writes=[bxc[qk]])
                        else:
                            P.op("dve", CP(xc[qk][:, 0:3], xc[qk][:, 512:515]), reads=[bxc[qk]], writes=[bxc[qk]])
                        P.op("act", lambda e, qk=qk: e.copy(out=xc[qk][:, 3:515], in_=ps[:, qk, :]), reads=[bps[qk]], writes=[bxc[qk]])
                        yield
                        if qk == 0 and blk < 4:
                            continue
                        P.op("dve", TS(ycv[qk], xc[qk][:, 0:512], cw[:, hh, qk, 0:1], cbv[:, hh, qk:qk + 1], ALU.mult, ALU.add),
                             reads=[bxc[qk], bC], writes=[bycv[qk]])
                        yield
                        for tp in range(1, 4):
                            P.op("dve", STT(ycv[qk], xc[qk][:, tp:tp + 512], cw[:, hh, qk, tp:tp + 1], ycv[qk], ALU.mult, ALU.add),
                                 reads=[bxc[qk], bC, bycv[qk]], writes=[bycv[qk]])
                            yield
                        P.op("act", ACTF(sgm[qk], ycv[qk], AF.Exp, scale=-1.0), reads=[bycv[qk]], writes=[bsgm[qk]])
                        P.op("act", ACTF(sgm[qk], sgm[qk], AF.Ln, bias=1.0), reads=[bsgm[qk]], writes=[bsgm[qk]])
                        yield
                        P.op("act", ACTF(sgm[qk], sgm[qk], AF.Exp, scale=-1.0), reads=[bsgm[qk]], writes=[bsgm[qk]])
                        P.op("dve", TT(qkT[pb][qk], ycv[qk], sgm[qk], ALU.mult), reads=[bycv[qk], bsgm[qk]], writes=[bqkT[pb][qk]])
                        yield
                    for j in range(4):
                        for kc in range(KC):
                            P.op("pe", MM(ps[:, 5, 384 + 2 * j:386 + 2 * j], H[:, kc, j * 128:(j + 1) * 128], W[:, kc, 768:770], kc == 0, kc == KC - 1),
                                 reads=[bW, bH], writes=[bps[5]])
                        yield
                    G = Gp[pb]
                    bG = bGp[pb]
                    gsb = G[:, 0:8].rearrange("p (j g) -> p j g", j=4)
                    th = G[:, 8:16].rearrange("p (j g) -> p j g", j=4)
                    e1, l1, li, nBe, tq, argw, argf = (G[:, 16 + 4 * i:20 + 4 * i] for i in range(7))
                    wv, fpv, eB = (G[:, 48 + 4 * i:52 + 4 * i] for i in range(3))
                    P.op("dve", TT(gsb, ps[:, 5, 384:392].rearrange("p (j g) -> p j g", j=4),
                                   gbv[:, hh:hh + 1, :].to_broadcast([128, 4, 2]), ALU.add), reads=[bps[5], bC], writes=[bG])
                    P.op("act", ACTF(th, gsb, AF.Exp, scale=2.0 / 15.0), reads=[bG], writes=[bG])
                    yield
                    P.op("act", ACTF(th, th, AF.Ln, bias=1.0), reads=[bG], writes=[bG])
                    P.op("act", ACTF(th, th, AF.Exp, scale=-1.0), reads=[bG], writes=[bG])
                    yield
                    P.op("dve", TS(th, th, -2.0, 1.0, ALU.mult, ALU.add), reads=[bG], writes=[bG])
                    P.op("act", ACTF(e1, th[:, :, 1], AF.Exp, scale=-15.0), reads=[bG], writes=[bG])
                    yield
                    P.op("act", ACTF(l1, e1, AF.Ln, bias=1.0), reads=[bG], writes=[bG])
                    P.op("dve", TS(li, th[:, :, 0], 15.0, None, ALU.mult), reads=[bG], writes=[bG])
                    yield
                    P.op("pe", MM(ps[:, 5, 400:404], tri, l1), reads=[bG, bC], writes=[bps[5]])
                    P.op("pe", MM(ps[:, 5, 416:420], onesf, l1), reads=[bG, bK], writes=[bps[5]])
                    yield
                    P.op("dve", CP(nBe, ps[:, 5, 416:420]), reads=[bps[5]], writes=[bG])
                    P.op("dve", TT(tq, li, nBe, ALU.subtract), reads=[bG], writes=[bG])
                    yield
                    P.op("dve", TT(argw, tq, ps[:, 5, 400:404], ALU.add), reads=[bG, bps[5]], writes=[bG])
                    P.op("dve", TT(argf, nBe, ps[:, 5, 400:404], ALU.subtract), reads=[bG, bps[5]], writes=[bG])
                    yield
                    P.op("act", ACTF(wv, argw, AF.Exp), reads=[bG], writes=[bG])
                    P.op("act", ACTF(fpv, argf, AF.Exp, bias=LNS), reads=[bG], writes=[bG])
                    P.op("act", ACTF(eB, nBe, AF.Exp, scale=-1.0), reads=[bG], writes=[bG])
                    yield

                def ml_tiles(blk, hi):
                    H = hbuf[hi]
                    bH = bhb[hi]
                    cur = blk >= 4
                    pb = blk % 2
                    G = Gp[pb]
                    bG = bGp[pb]
                    wv, fpv, eB = (G[:, 48 + 4 * i:52 + 4 * i] for i in range(3))
                    qT = qkT[pb][0]
                    kT = qkT[pb][1]
                    bqT = bqkT[pb][0]
                    bkT = bqkT[pb][1]
                    nv = 512 if cur else 256
                    mi = mo_cnt[0] % 2

                    def emit_vo(j):
                        vb_ = 2 + (j % 2)
                        for kc in range(KC):
                            P.op("pe", MM(ps[:, vb_, 0:nv], H[:, kc, j * 128:(j + 1) * 128], W[:, kc, 256:256 + nv], kc == 0, kc == KC - 1),
                                 reads=[bW, bH], writes=[bps[vb_]])

                    def tile(j):
                        t = blk * 4 + j
                        vb = 2 + (j % 2)
                        nb = 6 + (j % 2)
                        a = t % 2
                        P.op("dve", TS(vw[a][:, 0:256], ps[:, vb, 0:256], wv[:, j:j + 1], None, ALU.mult), reads=[bps[vb], bG], writes=[bvw[a]])
                        P.op("dve", CP(vw[a][:, 256:257], wv[:, j:j + 1]), reads=[bG], writes=[bvw[a]])
                        P.op("pe", TR(psbf(5)[:, 2 + a, :], kT[:, j * 128:(j + 1) * 128], identb), reads=[bkT, bK], writes=[bps[5]])
                        P.op("act", lambda e, a=a: e.copy(out=ktm[a], in_=psbf(5)[:, 2 + a, :]), reads=[bps[5]], writes=[bktm[a]])
                        yield
                        if cur:
                            u = cur_tile_head[0]
                            cur_tile_head[0] += 1
                            dd = dds[:, 8 * u:8 * u + 8]
                            bdd = P.buf(f"dd{u}")
                            P.op("act", lambda e, a=a, vb=vb: e.copy(out=vaug[a][:, 0:256], in_=ps[:, vb, 0:256]), reads=[bps[vb]], writes=[bva[a]])
                            P.op("act", ACTF(sig[a], ps[:, vb, 256:512], AF.Exp, scale=-1.0), reads=[bps[vb]], writes=[bsig[a]])
                            P.op("pe", MM(ps[:, 5, 0:128], kT[:, j * 128:(j + 1) * 128], qT[:, j * 128:(j + 1) * 128]),
                                 reads=[bqT, bkT], writes=[bps[5]])
                            P.op("dve", STT(ATs[a], ps[:, 5, 0:128], wv[:, j:j + 1], tri, ALU.mult, ALU.mult), reads=[bps[5], bG, bC], writes=[bATs[a]])
                            yield
                        if blk == 4 and j == 0:
                            P.op("dve", TS(C32, C32, flag[:, 0:1], None, ALU.mult), reads=[bC32, bC], writes=[bC32])
                        if cur:
                            P.op("dve", TS(Cb[a], C32, eB[:, j:j + 1], None, ALU.mult), reads=[bC32, bG], writes=[bCb[a]])
                            P.op("pe", MM(ps[:, nb, 0:257], ATs[a], vaug[a], True, False), reads=[bATs[a], bva[a]], writes=[bps[nb]])
                            P.op("pe", MM(ps[:, nb, 0:257], qT[:, j * 128:(j + 1) * 128], Cb[a], False, True), reads=[bqT, bCb[a]], writes=[bps[nb]])
                        P.op("pe", MM(ps[:, 4, 0:257], ktm[a], vw[a]), reads=[bktm[a], bvw[a]], writes=[bps[4]])
                        P.op("dve", STT(C32, C32, eB[:, j:j + 1], ps[:, 4, 0:257], ALU.mult, ALU.add), reads=[bC32, bG, bps[4]], writes=[bC32])
                        yield
                        if not cur:
                            return
                        P.op("dve", TT(dd[:, 0:1], ps[:, nb, 256:257], fpv[:, j:j + 1], ALU.mult), reads=[bps[nb], bG], writes=[bdd])
                        P.op("dve", TS(dd[:, 1:2], dd[:, 0:1], -1.0, 1.0, ALU.mult, ALU.max), reads=[bdd], writes=[bdd])
                        yield
                        P.op("dve", TT(dd[:, 1:2], dd[:, 1:2], dd[:, 0:1], ALU.max), reads=[bdd], writes=[bdd])
                        P.op("dve", RCP(dd[:, 1:2], dd[:, 1:2]), reads=[bdd], writes=[bdd])
                        yield
                        P.op("dve", TT(dd[:, 2:3], fpv[:, j:j + 1], dd[:, 1:2], ALU.mult), reads=[bdd, bG], writes=[bdd])
                        P.op("act", ACTF(junk2, ps[:, nb, 0:256], AF.Square, scale=dd[:, 2:3], accum=dd[:, 3:4]), reads=[bps[nb], bdd], writes=[bjunk2, bdd])
                        yield
                        P.op("act", ACTF(dd[:, 4:5], dd[:, 3:4], AF.Ln, scale=1.0 / 256.0, bias=EPS), reads=[bdd], writes=[bdd])
                        P.op("act", ACTF(dd[:, 4:5], dd[:, 4:5], AF.Exp, scale=-0.5), reads=[bdd], writes=[bdd])
                        yield
                        P.op("dve", TT(dd[:, 5:6], dd[:, 2:3], dd[:, 4:5], ALU.mult), reads=[bdd], writes=[bdd])
                        yield
                        P.op("dve", STT(hn[a], ps[:, nb, 0:256], dd[:, 5:6], wml[:, hh, :], ALU.mult, ALU.mult), reads=[bps[nb], bdd, bC], writes=[bhn[a]])
                        P.op("act", ACTF(sig[a], sig[a], AF.Ln, bias=1.0), reads=[bsig[a]], writes=[bsig[a]])
                        yield
                        P.op("act", ACTF(sig[a], sig[a], AF.Exp, scale=-1.0), reads=[bsig[a]], writes=[bsig[a]])
                        yield
                        P.op("dve", TT(hm[a], hn[a], sig[a], ALU.mult), reads=[bhn[a], bsig[a]], writes=[bhm[a]])
                        yield
                        for i2 in range(2):
                            P.op("pe", TR(psbf(5)[:, 4 + i2, :], hm[a][:, i2 * 128:(i2 + 1) * 128], identb), reads=[bhm[a], bK], writes=[bps[5]])
                        P.op("act", lambda e, mi=mi, j=j: e.copy(out=mob[mi][:, :, j * 128:(j + 1) * 128], in_=psbf(5)[:, 4:6, :]),
                             reads=[bps[5]], writes=[bmoa[mi]], relaxed=True)
                        yield

                    emit_vo(0)
                    yield
                    act_t = []
                    for j in range(4):
                        if j + 1 < 4:
                            emit_vo(j + 1)
                            yield
                        act_t.append(tile(j))
                        if len(act_t) == 2:
                            done0 = False
                            while not done0:
                                for g in list(act_t):
                                    try:
                                        next(g)
                                    except StopIteration:
                                        if g is act_t[0]:
                                            done0 = True
                                        act_t.remove(g)
                                yield
                        else:
                            for _ in range(3):
                                try:
                                    next(act_t[0])
                                except StopIteration:
                                    act_t.pop()
                                    break
                            yield
                    for g in act_t:
                        for _ in g:
                            yield
                    if cur:
                        cb4 = blk - 4
                        P.dma("sp", ms_d[2 * hh:2 * hh + 2, :, cb4 * 512:(cb4 + 1) * 512].rearrange("c p n -> p c n"), mob[mi],
                              reads=[bmoa[mi]], writes=[bms])
                        mo_cnt[0] += 1

                hmap[0] = nxt
                for _ in ml_prep(0, hmap[0]):
                    pass
                for blk in range(NBLK):
                    g2 = None
                    if blk + 1 < NBLK:
                        hmap[blk + 1] = load_h(blk + 1)
                        g2 = ml_prep(blk + 1, hmap[blk + 1])
                    interleave(ml_tiles(blk, hmap[blk]), g2)
            else:
                hp = pi - 4
                fin2q = []
                for blk in range(NBLK):
                    while fin2q:
                        fin2q.pop(0)()
                    hi = nxt
                    if blk + 1 < NBLK:
                        nxt = load_h(blk + 1)
                    H = hbuf[hi]
                    bH = bhb[hi]
                    cur = blk >= 4
                    c0 = 0 if cur else 128

                    def proj_mm(j):
                        b0 = 2 * (j % 2)
                        for hd in range(2):
                            for kc in range(KC):
                                P.op("pe", MM(ps[:, b0 + hd, c0:384], H[:, kc, j * 128:(j + 1) * 128], W[:, kc, hd * 384 + c0:hd * 384 + 384],
                                              kc == 0, kc == KC - 1), reads=[bW, bH], writes=[bps[b0 + hd]])

                    def tile_gen(j):
                        t = blk * 4 + j
                        b0 = 2 * (j % 2)
                        bb = [bps[b0], bps[b0 + 1]]
                        sl = t % 2
                        P.op("act", lambda e, t=t, b0=b0: e.copy(out=Vh[:, :, t, :], in_=ps[:, b0:b0 + 2, 256:384]), reads=bb, writes=[bVh], relaxed=True)
                        pq = ps[:, b0:b0 + 2, 0:256].rearrange("p h (g d) -> p h g d", g=4)
                        sq4 = sqb[sl].rearrange("p (h g) d -> p h g d", h=2)
                        t14 = t1[sl].rearrange("p (h g) d -> p h g d", h=2)
                        P.op("act", ACTF(sq4, pq, AF.Square), reads=bb, writes=[bsqb[sl]])
                        yield
                        P.op("dve", lambda e, sl=sl: e.tensor_reduce(out=rs[sl], in_=sqb[sl], axis=AX.X, op=ALU.add), reads=[bsqb[sl]], writes=[brs[sl]])
                        yield
                        P.op("act", ACTF(rs[sl], rs[sl], AF.Ln, scale=1.0 / 64.0, bias=EPS), reads=[brs[sl]], writes=[brs[sl]])
                        P.op("act", ACTF(rs[sl], rs[sl], AF.Exp, scale=-0.5), reads=[brs[sl]], writes=[brs[sl]])
                        yield
                        P.op("dve", TT(t14, pq, rs[sl].rearrange("p (h g) -> p h g", h=2)[:, :, :, None].to_broadcast([128, 2, 4, 64]), ALU.mult),
                             reads=bb + [brs[sl]], writes=[bt1[sl]])
                        yield
                        P.op("dve", TT(t1[sl], t1[sl], wqk, ALU.mult), reads=[bt1[sl], bC], writes=[bt1[sl]])
                        yield
                        P.op("act", lambda e, sl=sl: e.copy(out=qkb[sl], in_=t1[sl]), reads=[bt1[sl]], writes=[bqkb[sl]])
                        cosb = cosT[:, t:t + 1, :].to_broadcast([128, 8, 8])
                        sinb = sinT[:, t:t + 1, :].to_broadcast([128, 8, 8])
                        x1 = t1[sl][:, :, 0:8]
                        x2 = t1[sl][:, :, 8:16]
                        R_ = rt[sl]
                        P.op("dve", TT(R_[:, 0], x1, cosb, ALU.mult), reads=[bt1[sl], bK], writes=[brt[sl]])
                        P.op("dve", TT(R_[:, 1], x2, sinb, ALU.mult), reads=[bt1[sl], bK], writes=[brt[sl]], relaxed=True)
                        yield
                        P.op("dve", TT(R_[:, 2], x2, cosb, ALU.mult), reads=[bt1[sl], bK], writes=[brt[sl]], relaxed=True)
                        P.op("dve", TT(R_[:, 3], x1, sinb, ALU.mult), reads=[bt1[sl], bK], writes=[brt[sl]], relaxed=True)
                        yield
                        P.op("dve", TT(qkb[sl][:, :, 0:8], R_[:, 0], R_[:, 1], ALU.subtract), reads=[brt[sl]], writes=[bqkb[sl]])
                        P.op("dve", TT(qkb[sl][:, :, 8:16], R_[:, 2], R_[:, 3], ALU.add), reads=[brt[sl]], writes=[bqkb[sl]], relaxed=True)
                        yield
                        qk2 = qkb[sl].rearrange("p g d -> p (g d)")
                        for hd in range(2):
                            P.op("pe", TR(psbf(4)[:, 2 * hd, :], qk2[:, hd * 256 + 128:hd * 256 + 256], identb), reads=[bqkb[sl], bK], writes=[bps[4]])
                            if cur:
                                P.op("pe", TR(psbf(4)[:, 2 * hd + 1, :], qk2[:, hd * 256:hd * 256 + 128], identb), reads=[bqkb[sl], bK], writes=[bps[4]])
                        yield
                        for hd in range(2):
                            P.op("act", lambda e, hd=hd, t=t: e.copy(out=KT[hd][:, t * 128:(t + 1) * 128], in_=psbf(4)[:, 2 * hd, :]),
                                 reads=[bps[4]], writes=[bKT[hd]], relaxed=True)
                            if cur:
                                P.op("act", lambda e, hd=hd, j=j: e.copy(out=QT[hd][:, j * 128:(j + 1) * 128], in_=psbf(4)[:, 2 * hd + 1, :]),
                                     reads=[bps[4]], writes=[bQT[hd]], relaxed=True)
                        yield

                    proj_mm(0)
                    act_g = []
                    for j in range(4):
                        if j + 1 < 4:
                            proj_mm(j + 1)
                        act_g.append(tile_gen(j))
                        if len(act_g) == 2:
                            done0 = False
                            nstep = 0
                            while not done0:
                                for g in list(act_g):
                                    try:
                                        next(g)
                                    except StopIteration:
                                        if g is act_g[0]:
                                            done0 = True
                                        act_g.remove(g)
                                nstep += 1
                        elif j == 0:
                            for _ in range(4):
                                next(act_g[0])
                    for g in act_g:
                        for _ in g:
                            pass
                    if not cur:
                        continue
                    cb4 = blk - 4
                    ndone = 16 + 4 * cb4
                    for hd in range(2):
                        steps = [(c, 0, None) for c in range(ndone)] + [(ndone + jd, jd * 128, jd) for jd in (3, 2, 1, 0)]
                        nst = len(steps)

                        def emit_S(n):
                            c, q0, jd = steps[n]
                            sb0 = 4 + 2 * (n % 2)
                            for m in range(2):
                                P.op("pe", MM(ps[:, sb0 + m, q0:512], KT[hd][m * 64:(m + 1) * 64, c * 128:(c + 1) * 128], QT[hd][m * 64:(m + 1) * 64, q0:512]),
                                     reads=[bKT[hd], bQT[hd]], writes=[bps[sb0 + m]])

                        def emit_E(n):
                            c, q0, jd = steps[n]
                            sb0 = 4 + 2 * (n % 2)
                            pt = n % 3
                            P.op("act", ACTF(PT[pt][:, :, q0:512], ps[:, sb0:sb0 + 2, q0:512], AF.Exp, scale=0.125, bias=(pbias[:, 0:1] if c < 16 else None)),
                                 reads=[bps[sb0], bps[sb0 + 1], bC], writes=[bPT[pt]])
                            if jd is not None:
                                P.op("dve", MSET(PT[pt][64:128, :, q0:q0 + 64], 0.0), reads=[bPT[pt]], writes=[bPT[pt]])

                        def emit_PV(n):
                            c, q0, jd = steps[n]
                            pt = n % 3
                            for m in range(2):
                                P.op("pe", MM(ps[:, m, q0:512], Vh[:, hd, c, :], PT[pt][:, m, q0:512], c == 0, jd == 0),
                                     reads=[bVh, bPT[pt]], writes=[bps[m]])
                            if c == 0:
                                P.op("dve", CP(Pacc[0], PT[pt][:, 0, :]), reads=[bPT[pt]], writes=[bPacc[0]])
                            else:
                                P.op("dve", TT(Pacc[0][:, q0:512], Pacc[0][:, q0:512], PT[pt][:, 0, q0:512], ALU.add), reads=[bPT[pt], bPacc[0]], writes=[bPacc[0]])
                            P.op("pe", MM(ps[:, 3, q0:512], onesb, PT[pt][:, 1, q0:512], c == 0, jd == 0), reads=[bPT[pt], bK], writes=[bps[3]])

                        emit_S(0)
                        emit_S(1)
                        for n in range(nst):
                            emit_E(n)
                            if n + 2 < nst:
                                emit_S(n + 2)
                            emit_PV(n)
                            if fin2q and n >= 2:
                                fin2q.pop(0)()
                        while fin2q:
                            fin2q.pop(0)()
                        P.op("pe", MM(ps[:, 2, :], onesf, Pacc[0]), reads=[bPacc[0], bK], writes=[bps[2]])
                        P.op("dve", RCP(fz[0], ps[:, 2, :]), reads=[bps[2]], writes=[bfz[0]])
                        P.op("dve", TT(fz[1], ps[:, 0, :], fz[0], ALU.mult), reads=[bps[0], bfz[0]], writes=[bfz[1]])
                        P.op("dve", RCP(fz[2], ps[:, 3, :]), reads=[bps[3]], writes=[bfz[2]])
                        P.op("dve", TT(fz[3], ps[:, 1, :], fz[2], ALU.mult), reads=[bps[1], bfz[2]], writes=[bfz[3]])

                        def f2a():
                            P.op("dve", STT(fz[1], fz[3], neglam[:, 0:1], fz[1], ALU.mult, ALU.add), reads=[bfz[3], bfz[1], bK], writes=[bfz[1]])

                        def f2b():
                            P.op("act", ACTF(fz[4], fz[1], AF.Square), reads=[bfz[1]], writes=[bfz[4]])

                        def f2c():
                            P.op("pe", MM(ps[:, 2, :], onesf, fz[4]), reads=[bfz[4], bK], writes=[bps[2]])

                        def f2d():
                            P.op("act", ACTF(fz[0], ps[:, 2, :], AF.Ln, scale=1.0 / 128.0, bias=EPS), reads=[bps[2]], writes=[bfz[0]])

                        def f2e():
                            P.op("act", ACTF(fz[0], fz[0], AF.Exp, scale=-0.5), reads=[bfz[0]], writes=[bfz[0]])

                        def f2f(hd=hd, cb4=cb4):
                            mi = mo_cnt[0] % 2
                            mo_cnt[0] += 1
                            P.op("dve", STT(mob[mi][:, 0, :], fz[1], subw_s[:, 0:1], fz[0], ALU.mult, ALU.mult), reads=[bfz[1], bfz[0], bK], writes=[bmoa[mi]])
                            ch = 8 + 2 * hp + hd
                            P.dma("sp", ms_d[ch, :, cb4 * 512:(cb4 + 1) * 512], mob[mi][:, 0, :], reads=[bmoa[mi]], writes=[bms])

                        fin2q.extend([f2a, f2b, f2c, f2d, f2e, f2f])
                while fin2q:
                    fin2q.pop(0)()
        P.barrier()
    A.off = base_off

    bout = P.buf("out")
    if "C" in phases:
        h2T = A.alloc([KC, 1024], BF16)
        bh2a, bh2d = P.buf("h2a"), P.buf("h2d")
        wgu = [A.alloc([KC, 256], BF16) for _ in range(2)]
        bwgu = [P.buf(f"wgu{i}") for i in range(2)]
        wdb = [A.alloc([NFF, 128], BF16) for _ in range(2)]
        bwd = [P.buf(f"wd{i}") for i in range(2)]
        tmpf = [A.alloc([512], F32) for _ in range(2)]
        btmp = [P.buf(f"tmp{i}") for i in range(2)]
        xs = [A.alloc([8, 128], F32) for _ in range(2)]
        bxs = [P.buf(f"xs{i}") for i in range(2)]
        xn = [A.alloc([D], BF16) for _ in range(2)]
        bxn = [P.buf(f"xn{i}") for i in range(2)]
        junk = A.alloc([D], BF16)
        bjunk = P.buf("junk")
        sg = [A.alloc([512], F32) for _ in range(2)]
        bsg = [P.buf(f"sg{i}") for i in range(2)]
        un0 = A.off
        mxb = [A.alloc([KC, 512], BF16) for _ in range(2)]
        bmx = [P.buf(f"mx{i}") for i in range(2)]
        xsb = A.alloc([4, D], F32)
        bxsb = P.buf("xsb")
        A.off = un0
        actT = A.alloc([NFF, 1024], BF16)
        bact = P.buf("actT")
        wob = [wgu[i][:, :, 0:128] for i in range(2)]
        out_v = out_d.rearrange("(t p) d -> p t d", p=128)
        tcount = [0]
        wcount = [0]
        for TB in range(2):
            for sb in range(2):
                g512 = TB * 2 + sb
                mi = g512 % 2
                P.dma("sp", mxb[mi], ms_d[:, :, g512 * 512:(g512 + 1) * 512].rearrange("c p n -> p c n"),
                      reads=[bms], writes=[bmx[mi]])
                P.dma("sp", xsb, xv[:, 16 + 4 * g512:16 + 4 * g512 + 4, :], writes=[bxsb])
                pend = None
                for m in range(16):
                    wi = wcount[0] % 2
                    wcount[0] += 1
                    P.dma("pool", wob[wi], wo_d[m], writes=[bwgu[wi]])
                    ob = m % 2
                    for kc in range(KC):
                        P.op("pe", MM(ps[:, ob, :], wob[wi][:, kc, :], mxb[mi][:, kc, :], kc == 0, kc == KC - 1),
                             reads=[bwgu[wi], bmx[mi]], writes=[bps[ob]])
                    ti = tcount[0] % 2
                    tcount[0] += 1
                    P.op("act", ACTF(tmpf[ti], ps[:, ob, :], AF.Copy, scale=gate1[:, m:m + 1]), reads=[bps[ob], bK], writes=[btmp[ti]])
                    if pend is not None:
                        pend()

                    def pend(m=m, ti=ti):
                        tb = 2 + ti
                        for j in range(4):
                            P.op("pe", TR(ps[:, tb, j * 128:(j + 1) * 128], tmpf[ti][:, j * 128:(j + 1) * 128], identf), reads=[btmp[ti], bC], writes=[bps[tb]])
                        xslice = xsb[:, :, m * 128:(m + 1) * 128]
                        P.op("dve", TT(xslice, ps[:, tb, :].rearrange("p (j f) -> p j f", j=4), xslice, ALU.add), reads=[bps[tb], bxsb], writes=[bxsb])
                pend()
                ntiles = []
                for j in range(4):
                    col = sb * 512 + j * 128
                    ntiles.append((xsb[:, j, :], bxsb, A2, Sh2, (lambda kc, col=col: h2T[:, kc, col:col + 128]), bh2a, bh2d))
                norm_pipeline(ntiles, xn, bxn, junk, bjunk)
                P.dma("sp", out_v[:, 4 * g512:4 * g512 + 4, :], xsb, reads=[bxsb], writes=[bout])
            P.barrier()
            for jf in range(NFF):
                wi = wcount[0] % 2
                wcount[0] += 1
                P.dma("pool", wgu[wi], wgu_d[jf], writes=[bwgu[wi]])
                for sb in range(2):
                    for gu in range(2):
                        ob = 2 * (sb % 2) + gu + (4 if jf % 2 else 0)
                        for kc in range(KC):
                            P.op("pe", MM(ps[:, ob, :], wgu[wi][:, kc, gu * 128:(gu + 1) * 128], h2T[:, kc, sb * 512:(sb + 1) * 512], kc == 0, kc == KC - 1),
                                 reads=[bwgu[wi], bh2a, bh2d], writes=[bps[ob]])
                    ob = 2 * (sb % 2) + (4 if jf % 2 else 0)
                    si = tcount[0] % 2
                    tcount[0] += 1
                    P.op("act", ACTF(sg[si], ps[:, ob, :], AF.Silu), reads=[bps[ob]], writes=[bsg[si]])
                    P.op("dve", TT(actT[:, jf, sb * 512:(sb + 1) * 512], sg[si], ps[:, ob + 1, :], ALU.mult), reads=[bsg[si], bps[ob + 1]], writes=[bact], relaxed=True)
            pend = None
            for m in range(16):
                wi = m % 2
                P.dma("pool", wdb[wi], wd_d[m], writes=[bwd[wi]])
                xi = m % 2
                P.dma("sp", xs[xi], out_v[:, 8 * TB:8 * TB + 8, m * 128:(m + 1) * 128], reads=[bout], writes=[bxs[xi]], sem_buf=bxs[xi])
                for sb in range(2):
                    ob = sb
                    for fc in range(NFF):
                        P.op("pe", MM(ps[:, ob, :], wdb[wi][:, fc, :], actT[:, fc, sb * 512:(sb + 1) * 512], fc == 0, fc == NFF - 1),
                             reads=[bwd[wi], bact], writes=[bps[ob]])
                    ti = tcount[0] % 2
                    tcount[0] += 1
                    P.op("act", ACTF(tmpf[ti], ps[:, ob, :], AF.Copy, scale=gate2[:, m:m + 1]), reads=[bps[ob], bK], writes=[btmp[ti]])
                    if pend is not None:
                        pend()

                    def pend(m=m, sb=sb, ti=ti, xi=xi):
                        tb = 2 + ti
                        for j in range(4):
                            P.op("pe", TR(ps[:, tb, j * 128:(j + 1) * 128], tmpf[ti][:, j * 128:(j + 1) * 128], identf), reads=[btmp[ti], bC], writes=[bps[tb]])
                        xslice = xs[xi][:, sb * 4:(sb + 1) * 4, :]
                        P.op("dve", TT(xslice, ps[:, tb, :].rearrange("p (j f) -> p j f", j=4), xslice, ALU.add), reads=[bps[tb], bxs[xi]], writes=[bxs[xi]])
                        if sb == 1:
                            bo2 = P.buf(f"o2_{TB}_{m}")
                            P.dma("sp", out_v[:, 8 * TB:8 * TB + 8, m * 128:(m + 1) * 128], xs[xi], reads=[bxs[xi]], writes=[bout, bo2], sem_buf=bo2)
            pend()
            P.barrier()
    P.barrier()
    P.emit()
    return nc, P


def _prep_shared(inp):
    f = np.float32
    w_in = np.asarray(inp["w_in"], f)[0]
    cols = []
    for hh in range(4):
        cols += list(range(hh * 128, hh * 128 + 128)) + list(range(512 + hh * 128, 512 + hh * 128 + 128))
        cols += list(range(1024 + hh * 256, 1024 + hh * 256 + 256)) + list(range(2048 + hh * 256, 2048 + hh * 256 + 256))
        cols += [3072 + hh, 3076 + hh]
    for hp in range(4):
        for hd in range(2):
            h = 2 * hp + hd
            cols += list(range(3080 + h * 128, 3080 + h * 128 + 128)) + list(range(4104 + h * 128, 4104 + h * 128 + 128))
            cols += list(range(5128 + h * 128, 5128 + h * 128 + 128))
    cols = np.asarray(cols)
    sh = {}
    sh["w_in_l"] = np.ascontiguousarray(w_in.reshape(KC, 128, N_IN).transpose(1, 0, 2)[:, :, cols])
    sh["w_ada_l"] = np.ascontiguousarray(np.asarray(inp["w_ada"], f)[0].reshape(KC, 128, 6 * D))
    sh["wo_l"] = np.ascontiguousarray(np.asarray(inp["w_out"], f)[0].reshape(KC, 128, 16, 128).transpose(2, 1, 0, 3))
    wgu = np.asarray(inp["w_gate_up"], f)[0].reshape(KC, 128, 2, NFF, 128)
    sh["wgu_l"] = np.ascontiguousarray(wgu.transpose(3, 1, 0, 2, 4).reshape(NFF, 128, KC, 256))
    sh["wd_l"] = np.ascontiguousarray(np.asarray(inp["w_down"], f)[0].reshape(NFF, 128, 16, 128).transpose(2, 1, 0, 3))
    return sh


def _prep_cst(inp, b, half):
    f = np.float32
    c = np.zeros((128, NCST), f)

    def put(name, arr):
        o, n = CO[name]
        c[:, o:o + n] = np.asarray(arr, f).reshape(128, n)

    rep = lambda v: np.broadcast_to(np.asarray(v, f).reshape(1, -1), (128, np.asarray(v).size))
    put("c_l", np.asarray(inp["c"], f)[b].reshape(KC, 128).T)
    put("b_l", np.asarray(inp["b_ada"], f)[0].reshape(96, 128).T)
    put("n1w", np.asarray(inp["norm1_w"], f)[0].reshape(KC, 128).T)
    put("n2w", np.asarray(inp["norm2_w"], f)[0].reshape(KC, 128).T)
    cw = np.asarray(inp["mlstm_conv_w"], f)[0].reshape(4, 2, 4, 128)
    put("cw", cw.transpose(3, 2, 1, 0).reshape(128, 32))
    cb = np.asarray(inp["mlstm_conv_b"], f)[0].reshape(2, 4, 128)
    put("cb", cb.transpose(2, 1, 0).reshape(128, 8))
    gb = np.asarray(inp["mlstm_gate_b"], f)[0].reshape(2, 4)
    put("gb", rep(gb.T.reshape(-1)))
    put("nw", rep(np.asarray(inp["mlstm_norm_w"], f)[0].reshape(-1)))
    qw = np.asarray(inp["q_norm_w"], f)[0]
    kw = np.asarray(inp["k_norm_w"], f)[0]
    put("wqk", rep(np.concatenate([qw, qw, kw, kw, qw, qw, kw, kw])))
    put("lamv", rep(np.concatenate([np.asarray(inp[k], f)[0] for k in ("lambda_q1", "lambda_k1", "lambda_q2", "lambda_k2")])))
    put("subw", np.asarray(inp["subln_w"], f)[0].reshape(128, 1))
    put("flag", np.full((128, 1), float(half), f))
    put("pbias", np.full((128, 1), 0.0 if half else -30000.0, f))
    invf = (np.float32(500000.0) ** (-np.arange(0, 16, 2, dtype=np.float32) / np.float32(16))).astype(f)
    put("invf", rep(invf))
    put("ident", np.eye(128, dtype=f))
    put("tri", np.triu(np.ones((128, 128), f)))
    return c


def _in_maps(inp):
    sh = _prep_shared(inp)
    x = np.asarray(inp["x"], np.float32)
    pos = np.asarray(inp["positions"], np.int32)
    maps = []
    for core in range(8):
        b, half = core // 2, core % 2
        m = dict(sh)
        m["x"] = np.ascontiguousarray(np.concatenate([x[b, 0:TOK], x[b, half * TOK:(half + 1) * TOK]], axis=0))
        pp = np.concatenate([pos[b, 0:TOK], pos[b, half * TOK:(half + 1) * TOK]])
        m["pos"] = np.ascontiguousarray(pp.reshape(NTILE, 128).T)
        m["cst"] = _prep_cst(inp, b, half)
        maps.append(m)
    return maps


_NC = None


def kernel(**inputs):
    global _NC
    if _NC is None:
        _NC = build()[0]
    maps = _in_maps(inputs)
    res = run_bass_kernel_spmd(_NC, maps, core_ids=list(range(8)))
    out = np.empty((4, 2 * TOK, D), np.float32)
    for core in range(8):
        b, half = core // 2, core % 2
        out[b, half * TOK:(half + 1) * TOK] = res.results[core]["out"]
    return out
```

```python
import math
import os
import numpy as np
import concourse.bass as bass
import concourse.mybir as mybir
from concourse.bass_utils import run_bass_kernel_spmd

F32 = mybir.dt.float32
BF16 = mybir.dt.bfloat16
I32 = mybir.dt.int32
AF = mybir.ActivationFunctionType
ALU = mybir.AluOpType
AX = mybir.AxisListType

SEM_EPOCH = 16000
import os as _os
SAME_ENGINE_SYNC = _os.environ.get("K_SES", "1") == "1"


class Buf:
    __slots__ = ("name", "w", "rd", "dsem", "dcnt", "excl")

    def __init__(self, name, excl=False):
        self.name = name
        self.excl = excl
        self.w = None
        self.rd = {}
        self.dsem = None
        self.dcnt = 0


class Op:
    __slots__ = ("eng", "fn", "waits", "need_sig", "sigidx", "dma_sem")

    def __init__(self, eng, fn):
        self.eng = eng
        self.fn = fn
        self.waits = []
        self.need_sig = False
        self.sigidx = None
        self.dma_sem = None


class Prog:
    ENGS = ("pe", "act", "dve", "pool", "sp")

    def __init__(self, nc):
        self.nc = nc
        self.ops = {e: [] for e in self.ENGS}
        self.nsem = 0
        self.bufs = []

    def buf(self, name, excl=False):
        b = Buf(name, excl)
        self.bufs.append(b)
        return b

    def _newsem(self, name):
        self.nsem += 1
        return self.nc.alloc_semaphore(f"s{self.nsem}_{name}")

    def _collect(self, op, reads, writes, relaxed):
        toks = []
        for b in reads:
            if b.w is not None:
                toks.append((b.w, False))
            if b.excl:
                for k, t in b.rd.items():
                    if k != op.eng:
                        toks.append((t, False))
        for b in writes:
            if b.w is not None:
                toks.append((b.w, True))
            for t in b.rd.values():
                toks.append((t, False))
        for t, waw in toks:
            if t[0] == "op":
                p = t[1]
                if p is op:
                    continue
                if p.eng == op.eng and (p.eng == "pe" or not SAME_ENGINE_SYNC or (waw and relaxed)):
                    continue
                p.need_sig = True
            op.waits.append(t)

    def op(self, eng, fn, reads=(), writes=(), relaxed=False):
        o = Op(eng, fn)
        self._collect(o, reads, writes, relaxed)
        tok = ("op", o)
        for b in reads:
            b.rd[eng] = tok
        for b in writes:
            b.w = tok
            b.rd = {}
        self.ops[eng].append(o)
        return o

    def dma(self, eng, out_ap, in_ap, reads=(), writes=(), sem_buf=None):
        sb = sem_buf or (writes[0] if writes else reads[0])
        if sb.dsem is None:
            sb.dsem = self._newsem(sb.name)
        sb.dcnt += 16
        sem, cnt = sb.dsem, sb.dcnt

        def fn(e, out_ap=out_ap, in_ap=in_ap):
            return e.dma_start(out=out_ap, in_=in_ap)

        o = Op(eng, fn)
        o.dma_sem = sem
        self._collect(o, reads, writes, False)
        tok = ("sem", sem, cnt)
        for b in reads:
            b.rd[("d", id(sem))] = tok
        for b in writes:
            b.w = tok
            b.rd = {}
        self.ops[eng].append(o)
        return o

    def barrier(self):
        toks = []
        for b in self.bufs:
            if b.w is not None:
                toks.append(b.w)
            toks.extend(b.rd.values())
        for e in self.ENGS:
            for o in reversed(self.ops[e]):
                if o.fn is not None and o.dma_sem is None:
                    toks.append(("op", o))
                    break
        for e in self.ENGS:
            o = Op(e, None)
            for t in toks:
                if t[0] == "op":
                    if t[1].eng == e:
                        continue
                    t[1].need_sig = True
                o.waits.append(t)
            self.ops[e].append(o)
        for b in self.bufs:
            b.w = None
            b.rd = {}

    def final_wait(self, eng, bufs):
        o = Op(eng, None)
        for b in bufs:
            if b.w is not None:
                o.waits.append(b.w)
                if b.w[0] == "op":
                    b.w[1].need_sig = True
        self.ops[eng].append(o)

    def emit(self):
        nc = self.nc
        eng_sems = {}
        for e in self.ENGS:
            n = 0
            for o in self.ops[e]:
                if o.need_sig and o.dma_sem is None and o.fn is not None:
                    o.sigidx = n
                    n += 1
            eng_sems[e] = [self._newsem(f"{e}{i}") for i in range((n + SEM_EPOCH - 1) // SEM_EPOCH)]
        self.stats = {e: len(self.ops[e]) for e in self.ENGS}
        self.stats["nsem"] = self.nsem

        def resolve(t):
            if t[0] == "sem":
                return t[1], t[2]
            p = t[1]
            assert p.sigidx is not None, "waiting on op without signal"
            return eng_sems[p.eng][p.sigidx // SEM_EPOCH], p.sigidx % SEM_EPOCH + 1

        def run(e, h):
            waited = {}
            for o in self.ops[e]:
                need = {}
                for t in o.waits:
                    s, v = resolve(t)
                    k = id(s)
                    if waited.get(k, 0) >= v:
                        continue
                    if k not in need or need[k][1] < v:
                        need[k] = (s, v)
                for k, (s, v) in need.items():
                    h.wait_ge(s, v)
                    waited[k] = v
                if o.fn is None:
                    continue
                ins = o.fn(h)
                if o.dma_sem is not None:
                    ins.then_inc(o.dma_sem, 16)
                elif o.need_sig:
                    ins.then_inc(eng_sems[e][o.sigidx // SEM_EPOCH], 1)

        with nc.Block() as block:
            @block.tensor
            def _(h):
                run("pe", h)

            @block.scalar
            def _(h):
                run("act", h)

            @block.vector
            def _(h):
                run("dve", h)

            @block.gpsimd
            def _(h):
                run("pool", h)

            @block.sync
            def _(h):
                run("sp", h)


def MM(out, lhsT, rhs, start=True, stop=True):
    return lambda e: e.matmul(out, lhsT=lhsT, rhs=rhs, start=start, stop=stop)


def TR(out, in_, ident):
    return lambda e: e.transpose(out=out, in_=in_, identity=ident)


def ACTF(out, in_, func, scale=None, bias=None, accum=None):
    kw = {}
    if scale is not None:
        kw["scale"] = scale
    if bias is not None:
        kw["bias"] = bias
    if accum is not None:
        kw["accum_out"] = accum
    return lambda e: e.activation(out=out, in_=in_, func=func, **kw)


def TS(out, in0, s1, s2=None, op0=ALU.mult, op1=None):
    if op1 is None:
        return lambda e: e.tensor_scalar(out=out, in0=in0, scalar1=s1, scalar2=None, op0=op0)
    return lambda e: e.tensor_scalar(out=out, in0=in0, scalar1=s1, scalar2=s2, op0=op0, op1=op1)


def TT(out, in0, in1, op):
    return lambda e: e.tensor_tensor(out=out, in0=in0, in1=in1, op=op)


def STT(out, in0, scalar, in1, op0, op1):
    return lambda e: e.scalar_tensor_tensor(out=out, in0=in0, scalar=scalar, in1=in1, op0=op0, op1=op1)


def CP(out, in_):
    return lambda e: e.tensor_copy(out=out, in_=in_)


def MSET(ap, v):
    return lambda e: e.memset(ap, v)


def RCP(out, in_):
    return lambda e: e.reciprocal(out=out, in_=in_)


D = 2048
KC = 16
TOK = 2048
NTILE = 32
NBLK = 8
DFF = 5632
NFF = 44
N_IN = 6152
EPS = 1e-6
LAMBDA_INIT = 0.8 - 0.6 * math.exp(-0.3 * 0)
ML_W = 770
DA_W = 768
ARENA_BYTES = 211968

_CST = [("c_l", 16), ("b_l", 96), ("n1w", 16), ("n2w", 16), ("cw", 32), ("cb", 8), ("gb", 8), ("nw", 1024),
        ("wqk", 512), ("lamv", 256), ("subw", 1), ("flag", 1), ("pbias", 1), ("invf", 8), ("ident", 128),
        ("tri", 128)]
CO = {}
_o = 0
for _n, _s in _CST:
    CO[_n] = (_o, _s)
    _o += _s
NCST = _o


class Arena:
    def __init__(self, nc):
        self.t = nc.alloc_sbuf_tensor("arena", [128, ARENA_BYTES // 4], F32)
        self.off = 0

    def alloc(self, shape, dtype, parts=128, p0=0):
        n = int(np.prod(shape))
        esz = 2 if dtype == BF16 else 4
        nb = (n * esz + 31) // 32 * 32
        o = self.off
        self.off += nb
        assert self.off <= ARENA_BYTES, f"SBUF arena overflow {self.off}"
        v = self.t[p0:p0 + parts, o // 4:(o + nb) // 4]
        if dtype != F32:
            v = v.bitcast(dtype)
        v = v[:, 0:n]
        if len(shape) > 1:
            names = [f"a{i}" for i in range(len(shape))]
            kw = {nm: int(s) for nm, s in zip(names, shape)}
            v = v.rearrange("p (" + " ".join(names) + ") -> p " + " ".join(names), **kw)
        return v


def build(dbg=False, phases="0ABC"):
    nc = bass.Bass("TRN2", target_bir_lowering=False)
    P = Prog(nc)
    A = Arena(nc)

    def din(name, shape, dt=F32):
        return nc.dram_tensor(name, list(shape), dt, kind="ExternalInput").ap()

    def dscr(name, shape, dt):
        if dbg:
            return nc.dram_tensor(name, list(shape), dt, kind="ExternalOutput").ap()
        return nc.dram_tensor(name, list(shape), dt).ap()

    x_d = din("x", [2 * TOK, D])
    pos_d = din("pos", [128, NTILE], I32)
    cst_d = din("cst", [128, NCST])
    wada_d = din("w_ada_l", [KC, 128, 6 * D]).rearrange("k p n -> p k n")
    win_d = din("w_in_l", [128, KC, N_IN])
    wo_d = din("wo_l", [16, 128, KC, 128])
    wgu_d = din("wgu_l", [NFF, 128, KC, 256])
    wd_d = din("wd_l", [16, 128, NFF, 128])
    out_d = nc.dram_tensor("out", [TOK, D], F32, kind="ExternalOutput").ap()
    hs_d = dscr("hs", [NBLK, 128, KC, 512], BF16)
    ms_d = dscr("ms", [16, 128, TOK], BF16)
    modd = dscr("modd", [1, 6 * D], F32)

    ps = nc.alloc_psum_tensor("ps", [128, 8, 512], F32)
    bps = [P.buf(f"ps{i}", excl=True) for i in range(8)]

    def psbf(bank):
        return ps[:, bank, :].bitcast(BF16).rearrange("p (a b) -> p a b", a=8)

    CST = A.alloc([NCST], F32)
    bC = P.buf("cst")
    bK = P.buf("derived")
    P.dma("sp", CST, cst_d, writes=[bC])

    def cs(name):
        o, n = CO[name]
        return CST[:, o:o + n]

    identf = cs("ident")
    tri = cs("tri")
    flag = cs("flag")
    pbias = cs("pbias")
    posi = A.alloc([NTILE], I32)
    P.dma("sp", posi, pos_d, writes=[bK])
    identb = A.alloc([128], BF16)
    onesf = A.alloc([128], F32)
    onesb = A.alloc([128], BF16)
    mod = A.alloc([96], F32)
    A1 = A.alloc([16], F32)
    A2 = A.alloc([16], F32)
    Sh1 = mod[:, 0:16]
    Sh2 = mod[:, 48:64]
    gate1 = mod[:, 32:48]
    gate2 = mod[:, 80:96]
    cosT = A.alloc([NTILE, 8], F32)
    sinT = A.alloc([NTILE, 8], F32)
    stats = A.alloc([128], F32)
    neglam = A.alloc([1], F32)
    subw_s = A.alloc([1], F32)
    sc = A.alloc([16], F32)
    P.op("dve", CP(identb, identf), reads=[bC], writes=[bK])
    P.op("dve", MSET(onesf, 1.0), writes=[bK])
    P.op("dve", MSET(onesb, 1.0), writes=[bK])
    P.op("dve", MSET(stats, 0.0), writes=[bK])
    rp = A.off
    posf = A.alloc([NTILE], F32)
    ang = A.alloc([NTILE, 8], F32)
    ang2 = A.alloc([NTILE, 8], F32)
    ki = A.alloc([NTILE, 8], I32)
    kf = A.alloc([NTILE, 8], F32)
    P.op("dve", CP(posf, posi), reads=[bK], writes=[bK])
    P.op("dve", TT(ang, posf[:, :, None].to_broadcast([128, NTILE, 8]),
                   cs("invf")[:, None, :].to_broadcast([128, NTILE, 8]), ALU.mult), reads=[bK, bC], writes=[bK])
    for (dst, shift) in ((sinT, 0.0), (cosT, math.pi / 2)):
        P.op("dve", TS(ang2, ang, shift, None, ALU.add), reads=[bK], writes=[bK])
        P.op("dve", TS(ki, ang2, 1.0 / (2 * math.pi), None, ALU.mult), reads=[bK], writes=[bK])
        P.op("dve", CP(kf, ki), reads=[bK], writes=[bK])
        P.op("dve", STT(ang2, kf, -2 * math.pi, ang2, ALU.mult, ALU.add), reads=[bK], writes=[bK])
        P.op("dve", TS(ang2, ang2, -3.1415925, 3.1415925, ALU.max, ALU.min), reads=[bK], writes=[bK])
        P.op("act", ACTF(dst, ang2, AF.Sin), reads=[bK], writes=[bK])
    lamv = cs("lamv").rearrange("p (a b) -> p a b", a=4)
    lt = A.alloc([2, 64], F32)
    ls = A.alloc([2], F32)
    P.op("dve", TT(lt[:, 0, :], lamv[:, 0, :], lamv[:, 1, :], ALU.mult), reads=[bC], writes=[bK])
    P.op("dve", TT(lt[:, 1, :], lamv[:, 2, :], lamv[:, 3, :], ALU.mult), reads=[bC, bK], writes=[bK])
    P.op("dve", lambda e: e.tensor_reduce(out=ls, in_=lt, axis=AX.X, op=ALU.add), reads=[bK], writes=[bK])
    P.op("act", ACTF(ls, ls, AF.Exp), reads=[bK], writes=[bK])
    P.op("dve", STT(neglam, ls[:, 1:2], -LAMBDA_INIT, ls[:, 0:1], ALU.add, ALU.subtract), reads=[bK], writes=[bK])
    P.op("dve", TS(subw_s, cs("subw"), 1.0 - LAMBDA_INIT, None, ALU.mult), reads=[bC], writes=[bK])
    P.op("act", ACTF(sc, cs("c_l"), AF.Silu), reads=[bC], writes=[bK])
    base_off = A.off

    norm_uid = [0]

    def norm_s1(xsrc, bx, xn, bxn, junk, bjunk):
        uid = norm_uid[0]
        norm_uid[0] += 1
        st = stats[:, 2 * uid:2 * uid + 2]
        bst = P.buf(f"st{uid}")
        k = uid % 2
        P.op("act", ACTF(junk, xsrc, AF.Square, accum=st[:, 0:1]), reads=[bx, bK], writes=[bjunk, bst], relaxed=True)
        P.op("act", ACTF(st[:, 1:2], st[:, 0:1], AF.Ln, scale=1.0 / D, bias=EPS), reads=[bst], writes=[bst])
        P.op("act", ACTF(st[:, 1:2], st[:, 1:2], AF.Exp, scale=-0.5), reads=[bst], writes=[bst])
        P.op("dve", TS(xn[k], xsrc, st[:, 1:2], None, ALU.mult), reads=[bx, bst], writes=[bxn[k]])
        return k

    def norm_s2(k, Av, Shv, dst_of_kc, bdst_act, bdst_dve, xn, bxn):
        for hf in range(2):
            bank = 4 + 2 * k + hf
            pv = psbf(bank)
            for q in range(8):
                kc = hf * 8 + q
                P.op("pe", TR(pv[:, q, :], xn[k][:, kc * 128:(kc + 1) * 128], identb), reads=[bxn[k], bK], writes=[bps[bank]])
            for q in range(8):
                kc = hf * 8 + q
                if hf == 0 and q < 6:
                    P.op("act", ACTF(dst_of_kc(kc), pv[:, q, :], AF.Identity, scale=Av[:, kc:kc + 1], bias=Shv[:, kc:kc + 1]),
                         reads=[bps[bank], bK], writes=[bdst_act], relaxed=True)
                else:
                    P.op("dve", TS(dst_of_kc(kc), pv[:, q, :], Av[:, kc:kc + 1], Shv[:, kc:kc + 1], ALU.mult, ALU.add),
                         reads=[bps[bank], bK], writes=[bdst_dve], relaxed=True)

    def norm_pipeline(tiles, xn, bxn, junk, bjunk, after=None):
        ks = {}
        ks[0] = norm_s1(tiles[0][0], tiles[0][1], xn, bxn, junk, bjunk)
        for t, tl in enumerate(tiles):
            if t + 1 < len(tiles):
                ks[t + 1] = norm_s1(tiles[t + 1][0], tiles[t + 1][1], xn, bxn, junk, bjunk)
            norm_s2(ks[t], tl[2], tl[3], tl[4], tl[5], tl[6], xn, bxn)
            if after is not None:
                after(t)

    xv = x_d.rearrange("(t p) d -> p t d", p=128)
    bhs1 = P.buf("hs")
    bhs = [bhs1] * NBLK

    if "A" in phases:
        mrow = [A.alloc([512], F32, parts=1) for _ in range(2)]
        bmr = [P.buf(f"mrow{i}") for i in range(2)]
        wab = [A.alloc([KC, 512], F32) for _ in range(2)]
        bwa = [P.buf(f"wab{i}") for i in range(2)]
        bmd = P.buf("modd")
        modT = A.alloc([128], F32, parts=96)
        bmT = P.buf("modT")

        def ada_cols(cb):
            i = cb % 2
            P.dma("sp", wab[i], wada_d[:, :, cb * 512:(cb + 1) * 512], writes=[bwa[i]])
            for kc in range(KC):
                P.op("pe", MM(ps[0:1, i, :], sc[:, kc:kc + 1], wab[i][:, kc, :], kc == 0, kc == KC - 1),
                     reads=[bwa[i], bK], writes=[bps[i]])
            P.op("act", lambda e, i=i: e.copy(out=mrow[i], in_=ps[0:1, i, :]), reads=[bps[i]], writes=[bmr[i]])
            P.dma("pool", modd[:, cb * 512:(cb + 1) * 512], mrow[i], reads=[bmr[i]], writes=[bmd])

        def ada_finish(j0, j1):
            n = j1 - j0
            P.dma("sp", modT[0:n, :], modd[:, j0 * 128:j1 * 128].rearrange("o (j p) -> (o j) p", p=128), reads=[bmd], writes=[bmT])
            P.op("pe", TR(ps[:, 2, 0:n], modT[0:n, :], identf[0:n, 0:n]), reads=[bmT, bC], writes=[bps[2]])
            P.op("dve", TT(mod[:, j0:j1], ps[:, 2, 0:n], cs("b_l")[:, j0:j1], ALU.add), reads=[bps[2], bC], writes=[bK])

        for cb in range(8):
            ada_cols(cb)
        ada_finish(0, 32)
        P.op("dve", STT(A1, mod[:, 16:32], 1.0, cs("n1w"), ALU.add, ALU.mult), reads=[bK, bC], writes=[bK])

        xb = [A.alloc([4, D], F32) for _ in range(2)]
        bxb = [P.buf(f"xb{i}") for i in range(2)]
        xn = [A.alloc([D], BF16) for _ in range(2)]
        bxn = [P.buf(f"xn{i}") for i in range(2)]
        junk = A.alloc([D], BF16)
        bjunk = P.buf("junk")
        hTb = [A.alloc([KC, 512], BF16) for _ in range(2)]
        bhTa = [P.buf(f"hTa{i}") for i in range(2)]
        bhTd = [P.buf(f"hTd{i}") for i in range(2)]
        P.dma("sp", xb[0], xv[:, 0:4, :], writes=[bxb[0]])
        P.dma("sp", xb[1], xv[:, 4:8, :], writes=[bxb[1]])
        tiles = []
        for blk in range(NBLK):
            i = blk % 2
            for j in range(4):
                tiles.append((xb[i][:, j, :], bxb[i], A1, Sh1, (lambda kc, i=i, j=j: hTb[i][:, kc, j * 128:(j + 1) * 128]), bhTa[i], bhTd[i]))

        def after_tile(t):
            blk, j = divmod(t, 4)
            i = blk % 2
            if j == 3:
                P.dma("pool", hs_d[blk], hTb[i], reads=[bhTa[i], bhTd[i]], writes=[bhs[blk]])
                if blk + 2 < NBLK:
                    P.dma("sp", xb[i], xv[:, 4 * blk + 8:4 * blk + 12, :], writes=[bxb[i]])
            if j in (1, 3):
                cb = 8 + 2 * blk + (j // 2)
                ada_cols(cb)

        norm_pipeline(tiles, xn, bxn, junk, bjunk, after=after_tile)
        ada_finish(32, 96)
        P.op("dve", STT(A2, mod[:, 64:80], 1.0, cs("n2w"), ALU.add, ALU.mult), reads=[bK, bC], writes=[bK])
        P.barrier()
    A.off = base_off

    bms = P.buf("ms")

    if "B" in phases:
        wbuf = [A.alloc([KC, ML_W], BF16) for _ in range(2)]
        bwb = [P.buf(f"wb{i}") for i in range(2)]
        hbuf = [A.alloc([KC, 512], BF16) for _ in range(2)]
        bhb = [P.buf(f"hb{i}") for i in range(2)]
        mob = [A.alloc([2, 512], BF16) for _ in range(2)]
        bmoa = [P.buf(f"moa{i}") for i in range(2)]
        xc = [A.alloc([515], F32) for _ in range(2)]
        bxc = [P.buf(f"xc{i}") for i in range(2)]
        ycv = [A.alloc([512], F32) for _ in range(2)]
        bycv = [P.buf(f"ycv{i}") for i in range(2)]
        sgm = [A.alloc([512], F32) for _ in range(2)]
        bsgm = [P.buf(f"sgm{i}") for i in range(2)]
        qkT = [[A.alloc([512], BF16) for _ in range(2)] for _ in range(2)]
        bqkT = [[P.buf(f"qkT{p}_{i}") for i in range(2)] for p in range(2)]
        Gp = [A.alloc([96], F32) for _ in range(2)]
        bGp = [P.buf(f"G{p}") for p in range(2)]
        vaug = [A.alloc([257], BF16) for _ in range(2)]
        bva = [P.buf(f"vaug{i}") for i in range(2)]
        vw = [A.alloc([257], BF16) for _ in range(2)]
        bvw = [P.buf(f"vw{i}") for i in range(2)]
        ktm = [A.alloc([128], BF16) for _ in range(2)]
        bktm = [P.buf(f"ktm{i}") for i in range(2)]
        ATs = [A.alloc([128], BF16) for _ in range(2)]
        bATs = [P.buf(f"ATs{i}") for i in range(2)]
        C32 = A.alloc([257], F32)
        bC32 = P.buf("C32")
        Cb = [A.alloc([257], BF16) for _ in range(2)]
        bCb = [P.buf(f"Cb{i}") for i in range(2)]
        dds = A.alloc([64 * 8], F32)
        junk2 = A.alloc([256], BF16)
        bjunk2 = P.buf("junk2")
        hn = [A.alloc([256], F32) for _ in range(2)]
        bhn = [P.buf(f"hn{i}") for i in range(2)]
        sig = [A.alloc([256], F32) for _ in range(2)]
        bsig = [P.buf(f"sig{i}") for i in range(2)]
        hm = [A.alloc([256], BF16) for _ in range(2)]
        bhm = [P.buf(f"hm{i}") for i in range(2)]
        KT = [A.alloc([NTILE * 128], BF16) for _ in range(2)]
        bKT = [P.buf(f"KT{i}") for i in range(2)]
        Vh = A.alloc([2, NTILE, 128], BF16)
        bVh = P.buf("Vh")
        QT = [A.alloc([512], BF16) for _ in range(2)]
        bQT = [P.buf(f"QT{i}") for i in range(2)]
        sqb = [A.alloc([8, 64], F32) for _ in range(2)]
        bsqb = [P.buf(f"sqb{i}") for i in range(2)]
        rs = [A.alloc([8], F32) for _ in range(2)]
        brs = [P.buf(f"rs{i}") for i in range(2)]
        t1 = [A.alloc([8, 64], F32) for _ in range(2)]
        bt1 = [P.buf(f"t1{i}") for i in range(2)]
        qkb = [A.alloc([8, 64], BF16) for _ in range(2)]
        bqkb = [P.buf(f"qkb{i}") for i in range(2)]
        rt = [A.alloc([4, 8, 8], F32) for _ in range(2)]
        brt = [P.buf(f"rt{i}") for i in range(2)]
        PT = [A.alloc([2, 512], BF16) for _ in range(3)]
        bPT = [P.buf(f"PT{i}") for i in range(3)]
        Pacc = [A.alloc([512], F32) for _ in range(2)]
        bPacc = [P.buf(f"Pacc{i}") for i in range(2)]
        fz = [A.alloc([512], F32) for _ in range(5)]
        bfz = [P.buf(f"fz{i}") for i in range(5)]
        def interleave(g1, g2):
            gens = [g for g in (g1, g2) if g is not None]
            while gens:
                for g in list(gens):
                    try:
                        next(g)
                    except StopIteration:
                        gens.remove(g)

        P.op("dve", MSET(dds, 0.0), writes=[bK])
        for i in range(2):
            P.op("dve", MSET(vaug[i][:, 256:257], 1.0), writes=[bva[i]])

        wml = cs("nw").rearrange("p (h v) -> p h v", h=4)
        cw = cs("cw").rearrange("p (h q t) -> p h q t", h=4, q=2)
        cbv = cs("cb").rearrange("p (h q) -> p h q", h=4)
        gbv = cs("gb").rearrange("p (h g) -> p h g", h=4)
        wqk = cs("wqk").rearrange("p (g d) -> p g d", g=8)
        LNS = math.log(128.0 ** -0.5)

        npass = 8
        pass_cols = [(hh * ML_W, ML_W) for hh in range(4)] + [(4 * ML_W + hp * DA_W, DA_W) for hp in range(4)]

        def load_w(pi):
            o, n = pass_cols[pi]
            P.dma("pool", wbuf[pi % 2][:, :, 0:n], win_d[:, :, o:o + n], writes=[bwb[pi % 2]])

        hcount = [0]

        def load_h(blk):
            i = hcount[0] % 2
            hcount[0] += 1
            P.dma("sp", hbuf[i], hs_d[blk], reads=[bhs[blk]], writes=[bhb[i]])
            return i

        load_w(0)
        cur_tile_head = [0]
        mo_cnt = [0]
        KB = os.environ.get("KDBG_B", "")
        for pi in range(npass):
            if pi + 1 < npass:
                load_w(pi + 1)
            if (KB == "ml" and pi >= 4) or (KB == "da" and pi < 4) or (KB == "ml1" and pi >= 1) or (KB == "da1" and pi != 4):
                continue
            W = wbuf[pi % 2]
            bW = bwb[pi % 2]
            nxt = load_h(0)
            if pi < 4:
                hh = pi
                P.op("dve", MSET(C32, 0.0), writes=[bC32])
                hmap = {}

                def ml_prep(blk, hi):
                    H = hbuf[hi]
                    bH = bhb[hi]
                    cur = blk >= 4
                    pb = blk % 2
                    for qk in (1, 0):
                        if qk == 0 and blk < 3:
                            continue
                        for kc in range(KC):
                            P.op("pe", MM(ps[:, qk, :], W[:, kc, qk * 128:(qk + 1) * 128], H[:, kc, :], kc == 0, kc == KC - 1),
                                 reads=[bW, bH], writes=[bps[qk]])
                        yield
                        first = (blk == 0) if qk == 1 else (blk == 3)
                        if first:
                            P.op("dve", MSET(xc[qk][:, 0:3], 0.0), writes=[bxc[qk]])
                        elif blk == 4:
                            P.op("dve", TS(xc[qk][:, 0:3], xc[qk][:, 512:515], flag[:, 0:1], None, ALU.mult),
                                 reads=[bxc[qk], bC], writes=[bxc[qk]])
                        else:
                            P.op("dve", CP(xc[qk][:, 0:3], xc[qk][:, 512:515]), reads=[bxc[qk]], writes=[bxc[qk]])
                        P.op("act", lambda e, qk=qk: e.copy(out=xc[qk][:, 3:515], in_=ps[:, qk, :]), reads=[bps[qk]], writes=[bxc[qk]])
                        yield
                        if qk == 0 and blk < 4:
                            continue
                        P.op("dve", TS(ycv[qk], xc[qk][:, 0:512], cw[:, hh, qk, 0:1], cbv[:, hh, qk:qk + 1], ALU.mult, ALU.add),
                             reads=[bxc[qk], bC], writes=[bycv[qk]])
                        yield
                        for tp in range(1, 4):
                            P.op("dve", STT(ycv[qk], xc[qk][:, tp:tp + 512], cw[:, hh, qk, tp:tp + 1], ycv[qk], ALU.mult, ALU.add),
                                 reads=[bxc[qk], bC, bycv[qk]], writes=[bycv[qk]])
                            yield
                        P.op("act", ACTF(sgm[qk], ycv[qk], AF.Exp, scale=-1.0), reads=[bycv[qk]], writes=[bsgm[qk]])
                        P.op("act", ACTF(sgm[qk], sgm[qk], AF.Ln, bias=1.0), reads=[bsgm[qk]], writes=[bsgm[qk]])
                        yield
                        P.op("act", ACTF(sgm[qk], sgm[qk], AF.Exp, scale=-1.0), reads=[bsgm[qk]], writes=[bsgm[qk]])
                        P.op("dve", TT(qkT[pb][qk], ycv[qk], sgm[qk], ALU.mult), reads=[bycv[qk], bsgm[qk]], writes=[bqkT[pb][qk]])
                        yield
                    for j in range(4):
                        for kc in range(KC):
                            P.op("pe", MM(ps[:, 5, 384 + 2 * j:386 + 2 * j], H[:, kc, j * 128:(j + 1) * 128], W[:, kc, 768:770], kc == 0, kc == KC - 1),
                                 reads=[bW, bH], writes=[bps[5]])
                        yield
                    G = Gp[pb]
                    bG = bGp[pb]
                    gsb = G[:, 0:8].rearrange("p (j g) -> p j g", j=4)
                    th = G[:, 8:16].rearrange("p (j g) -> p j g", j=4)
                    e1, l1, li, nBe, tq, argw, argf = (G[:, 16 + 4 * i:20 + 4 * i] for i in range(7))
                    wv, fpv, eB = (G[:, 48 + 4 * i:52 + 4 * i] for i in range(3))
                    P.op("dve", TT(gsb, ps[:, 5, 384:392].rearrange("p (j g) -> p j g", j=4),
                                   gbv[:, hh:hh + 1, :].to_broadcast([128, 4, 2]), ALU.add), reads=[bps[5], bC], writes=[bG])
                    P.op("act", ACTF(th, gsb, AF.Exp, scale=2.0 / 15.0), reads=[bG], writes=[bG])
                    yield
                    P.op("act", ACTF(th, th, AF.Ln, bias=1.0), reads=[bG], writes=[bG])
                    P.op("act", ACTF(th, th, AF.Exp, scale=-1.0), reads=[bG], writes=[bG])
                    yield
                    P.op("dve", TS(th, th, -2.0, 1.0, ALU.mult, ALU.add), reads=[bG], writes=[bG])
                    P.op("act", ACTF(e1, th[:, :, 1], AF.Exp, scale=-15.0), reads=[bG], writes=[bG])
                    yield
                    P.op("act", ACTF(l1, e1, AF.Ln, bias=1.0), reads=[bG], writes=[bG])
                    P.op("dve", TS(li, th[:, :, 0], 15.0, None, ALU.mult), reads=[bG], writes=[bG])
                    yield
                    P.op("pe", MM(ps[:, 5, 400:404], tri, l1), reads=[bG, bC], writes=[bps[5]])
                    P.op("pe", MM(ps[:, 5, 416:420], onesf, l1), reads=[bG, bK], writes=[bps[5]])
                    yield
                    P.op("dve", CP(nBe, ps[:, 5, 416:420]), reads=[bps[5]], writes=[bG])
                    P.op("dve", TT(tq, li, nBe, ALU.subtract), reads=[bG], writes=[bG])
                    yield
                    P.op("dve", TT(argw, tq, ps[:, 5, 400:404], ALU.add), reads=[bG, bps[5]], writes=[bG])
                    P.op("dve", TT(argf, nBe, ps[:, 5, 400:404], ALU.subtract), reads=[bG, bps[5]], writes=[bG])
                    yield
                    P.op("act", ACTF(wv, argw, AF.Exp), reads=[bG], writes=[bG])
                    P.op("act", ACTF(fpv, argf, AF.Exp, bias=LNS), reads=[bG], writes=[bG])
                    P.op("act", ACTF(eB, nBe, AF.Exp, scale=-1.0), reads=[bG], writes=[bG])
                    yield

                def ml_tiles(blk, hi):
                    H = hbuf[hi]
                    bH = bhb[hi]
                    cur = blk >= 4
                    pb = blk % 2
                    G = Gp[pb]
                    bG = bGp[pb]
                    wv, fpv, eB = (G[:, 48 + 4 * i:52 + 4 * i] for i in range(3))
                    qT = qkT[pb][0]
                    kT = qkT[pb][1]
                    bqT = bqkT[pb][0]
                    bkT = bqkT[pb][1]
                    nv = 512 if cur else 256
                    mi = mo_cnt[0] % 2

                    def emit_vo(j):
                        vb_ = 2 + (j % 2)
                        for kc in range(KC):
                            P.op("pe", MM(ps[:, vb_, 0:nv], H[:, kc, j * 128:(j + 1) * 128], W[:, kc, 256:256 + nv], kc == 0, kc == KC - 1),
                                 reads=[bW, bH], writes=[bps[vb_]])

                    def tile(j):
                        t = blk * 4 + j
                        vb = 2 + (j % 2)
                        nb = 6 + (j % 2)
                        a = t % 2
                        P.op("dve", TS(vw[a][:, 0:256], ps[:, vb, 0:256], wv[:, j:j + 1], None, ALU.mult), reads=[bps[vb], bG], writes=[bvw[a]])
                        P.op("dve", CP(vw[a][:, 256:257], wv[:, j:j + 1]), reads=[bG], writes=[bvw[a]])
                        P.op("pe", TR(psbf(5)[:, 2 + a, :], kT[:, j * 128:(j + 1) * 128], identb), reads=[bkT, bK], writes=[bps[5]])
                        P.op("act", lambda e, a=a: e.copy(out=ktm[a], in_=psbf(5)[:, 2 + a, :]), reads=[bps[5]], writes=[bktm[a]])
                        yield
                        if cur:
                            u = cur_tile_head[0]
                            cur_tile_head[0] += 1
                            dd = dds[:, 8 * u:8 * u + 8]
                            bdd = P.buf(f"dd{u}")
                            P.op("act", lambda e, a=a, vb=vb: e.copy(out=vaug[a][:, 0:256], in_=ps[:, vb, 0:256]), reads=[bps[vb]], writes=[bva[a]])
                            P.op("act", ACTF(sig[a], ps[:, vb, 256:512], AF.Exp, scale=-1.0), reads=[bps[vb]], writes=[bsig[a]])
                            P.op("pe", MM(ps[:, 5, 0:128], kT[:, j * 128:(j + 1) * 128], qT[:, j * 128:(j + 1) * 128]),
                                 reads=[bqT, bkT], writes=[bps[5]])
                            P.op("dve", STT(ATs[a], ps[:, 5, 0:128], wv[:, j:j + 1], tri, ALU.mult, ALU.mult), reads=[bps[5], bG, bC], writes=[bATs[a]])
                            yield
                        if blk == 4 and j == 0:
                            P.op("dve", TS(C32, C32, flag[:, 0:1], None, ALU.mult), reads=[bC32, bC], writes=[bC32])
                        if cur:
                            P.op("dve", TS(Cb[a], C32, eB[:, j:j + 1], None, ALU.mult), reads=[bC32, bG], writes=[bCb[a]])
                            P.op("pe", MM(ps[:, nb, 0:257], ATs[a], vaug[a], True, False), reads=[bATs[a], bva[a]], writes=[bps[nb]])
                            P.op("pe", MM(ps[:, nb, 0:257], qT[:, j * 128:(j + 1) * 128], Cb[a], False, True), reads=[bqT, bCb[a]], writes=[bps[nb]])
                        P.op("pe", MM(ps[:, 4, 0:257], ktm[a], vw[a]), reads=[bktm[a], bvw[a]], writes=[bps[4]])
                        P.op("dve", STT(C32, C32, eB[:, j:j + 1], ps[:, 4, 0:257], ALU.mult, ALU.add), reads=[bC32, bG, bps[4]], writes=[bC32])
                        yield
                        if not cur:
                            return
                        P.op("dve", TT(dd[:, 0:1], ps[:, nb, 256:257], fpv[:, j:j + 1], ALU.mult), reads=[bps[nb], bG], writes=[bdd])
                        P.op("dve", TS(dd[:, 1:2], dd[:, 0:1], -1.0, 1.0, ALU.mult, ALU.max), reads=[bdd], writes=[bdd])
                        yield
                        P.op("dve", TT(dd[:, 1:2], dd[:, 1:2], dd[:, 0:1], ALU.max), reads=[bdd], writes=[bdd])
                        P.op("dve", RCP(dd[:, 1:2], dd[:, 1:2]), reads=[bdd], writes=[bdd])
                        yield
                        P.op("dve", TT(dd[:, 2:3], fpv[:, j:j + 1], dd[:, 1:2], ALU.mult), reads=[bdd, bG], writes=[bdd])
                        P.op("act", ACTF(junk2, ps[:, nb, 0:256], AF.Square, scale=dd[:, 2:3], accum=dd[:, 3:4]), reads=[bps[nb], bdd], writes=[bjunk2, bdd])
                        yield
                        P.op("act", ACTF(dd[:, 4:5], dd[:, 3:4], AF.Ln, scale=1.0 / 256.0, bias=EPS), reads=[bdd], writes=[bdd])
                        P.op("act", ACTF(dd[:, 4:5], dd[:, 4:5], AF.Exp, scale=-0.5), reads=[bdd], writes=[bdd])
                        yield
                        P.op("dve", TT(dd[:, 5:6], dd[:, 2:3], dd[:, 4:5], ALU.mult), reads=[bdd], writes=[bdd])
                        yield
                        P.op("dve", STT(hn[a], ps[:, nb, 0:256], dd[:, 5:6], wml[:, hh, :], ALU.mult, ALU.mult), reads=[bps[nb], bdd, bC], writes=[bhn[a]])
                        P.op("act", ACTF(sig[a], sig[a], AF.Ln, bias=1.0), reads=[bsig[a]], writes=[bsig[a]])
                        yield
                        P.op("act", ACTF(sig[a], sig[a], AF.Exp, scale=-1.0), reads=[bsig[a]], writes=[bsig[a]])
                        yield
                        P.op("dve", TT(hm[a], hn[a], sig[a], ALU.mult), reads=[bhn[a], bsig[a]], writes=[bhm[a]])
                        yield
                        for i2 in range(2):
                            P.op("pe", TR(psbf(5)[:, 4 + i2, :], hm[a][:, i2 * 128:(i2 + 1) * 128], identb), reads=[bhm[a], bK], writes=[bps[5]])
                        P.op("act", lambda e, mi=mi, j=j: e.copy(out=mob[mi][:, :, j * 128:(j + 1) * 128], in_=psbf(5)[:, 4:6, :]),
                             reads=[bps[5]], writes=[bmoa[mi]], relaxed=True)
                        yield

                    emit_vo(0)
                    yield
                    act_t = []
                    for j in range(4):
                        if j + 1 < 4:
                            emit_vo(j + 1)
                            yield
                        act_t.append(tile(j))
                        if len(act_t) == 2:
                            done0 = False
                            while not done0:
                                for g in list(act_t):
                                    try:
                                        next(g)
                                    except StopIteration:
                                        if g is act_t[0]:
                                            done0 = True
                                        act_t.remove(g)
                                yield
                        else:
                            for _ in range(3):
                                try:
                                    next(act_t[0])
                                except StopIteration:
                                    act_t.pop()
                                    break
                            yield
                    for g in act_t:
                        for _ in g:
                            yield
                    if cur:
                        cb4 = blk - 4
                        P.dma("sp", ms_d[2 * hh:2 * hh + 2, :, cb4 * 512:(cb4 + 1) * 512].rearrange("c p n -> p c n"), mob[mi],
                              reads=[bmoa[mi]], writes=[bms])
                        mo_cnt[0] += 1

                hmap[0] = nxt
                for _ in ml_prep(0, hmap[0]):
                    pass
                for blk in range(NBLK):
                    g2 = None
                    if blk + 1 < NBLK:
                        hmap[blk + 1] = load_h(blk + 1)
                        g2 = ml_prep(blk + 1, hmap[blk + 1])
                    interleave(ml_tiles(blk, hmap[blk]), g2)
            else:
                hp = pi - 4
                fin2q = []
                for blk in range(NBLK):
                    while fin2q:
                        fin2q.pop(0)()
                    hi = nxt
                    if blk + 1 < NBLK:
                        nxt = load_h(blk + 1)
                    H = hbuf[hi]
                    bH = bhb[hi]
                    cur = blk >= 4
                    c0 = 0 if cur else 128

                    def proj_mm(j):
                        b0 = 2 * (j % 2)
                        for hd in range(2):
                            for kc in range(KC):
                                P.op("pe", MM(ps[:, b0 + hd, c0:384], H[:, kc, j * 128:(j + 1) * 128], W[:, kc, hd * 384 + c0:hd * 384 + 384],
                                              kc == 0, kc == KC - 1), reads=[bW, bH], writes=[bps[b0 + hd]])

                    def tile_gen(j):
                        t = blk * 4 + j
                        b0 = 2 * (j % 2)
                        bb = [bps[b0], bps[b0 + 1]]
                        sl = t % 2
                        P.op("act", lambda e, t=t, b0=b0: e.copy(out=Vh[:, :, t, :], in_=ps[:, b0:b0 + 2, 256:384]), reads=bb, writes=[bVh], relaxed=True)
                        pq = ps[:, b0:b0 + 2, 0:256].rearrange("p h (g d) -> p h g d", g=4)
                        sq4 = sqb[sl].rearrange("p (h g) d -> p h g d", h=2)
                        t14 = t1[sl].rearrange("p (h g) d -> p h g d", h=2)
                        P.op("act", ACTF(sq4, pq, AF.Square), reads=bb, writes=[bsqb[sl]])
                        yield
                        P.op("dve", lambda e, sl=sl: e.tensor_reduce(out=rs[sl], in_=sqb[sl], axis=AX.X, op=ALU.add), reads=[bsqb[sl]], writes=[brs[sl]])
                        yield
                        P.op("act", ACTF(rs[sl], rs[sl], AF.Ln, scale=1.0 / 64.0, bias=EPS), reads=[brs[sl]], writes=[brs[sl]])
                        P.op("act", ACTF(rs[sl], rs[sl], AF.Exp, scale=-0.5), reads=[brs[sl]], writes=[brs[sl]])
                        yield
                        P.op("dve", TT(t14, pq, rs[sl].rearrange("p (h g) -> p h g", h=2)[:, :, :, None].to_broadcast([128, 2, 4, 64]), ALU.mult),
                             reads=bb + [brs[sl]], writes=[bt1[sl]])
                        yield
                        P.op("dve", TT(t1[sl], t1[sl], wqk, ALU.mult), reads=[bt1[sl], bC], writes=[bt1[sl]])
                        yield
                        P.op("act", lambda e, sl=sl: e.copy(out=qkb[sl], in_=t1[sl]), reads=[bt1[sl]], writes=[bqkb[sl]])
                        cosb = cosT[:, t:t + 1, :].to_broadcast([128, 8, 8])
                        sinb = sinT[:, t:t + 1, :].to_broadcast([128, 8, 8])
                        x1 = t1[sl][:, :, 0:8]
                        x2 = t1[sl][:, :, 8:16]
                        R_ = rt[sl]
                        P.op("dve", TT(R_[:, 0], x1, cosb, ALU.mult), reads=[bt1[sl], bK], writes=[brt[sl]])
                        P.op("dve", TT(R_[:, 1], x2, sinb, ALU.mult), reads=[bt1[sl], bK], writes=[brt[sl]], relaxed=True)
                        yield
                        P.op("dve", TT(R_[:, 2], x2, cosb, ALU.mult), reads=[bt1[sl], bK], writes=[brt[sl]], relaxed=True)
                        P.op("dve", TT(R_[:, 3], x1, sinb, ALU.mult), reads=[bt1[sl], bK], writes=[brt[sl]], relaxed=True)
                        yield
                        P.op("dve", TT(qkb[sl][:, :, 0:8], R_[:, 0], R_[:, 1], ALU.subtract), reads=[brt[sl]], writes=[bqkb[sl]])
                        P.op("dve", TT(qkb[sl][:, :, 8:16], R_[:, 2], R_[:, 3], ALU.add), reads=[brt[sl]], writes=[bqkb[sl]], relaxed=True)
                        yield
                        qk2 = qkb[sl].rearrange("p g d -> p (g d)")
                        for hd in range(2):
                            P.op("pe", TR(psbf(4)[:, 2 * hd, :], qk2[:, hd * 256 + 128:hd * 256 + 256], identb), reads=[bqkb[sl], bK], writes=[bps[4]])
                            if cur:
                                P.op("pe", TR(psbf(4)[:, 2 * hd + 1, :], qk2[:, hd * 256:hd * 256 + 128], identb), reads=[bqkb[sl], bK], writes=[bps[4]])
                        yield
                        for hd in range(2):
                            P.op("act", lambda e, hd=hd, t=t: e.copy(out=KT[hd][:, t * 128:(t + 1) * 128], in_=psbf(4)[:, 2 * hd, :]),
                                 reads=[bps[4]], writes=[bKT[hd]], relaxed=True)
                            if cur:
                                P.op("act", lambda e, hd=hd, j=j: e.copy(out=QT[hd][:, j * 128:(j + 1) * 128], in_=psbf(4)[:, 2 * hd + 1, :]),
                                     reads=[bps[4]], writes=[bQT[hd]], relaxed=True)
                        yield

                    proj_mm(0)
                    act_g = []
                    for j in range(4):
                        if j + 1 < 4:
                            proj_mm(j + 1)
                        act_g.append(tile_gen(j))
                        if len(act_g) == 2:
                            done0 = False
                            nstep = 0
                            while not done0:
                                for g in list(act_g):
                                    try:
                                        next(g)
                                    except StopIteration:
                                        if g is act_g[0]:
                                            done0 = True
                                        act_g.remove(g)
                                nstep += 1
                        elif j == 0:
                            for _ in range(4):
                                next(act_g[0])
                    for g in act_g:
                        for _ in g:
                            pass
                    if not cur:
                        continue
                    cb4 = blk - 4
                    ndone = 16 + 4 * cb4
                    for hd in range(2):
                        steps = [(c, 0, None) for c in range(ndone)] + [(ndone + jd, jd * 128, jd) for jd in (3, 2, 1, 0)]
                        nst = len(steps)

                        def emit_S(n):
                            c, q0, jd = steps[n]
                            sb0 = 4 + 2 * (n % 2)
                            for m in range(2):
                                P.op("pe", MM(ps[:, sb0 + m, q0:512], KT[hd][m * 64:(m + 1) * 64, c * 128:(c + 1) * 128], QT[hd][m * 64:(m + 1) * 64, q0:512]),
                                     reads=[bKT[hd], bQT[hd]], writes=[bps[sb0 + m]])

                        def emit_E(n):
                            c, q0, jd = steps[n]
                            sb0 = 4 + 2 * (n % 2)
                            pt = n % 3
                            P.op("act", ACTF(PT[pt][:, :, q0:512], ps[:, sb0:sb0 + 2, q0:512], AF.Exp, scale=0.125, bias=(pbias[:, 0:1] if c < 16 else None)),
                                 reads=[bps[sb0], bps[sb0 + 1], bC], writes=[bPT[pt]])
                            if jd is not None:
                                P.op("dve", MSET(PT[pt][64:128, :, q0:q0 + 64], 0.0), reads=[bPT[pt]], writes=[bPT[pt]])

                        def emit_PV(n):
                            c, q0, jd = steps[n]
                            pt = n % 3
                            for m in range(2):
                                P.op("pe", MM(ps[:, m, q0:512], Vh[:, hd, c, :], PT[pt][:, m, q0:512], c == 0, jd == 0),
                                     reads=[bVh, bPT[pt]], writes=[bps[m]])
                            if c == 0:
                                P.op("dve", CP(Pacc[0], PT[pt][:, 0, :]), reads=[bPT[pt]], writes=[bPacc[0]])
                            else:
                                P.op("dve", TT(Pacc[0][:, q0:512], Pacc[0][:, q0:512], PT[pt][:, 0, q0:512], ALU.add), reads=[bPT[pt], bPacc[0]], writes=[bPacc[0]])
                            P.op("pe", MM(ps[:, 3, q0:512], onesb, PT[pt][:, 1, q0:512], c == 0, jd == 0), reads=[bPT[pt], bK], writes=[bps[3]])

                        emit_S(0)
                        emit_S(1)
                        for n in range(nst):
                            emit_E(n)
                            if n + 2 < nst:
                                emit_S(n + 2)
                            emit_PV(n)
                            if fin2q and n >= 2:
                                fin2q.pop(0)()
                        while fin2q:
                            fin2q.pop(0)()
                        P.op("pe", MM(ps[:, 2, :], onesf, Pacc[0]), reads=[bPacc[0], bK], writes=[bps[2]])
                        P.op("dve", RCP(fz[0], ps[:, 2, :]), reads=[bps[2]], writes=[bfz[0]])
                        P.op("dve", TT(fz[1], ps[:, 0, :], fz[0], ALU.mult), reads=[bps[0], bfz[0]], writes=[bfz[1]])
                        P.op("dve", RCP(fz[2], ps[:, 3, :]), reads=[bps[3]], writes=[bfz[2]])
                        P.op("dve", TT(fz[3], ps[:, 1, :], fz[2], ALU.mult), reads=[bps[1], bfz[2]], writes=[bfz[3]])

                        def f2a():
                            P.op("dve", STT(fz[1], fz[3], neglam[:, 0:1], fz[1], ALU.mult, ALU.add), reads=[bfz[3], bfz[1], bK], writes=[bfz[1]])

                        def f2b():
                            P.op("act", ACTF(fz[4], fz[1], AF.Square), reads=[bfz[1]], writes=[bfz[4]])

                        def f2c():
                            P.op("pe", MM(ps[:, 2, :], onesf, fz[4]), reads=[bfz[4], bK], writes=[bps[2]])

                        def f2d():
                            P.op("act", ACTF(fz[0], ps[:, 2, :], AF.Ln, scale=1.0 / 128.0, bias=EPS), reads=[bps[2]], writes=[bfz[0]])

                        def f2e():
                            P.op("act", ACTF(fz[0], fz[0], AF.Exp, scale=-0.5), reads=[bfz[0]], writes=[bfz[0]])

                        def f2f(hd=hd, cb4=cb4):
                            mi = mo_cnt[0] % 2
                            mo_cnt[0] += 1
                            P.op("dve", STT(mob[mi][:, 0, :], fz[1], subw_s[:, 0:1], fz[0], ALU.mult, ALU.mult), reads=[bfz[1], bfz[0], bK], writes=[bmoa[mi]])
                            ch = 8 + 2 * hp + hd
                            P.dma("sp", ms_d[ch, :, cb4 * 512:(cb4 + 1) * 512], mob[mi][:, 0, :], reads=[bmoa[mi]], writes=[bms])

                        fin2q.extend([f2a, f2b, f2c, f2d, f2e, f2f])
                while fin2q:
                    fin2q.pop(0)()
        P.barrier()
    A.off = base_off

    bout = P.buf("out")
    if "C" in phases:
        h2T = A.alloc([KC, 1024], BF16)
        bh2a, bh2d = P.buf("h2a"), P.buf("h2d")
        wgu = [A.alloc([KC, 256], BF16) for _ in range(2)]
        bwgu = [P.buf(f"wgu{i}") for i in range(2)]
        wdb = [A.alloc([NFF, 128], BF16) for _ in range(2)]
        bwd = [P.buf(f"wd{i}") for i in range(2)]
        tmpf = [A.alloc([2, 512], F32) for _ in range(2)]
        btmp = [P.buf(f"tmp{i}") for i in range(2)]
        un0 = A.off
        mxb = [A.alloc([KC, 512], BF16) for _ in range(2)]
        bmx = [P.buf(f"mx{i}") for i in range(2)]
        xsb = A.alloc([8, D], F32)
        bxsb = P.buf("xsb")
        xn = [A.alloc([D], BF16) for _ in range(2)]
        bxn = [P.buf(f"xn{i}") for i in range(2)]
        junk = A.alloc([D], BF16)
        bjunk = P.buf("junk")
        un1 = A.off
        A.off = un0
        actT = A.alloc([NFF, 1024], BF16)
        bact = P.buf("actT")
        xs = [A.alloc([8, 128], F32) for _ in range(2)]
        bxs = [P.buf(f"xs{i}") for i in range(2)]
        sg = [A.alloc([512], F32) for _ in range(2)]
        bsg = [P.buf(f"sg{i}") for i in range(2)]
        A.off = max(A.off, un1)
        wob = [wgu[i][:, :, 0:128] for i in range(2)]
        out_v = out_d.rearrange("(t p) d -> p t d", p=128)
        tcount = [0]
        wcount = [0]
        for TB in range(2):
            for sbk in range(2):
                g512 = TB * 2 + sbk
                P.dma("sp", mxb[sbk], ms_d[:, :, g512 * 512:(g512 + 1) * 512].rearrange("c p n -> p c n"), reads=[bms], writes=[bmx[sbk]])
                P.dma("sp", xsb[:, 4 * sbk:4 * sbk + 4, :], xv[:, 16 + 4 * g512:16 + 4 * g512 + 4, :], writes=[bxsb])
            pend = None
            for m in range(16):
                wi = wcount[0] % 2
                wcount[0] += 1
                P.dma("pool", wob[wi], wo_d[m], writes=[bwgu[wi]])
                pb = 2 * (m % 2)
                for sbk in range(2):
                    for kc in range(KC):
                        P.op("pe", MM(ps[:, pb + sbk, :], wob[wi][:, kc, :], mxb[sbk][:, kc, :], kc == 0, kc == KC - 1),
                             reads=[bwgu[wi], bmx[sbk]], writes=[bps[pb + sbk]])
                ti = tcount[0] % 2
                tcount[0] += 1
                P.op("act", ACTF(tmpf[ti], ps[:, pb:pb + 2, :], AF.Copy, scale=gate1[:, m:m + 1]), reads=[bps[pb], bps[pb + 1], bK], writes=[btmp[ti]])
                if pend is not None:
                    pend()

                def pend(m=m, ti=ti):
                    tb0 = 4 + 2 * ti
                    for j in range(8):
                        P.op("pe", TR(ps[:, tb0 + j // 4, (j % 4) * 128:(j % 4 + 1) * 128], tmpf[ti][:, j // 4, (j % 4) * 128:(j % 4 + 1) * 128], identf),
                             reads=[btmp[ti], bC], writes=[bps[tb0 + j // 4]])
                    xslice = xsb[:, :, m * 128:(m + 1) * 128]
                    P.op("dve", TT(xslice, ps[:, tb0:tb0 + 2, :].rearrange("p b (j f) -> p (b j) f", j=4), xslice, ALU.add),
                         reads=[bps[tb0], bps[tb0 + 1], bxsb], writes=[bxsb])
            pend()
            ntiles = []
            for j in range(8):
                col = j * 128
                ntiles.append((xsb[:, j, :], bxsb, A2, Sh2, (lambda kc, col=col: h2T[:, kc, col:col + 128]), bh2a, bh2d))
            norm_pipeline(ntiles, xn, bxn, junk, bjunk)
            P.dma("sp", out_v[:, 8 * TB:8 * TB + 8, :], xsb, reads=[bxsb], writes=[bout])
            P.barrier()
            for jf in range(NFF):
                wi = wcount[0] % 2
                wcount[0] += 1
                P.dma("pool", wgu[wi], wgu_d[jf], writes=[bwgu[wi]])
                for sb in range(2):
                    for gu in range(2):
                        ob = 2 * (sb % 2) + gu + (4 if jf % 2 else 0)
                        for kc in range(KC):
                            P.op("pe", MM(ps[:, ob, :], wgu[wi][:, kc, gu * 128:(gu + 1) * 128], h2T[:, kc, sb * 512:(sb + 1) * 512], kc == 0, kc == KC - 1),
                                 reads=[bwgu[wi], bh2a, bh2d], writes=[bps[ob]])
                    ob = 2 * (sb % 2) + (4 if jf % 2 else 0)
                    si = tcount[0] % 2
                    tcount[0] += 1
                    P.op("act", ACTF(sg[si], ps[:, ob, :], AF.Silu), reads=[bps[ob]], writes=[bsg[si]])
                    P.op("dve", TT(actT[:, jf, sb * 512:(sb + 1) * 512], sg[si], ps[:, ob + 1, :], ALU.mult), reads=[bsg[si], bps[ob + 1]], writes=[bact], relaxed=True)
            pend = None
            for m in range(16):
                wi = m % 2
                P.dma("pool", wdb[wi], wd_d[m], writes=[bwd[wi]])
                xi = m % 2
                P.dma("sp", xs[xi], out_v[:, 8 * TB:8 * TB + 8, m * 128:(m + 1) * 128], reads=[bout], writes=[bxs[xi]], sem_buf=bxs[xi])
                for sb in range(2):
                    ob = sb
                    for fc in range(NFF):
                        P.op("pe", MM(ps[:, ob, :], wdb[wi][:, fc, :], actT[:, fc, sb * 512:(sb + 1) * 512], fc == 0, fc == NFF - 1),
                             reads=[bwd[wi], bact], writes=[bps[ob]])
                    ti = tcount[0] % 2
                    tcount[0] += 1
                    P.op("act", ACTF(tmpf[ti][:, 0, :], ps[:, ob, :], AF.Copy, scale=gate2[:, m:m + 1]), reads=[bps[ob], bK], writes=[btmp[ti]])
                    if pend is not None:
                        pend()

                    def pend(m=m, sb=sb, ti=ti, xi=xi):
                        tb = 2 + ti
                        for j in range(4):
                            P.op("pe", TR(ps[:, tb, j * 128:(j + 1) * 128], tmpf[ti][:, 0, j * 128:(j + 1) * 128], identf), reads=[btmp[ti], bC], writes=[bps[tb]])
                        xslice = xs[xi][:, sb * 4:(sb + 1) * 4, :]
                        P.op("dve", TT(xslice, ps[:, tb, :].rearrange("p (j f) -> p j f", j=4), xslice, ALU.add), reads=[bps[tb], bxs[xi]], writes=[bxs[xi]])
                        if sb == 1:
                            bo2 = P.buf(f"o2_{TB}_{m}")
                            P.dma("sp", out_v[:, 8 * TB:8 * TB + 8, m * 128:(m + 1) * 128], xs[xi], reads=[bxs[xi]], writes=[bout, bo2], sem_buf=bo2)
            pend()
            P.barrier()
    P.barrier()
    P.emit()
    return nc, P


def _prep_shared(inp):
    f = np.float32
    w_in = np.asarray(inp["w_in"], f)[0]
    cols = []
    for hh in range(4):
        cols += list(range(hh * 128, hh * 128 + 128)) + list(range(512 + hh * 128, 512 + hh * 128 + 128))
        cols += list(range(1024 + hh * 256, 1024 + hh * 256 + 256)) + list(range(2048 + hh * 256, 2048 + hh * 256 + 256))
        cols += [3072 + hh, 3076 + hh]
    for hp in range(4):
        for hd in range(2):
            h = 2 * hp + hd
            cols += list(range(3080 + h * 128, 3080 + h * 128 + 128)) + list(range(4104 + h * 128, 4104 + h * 128 + 128))
            cols += list(range(5128 + h * 128, 5128 + h * 128 + 128))
    cols = np.asarray(cols)
    sh = {}
    sh["w_in_l"] = np.ascontiguousarray(w_in.reshape(KC, 128, N_IN).transpose(1, 0, 2)[:, :, cols])
    sh["w_ada_l"] = np.ascontiguousarray(np.asarray(inp["w_ada"], f)[0].reshape(KC, 128, 6 * D))
    sh["wo_l"] = np.ascontiguousarray(np.asarray(inp["w_out"], f)[0].reshape(KC, 128, 16, 128).transpose(2, 1, 0, 3))
    wgu = np.asarray(inp["w_gate_up"], f)[0].reshape(KC, 128, 2, NFF, 128)
    sh["wgu_l"] = np.ascontiguousarray(wgu.transpose(3, 1, 0, 2, 4).reshape(NFF, 128, KC, 256))
    sh["wd_l"] = np.ascontiguousarray(np.asarray(inp["w_down"], f)[0].reshape(NFF, 128, 16, 128).transpose(2, 1, 0, 3))
    return sh


def _prep_cst(inp, b, half):
    f = np.float32
    c = np.zeros((128, NCST), f)

    def put(name, arr):
        o, n = CO[name]
        c[:, o:o + n] = np.asarray(arr, f).reshape(128, n)

    rep = lambda v: np.broadcast_to(np.asarray(v, f).reshape(1, -1), (128, np.asarray(v).size))
    put("c_l", np.asarray(inp["c"], f)[b].reshape(KC, 128).T)
    put("b_l", np.asarray(inp["b_ada"], f)[0].reshape(96, 128).T)
    put("n1w", np.asarray(inp["norm1_w"], f)[0].reshape(KC, 128).T)
    put("n2w", np.asarray(inp["norm2_w"], f)[0].reshape(KC, 128).T)
    cw = np.asarray(inp["mlstm_conv_w"], f)[0].reshape(4, 2, 4, 128)
    put("cw", cw.transpose(3, 2, 1, 0).reshape(128, 32))
    cb = np.asarray(inp["mlstm_conv_b"], f)[0].reshape(2, 4, 128)
    put("cb", cb.transpose(2, 1, 0).reshape(128, 8))
    gb = np.asarray(inp["mlstm_gate_b"], f)[0].reshape(2, 4)
    put("gb", rep(gb.T.reshape(-1)))
    put("nw", rep(np.asarray(inp["mlstm_norm_w"], f)[0].reshape(-1)))
    qw = np.asarray(inp["q_norm_w"], f)[0]
    kw = np.asarray(inp["k_norm_w"], f)[0]
    put("wqk", rep(np.concatenate([qw, qw, kw, kw, qw, qw, kw, kw])))
    put("lamv", rep(np.concatenate([np.asarray(inp[k], f)[0] for k in ("lambda_q1", "lambda_k1", "lambda_q2", "lambda_k2")])))
    put("subw", np.asarray(inp["subln_w"], f)[0].reshape(128, 1))
    put("flag", np.full((128, 1), float(half), f))
    put("pbias", np.full((128, 1), 0.0 if half else -30000.0, f))
    invf = (np.float32(500000.0) ** (-np.arange(0, 16, 2, dtype=np.float32) / np.float32(16))).astype(f)
    put("invf", rep(invf))
    put("ident", np.eye(128, dtype=f))
    put("tri", np.triu(np.ones((128, 128), f)))
    return c


def _in_maps(inp):
    sh = _prep_shared(inp)
    x = np.asarray(inp["x"], np.float32)
    pos = np.asarray(inp["positions"], np.int32)
    maps = []
    for core in range(8):
        b, half = core // 2, core % 2
        m = dict(sh)
        m["x"] = np.ascontiguousarray(np.concatenate([x[b, 0:TOK], x[b, half * TOK:(half + 1) * TOK]], axis=0))
        pp = np.concatenate([pos[b, 0:TOK], pos[b, half * TOK:(half + 1) * TOK]])
        m["pos"] = np.ascontiguousarray(pp.reshape(NTILE, 128).T)
        m["cst"] = _prep_cst(inp, b, half)
        maps.append(m)
    return maps


_NC = None


def kernel(**inputs):
    global _NC
    if _NC is None:
        _NC = build()[0]
    maps = _in_maps(inputs)
    res = run_bass_kernel_spmd(_NC, maps, core_ids=list(range(8)))
    out = np.empty((4, 2 * TOK, D), np.float32)
    for core in range(8):
        b, half = core // 2, core % 2
        out[b, half * TOK:(half + 1) * TOK] = res.results[core]["out"]
    return out
```

```python
import math
import os
import numpy as np
import concourse.bass as bass
import concourse.mybir as mybir
from concourse.bass_utils import run_bass_kernel_spmd

F32 = mybir.dt.float32
BF16 = mybir.dt.bfloat16
I32 = mybir.dt.int32
AF = mybir.ActivationFunctionType
ALU = mybir.AluOpType
AX = mybir.AxisListType

SEM_EPOCH = 16000
import os as _os
SAME_ENGINE_SYNC = _os.environ.get("K_SES", "1") == "1"


class Buf:
    __slots__ = ("name", "w", "rd", "dsem", "dcnt", "excl")

    def __init__(self, name, excl=False):
        self.name = name
        self.excl = excl
        self.w = None
        self.rd = {}
        self.dsem = None
        self.dcnt = 0


class Op:
    __slots__ = ("eng", "fn", "waits", "need_sig", "sigidx", "dma_sem")

    def __init__(self, eng, fn):
        self.eng = eng
        self.fn = fn
        self.waits = []
        self.need_sig = False
        self.sigidx = None
        self.dma_sem = None


class Prog:
    ENGS = ("pe", "act", "dve", "pool", "sp")

    def __init__(self, nc):
        self.nc = nc
        self.ops = {e: [] for e in self.ENGS}
        self.nsem = 0
        self.bufs = []

    def buf(self, name, excl=False):
        b = Buf(name, excl)
        self.bufs.append(b)
        return b

    def _newsem(self, name):
        self.nsem += 1
        return self.nc.alloc_semaphore(f"s{self.nsem}_{name}")

    def _collect(self, op, reads, writes, relaxed):
        toks = []
        for b in reads:
            if b.w is not None:
                toks.append((b.w, False))
            if b.excl:
                for k, t in b.rd.items():
                    if k != op.eng:
                        toks.append((t, False))
        for b in writes:
            if b.w is not None:
                toks.append((b.w, True))
            for t in b.rd.values():
                toks.append((t, False))
        for t, waw in toks:
            if t[0] == "op":
                p = t[1]
                if p is op:
                    continue
                if p.eng == op.eng and (p.eng == "pe" or not SAME_ENGINE_SYNC or (waw and relaxed)):
                    continue
                p.need_sig = True
            op.waits.append(t)

    def op(self, eng, fn, reads=(), writes=(), relaxed=False):
        o = Op(eng, fn)
        self._collect(o, reads, writes, relaxed)
        tok = ("op", o)
        for b in reads:
            b.rd[eng] = tok
        for b in writes:
            b.w = tok
            b.rd = {}
        self.ops[eng].append(o)
        return o

    def dma(self, eng, out_ap, in_ap, reads=(), writes=(), sem_buf=None):
        sb = sem_buf or (writes[0] if writes else reads[0])
        if sb.dsem is None:
            sb.dsem = self._newsem(sb.name)
        sb.dcnt += 16
        sem, cnt = sb.dsem, sb.dcnt

        def fn(e, out_ap=out_ap, in_ap=in_ap):
            return e.dma_start(out=out_ap, in_=in_ap)

        o = Op(eng, fn)
        o.dma_sem = sem
        self._collect(o, reads, writes, False)
        tok = ("sem", sem, cnt)
        for b in reads:
            b.rd[("d", id(sem))] = tok
        for b in writes:
            b.w = tok
            b.rd = {}
        self.ops[eng].append(o)
        return o

    def barrier(self):
        toks = []
        for b in self.bufs:
            if b.w is not None:
                toks.append(b.w)
            toks.extend(b.rd.values())
        for e in self.ENGS:
            for o in reversed(self.ops[e]):
                if o.fn is not None and o.dma_sem is None:
                    toks.append(("op", o))
                    break
        for e in self.ENGS:
            o = Op(e, None)
            for t in toks:
                if t[0] == "op":
                    if t[1].eng == e:
                        continue
                    t[1].need_sig = True
                o.waits.append(t)
            self.ops[e].append(o)
        for b in self.bufs:
            b.w = None
            b.rd = {}

    def final_wait(self, eng, bufs):
        o = Op(eng, None)
        for b in bufs:
            if b.w is not None:
                o.waits.append(b.w)
                if b.w[0] == "op":
                    b.w[1].need_sig = True
        self.ops[eng].append(o)

    def emit(self):
        nc = self.nc
        eng_sems = {}
        for e in self.ENGS:
            n = 0
            for o in self.ops[e]:
                if o.need_sig and o.dma_sem is None and o.fn is not None:
                    o.sigidx = n
                    n += 1
            eng_sems[e] = [self._newsem(f"{e}{i}") for i in range((n + SEM_EPOCH - 1) // SEM_EPOCH)]
        self.stats = {e: len(self.ops[e]) for e in self.ENGS}
        self.stats["nsem"] = self.nsem

        def resolve(t):
            if t[0] == "sem":
                return t[1], t[2]
            p = t[1]
            assert p.sigidx is not None, "waiting on op without signal"
            return eng_sems[p.eng][p.sigidx // SEM_EPOCH], p.sigidx % SEM_EPOCH + 1

        def run(e, h):
            waited = {}
            for o in self.ops[e]:
                need = {}
                for t in o.waits:
                    s, v = resolve(t)
                    k = id(s)
                    if waited.get(k, 0) >= v:
                        continue
                    if k not in need or need[k][1] < v:
                        need[k] = (s, v)
                for k, (s, v) in need.items():
                    h.wait_ge(s, v)
                    waited[k] = v
                if o.fn is None:
                    continue
                ins = o.fn(h)
                if o.dma_sem is not None:
                    ins.then_inc(o.dma_sem, 16)
                elif o.need_sig:
                    ins.then_inc(eng_sems[e][o.sigidx // SEM_EPOCH], 1)

        with nc.Block() as block:
            @block.tensor
            def _(h):
                run("pe", h)

            @block.scalar
            def _(h):
                run("act", h)

            @block.vector
            def _(h):
                run("dve", h)

            @block.gpsimd
            def _(h):
                run("pool", h)

            @block.sync
            def _(h):
                run("sp", h)


def MM(out, lhsT, rhs, start=True, stop=True):
    return lambda e: e.matmul(out, lhsT=lhsT, rhs=rhs, start=start, stop=stop)


def TR(out, in_, ident):
    return lambda e: e.transpose(out=out, in_=in_, identity=ident)


def ACTF(out, in_, func, scale=None, bias=None, accum=None):
    kw = {}
    if scale is not None:
        kw["scale"] = scale
    if bias is not None:
        kw["bias"] = bias
    if accum is not None:
        kw["accum_out"] = accum
    return lambda e: e.activation(out=out, in_=in_, func=func, **kw)


def TS(out, in0, s1, s2=None, op0=ALU.mult, op1=None):
    if op1 is None:
        return lambda e: e.tensor_scalar(out=out, in0=in0, scalar1=s1, scalar2=None, op0=op0)
    return lambda e: e.tensor_scalar(out=out, in0=in0, scalar1=s1, scalar2=s2, op0=op0, op1=op1)


def TT(out, in0, in1, op):
    return lambda e: e.tensor_tensor(out=out, in0=in0, in1=in1, op=op)


def STT(out, in0, scalar, in1, op0, op1):
    return lambda e: e.scalar_tensor_tensor(out=out, in0=in0, scalar=scalar, in1=in1, op0=op0, op1=op1)


def CP(out, in_):
    return lambda e: e.tensor_copy(out=out, in_=in_)


def MSET(ap, v):
    return lambda e: e.memset(ap, v)


def RCP(out, in_):
    return lambda e: e.reciprocal(out=out, in_=in_)


D = 2048
KC = 16
TOK = 2048
NTILE = 32
NBLK = 8
DFF = 5632
NFF = 44
N_IN = 6152
EPS = 1e-6
LAMBDA_INIT = 0.8 - 0.6 * math.exp(-0.3 * 0)
ML_W = 770
DA_W = 768
ARENA_BYTES = 211968

_CST = [("c_l", 16), ("b_l", 96), ("n1w", 16), ("n2w", 16), ("cw", 32), ("cb", 8), ("gb", 8), ("nw", 1024),
        ("wqk", 512), ("lamv", 256), ("subw", 1), ("flag", 1), ("pbias", 1), ("invf", 8), ("ident", 128),
        ("tri", 128)]
CO = {}
_o = 0
for _n, _s in _CST:
    CO[_n] = (_o, _s)
    _o += _s
NCST = _o


class Arena:
    def __init__(self, nc):
        self.t = nc.alloc_sbuf_tensor("arena", [128, ARENA_BYTES // 4], F32)
        self.off = 0

    def alloc(self, shape, dtype, parts=128, p0=0):
        n = int(np.prod(shape))
        esz = 2 if dtype == BF16 else 4
        nb = (n * esz + 31) // 32 * 32
        o = self.off
        self.off += nb
        assert self.off <= ARENA_BYTES, f"SBUF arena overflow {self.off}"
        v = self.t[p0:p0 + parts, o // 4:(o + nb) // 4]
        if dtype != F32:
            v = v.bitcast(dtype)
        v = v[:, 0:n]
        if len(shape) > 1:
            names = [f"a{i}" for i in range(len(shape))]
            kw = {nm: int(s) for nm, s in zip(names, shape)}
            v = v.rearrange("p (" + " ".join(names) + ") -> p " + " ".join(names), **kw)
        return v


def build(dbg=False, phases="0ABC"):
    nc = bass.Bass("TRN2", target_bir_lowering=False)
    P = Prog(nc)
    A = Arena(nc)

    def din(name, shape, dt=F32):
        return nc.dram_tensor(name, list(shape), dt, kind="ExternalInput").ap()

    def dscr(name, shape, dt):
        if dbg:
            return nc.dram_tensor(name, list(shape), dt, kind="ExternalOutput").ap()
        return nc.dram_tensor(name, list(shape), dt).ap()

    x_d = din("x", [2 * TOK, D])
    pos_d = din("pos", [128, NTILE], I32)
    cst_d = din("cst", [128, NCST])
    wada_d = din("w_ada_l", [24, 128, KC, 512])
    win_d = din("w_in_l", [128, KC, N_IN])
    wo_d = din("wo_l", [16, 128, KC, 128])
    wgu_d = din("wgu_l", [NFF, 128, KC, 256])
    wd_d = din("wd_l", [16, 128, NFF, 128])
    out_d = nc.dram_tensor("out", [TOK, D], F32, kind="ExternalOutput").ap()
    hs_d = dscr("hs", [NBLK, 128, KC, 512], BF16)
    ms_d = dscr("ms", [16, 128, TOK], BF16)
    modd = dscr("modd", [1, 6 * D], F32)

    ps = nc.alloc_psum_tensor("ps", [128, 8, 512], F32)
    bps = [P.buf(f"ps{i}", excl=True) for i in range(8)]

    def psbf(bank):
        return ps[:, bank, :].bitcast(BF16).rearrange("p (a b) -> p a b", a=8)

    CST = A.alloc([NCST], F32)
    bC = P.buf("cst")
    bK = P.buf("derived")
    P.dma("sp", CST, cst_d, writes=[bC])

    def cs(name):
        o, n = CO[name]
        return CST[:, o:o + n]

    identf = cs("ident")
    tri = cs("tri")
    flag = cs("flag")
    pbias = cs("pbias")
    posi = A.alloc([NTILE], I32)
    P.dma("sp", posi, pos_d, writes=[bK])
    identb = A.alloc([128], BF16)
    onesf = A.alloc([128], F32)
    onesb = A.alloc([128], BF16)
    mod = A.alloc([96], F32)
    A1 = A.alloc([16], F32)
    A2 = A.alloc([16], F32)
    Sh1 = mod[:, 0:16]
    Sh2 = mod[:, 48:64]
    gate1 = mod[:, 32:48]
    gate2 = mod[:, 80:96]
    cosT = A.alloc([NTILE, 8], F32)
    sinT = A.alloc([NTILE, 8], F32)
    stats = A.alloc([128], F32)
    neglam = A.alloc([1], F32)
    subw_s = A.alloc([1], F32)
    sc = A.alloc([16], F32)
    P.op("dve", CP(identb, identf), reads=[bC], writes=[bK])
    P.op("dve", MSET(onesf, 1.0), writes=[bK])
    P.op("dve", MSET(onesb, 1.0), writes=[bK])
    P.op("dve", MSET(stats, 0.0), writes=[bK])
    rp = A.off
    posf = A.alloc([NTILE], F32)
    ang = A.alloc([NTILE, 8], F32)
    ang2 = A.alloc([NTILE, 8], F32)
    ki = A.alloc([NTILE, 8], I32)
    kf = A.alloc([NTILE, 8], F32)
    P.op("dve", CP(posf, posi), reads=[bK], writes=[bK])
    P.op("dve", TT(ang, posf[:, :, None].to_broadcast([128, NTILE, 8]),
                   cs("invf")[:, None, :].to_broadcast([128, NTILE, 8]), ALU.mult), reads=[bK, bC], writes=[bK])
    for (dst, shift) in ((sinT, 0.0), (cosT, math.pi / 2)):
        P.op("dve", TS(ang2, ang, shift, None, ALU.add), reads=[bK], writes=[bK])
        P.op("dve", TS(ki, ang2, 1.0 / (2 * math.pi), None, ALU.mult), reads=[bK], writes=[bK])
        P.op("dve", CP(kf, ki), reads=[bK], writes=[bK])
        P.op("dve", STT(ang2, kf, -2 * math.pi, ang2, ALU.mult, ALU.add), reads=[bK], writes=[bK])
        P.op("dve", TS(ang2, ang2, -3.1415925, 3.1415925, ALU.max, ALU.min), reads=[bK], writes=[bK])
        P.op("act", ACTF(dst, ang2, AF.Sin), reads=[bK], writes=[bK])
    lamv = cs("lamv").rearrange("p (a b) -> p a b", a=4)
    lt = A.alloc([2, 64], F32)
    ls = A.alloc([2], F32)
    P.op("dve", TT(lt[:, 0, :], lamv[:, 0, :], lamv[:, 1, :], ALU.mult), reads=[bC], writes=[bK])
    P.op("dve", TT(lt[:, 1, :], lamv[:, 2, :], lamv[:, 3, :], ALU.mult), reads=[bC, bK], writes=[bK])
    P.op("dve", lambda e: e.tensor_reduce(out=ls, in_=lt, axis=AX.X, op=ALU.add), reads=[bK], writes=[bK])
    P.op("act", ACTF(ls, ls, AF.Exp), reads=[bK], writes=[bK])
    P.op("dve", STT(neglam, ls[:, 1:2], -LAMBDA_INIT, ls[:, 0:1], ALU.add, ALU.subtract), reads=[bK], writes=[bK])
    P.op("dve", TS(subw_s, cs("subw"), 1.0 - LAMBDA_INIT, None, ALU.mult), reads=[bC], writes=[bK])
    P.op("act", ACTF(sc, cs("c_l"), AF.Silu), reads=[bC], writes=[bK])
    base_off = A.off

    norm_uid = [0]

    def norm_s1(xsrc, bx, xn, bxn, junk, bjunk):
        uid = norm_uid[0]
        norm_uid[0] += 1
        st = stats[:, 2 * uid:2 * uid + 2]
        bst = P.buf(f"st{uid}")
        k = uid % 2
        P.op("act", ACTF(junk, xsrc, AF.Square, accum=st[:, 0:1]), reads=[bx, bK], writes=[bjunk, bst], relaxed=True)
        P.op("act", ACTF(st[:, 1:2], st[:, 0:1], AF.Ln, scale=1.0 / D, bias=EPS), reads=[bst], writes=[bst])
        P.op("act", ACTF(st[:, 1:2], st[:, 1:2], AF.Exp, scale=-0.5), reads=[bst], writes=[bst])
        P.op("dve", TS(xn[k], xsrc, st[:, 1:2], None, ALU.mult), reads=[bx, bst], writes=[bxn[k]])
        return k

    def norm_s2(k, Av, Shv, dst_of_kc, bdst_act, bdst_dve, xn, bxn):
        for hf in range(2):
            bank = 4 + 2 * k + hf
            pv = psbf(bank)
            for q in range(8):
                kc = hf * 8 + q
                P.op("pe", TR(pv[:, q, :], xn[k][:, kc * 128:(kc + 1) * 128], identb), reads=[bxn[k], bK], writes=[bps[bank]])
            for q in range(8):
                kc = hf * 8 + q
                if hf == 0 and q < 6:
                    P.op("act", ACTF(dst_of_kc(kc), pv[:, q, :], AF.Identity, scale=Av[:, kc:kc + 1], bias=Shv[:, kc:kc + 1]),
                         reads=[bps[bank], bK], writes=[bdst_act], relaxed=True)
                else:
                    P.op("dve", TS(dst_of_kc(kc), pv[:, q, :], Av[:, kc:kc + 1], Shv[:, kc:kc + 1], ALU.mult, ALU.add),
                         reads=[bps[bank], bK], writes=[bdst_dve], relaxed=True)

    def norm_pipeline(tiles, xn, bxn, junk, bjunk, after=None):
        ks = {}
        ks[0] = norm_s1(tiles[0][0], tiles[0][1], xn, bxn, junk, bjunk)
        for t, tl in enumerate(tiles):
            if t + 1 < len(tiles):
                ks[t + 1] = norm_s1(tiles[t + 1][0], tiles[t + 1][1], xn, bxn, junk, bjunk)
            norm_s2(ks[t], tl[2], tl[3], tl[4], tl[5], tl[6], xn, bxn)
            if after is not None:
                after(t)

    xv = x_d.rearrange("(t p) d -> p t d", p=128)
    bhs1 = P.buf("hs")
    bhs = [bhs1] * NBLK

    if "A" in phases:
        mrow = [A.alloc([512], F32, parts=1) for _ in range(2)]
        bmr = [P.buf(f"mrow{i}") for i in range(2)]
        wab = [A.alloc([KC, 512], F32) for _ in range(2)]
        bwa = [P.buf(f"wab{i}") for i in range(2)]
        bmd = P.buf("modd")
        modT = A.alloc([128], F32, parts=96)
        bmT = P.buf("modT")

        def ada_cols(cb):
            i = cb % 2
            P.dma("sp", wab[i], wada_d[cb], writes=[bwa[i]])
            for kc in range(KC):
                P.op("pe", MM(ps[0:1, i, :], sc[:, kc:kc + 1], wab[i][:, kc, :], kc == 0, kc == KC - 1),
                     reads=[bwa[i], bK], writes=[bps[i]])
            P.op("act", lambda e, i=i: e.copy(out=mrow[i], in_=ps[0:1, i, :]), reads=[bps[i]], writes=[bmr[i]])
            P.dma("pool", modd[:, cb * 512:(cb + 1) * 512], mrow[i], reads=[bmr[i]], writes=[bmd])

        def ada_finish(j0, j1):
            n = j1 - j0
            P.dma("sp", modT[0:n, :], modd[:, j0 * 128:j1 * 128].rearrange("o (j p) -> (o j) p", p=128), reads=[bmd], writes=[bmT])
            P.op("pe", TR(ps[:, 2, 0:n], modT[0:n, :], identf[0:n, 0:n]), reads=[bmT, bC], writes=[bps[2]])
            P.op("dve", TT(mod[:, j0:j1], ps[:, 2, 0:n], cs("b_l")[:, j0:j1], ALU.add), reads=[bps[2], bC], writes=[bK])

        for cb in range(8):
            ada_cols(cb)
        ada_finish(0, 32)
        P.op("dve", STT(A1, mod[:, 16:32], 1.0, cs("n1w"), ALU.add, ALU.mult), reads=[bK, bC], writes=[bK])

        xb = [A.alloc([4, D], F32) for _ in range(2)]
        bxb = [P.buf(f"xb{i}") for i in range(2)]
        xn = [A.alloc([D], BF16) for _ in range(2)]
        bxn = [P.buf(f"xn{i}") for i in range(2)]
        junk = A.alloc([D], BF16)
        bjunk = P.buf("junk")
        hTb = [A.alloc([KC, 512], BF16) for _ in range(2)]
        bhTa = [P.buf(f"hTa{i}") for i in range(2)]
        bhTd = [P.buf(f"hTd{i}") for i in range(2)]
        P.dma("sp", xb[0], xv[:, 0:4, :], writes=[bxb[0]])
        P.dma("sp", xb[1], xv[:, 4:8, :], writes=[bxb[1]])
        tiles = []
        for blk in range(NBLK):
            i = blk % 2
            for j in range(4):
                tiles.append((xb[i][:, j, :], bxb[i], A1, Sh1, (lambda kc, i=i, j=j: hTb[i][:, kc, j * 128:(j + 1) * 128]), bhTa[i], bhTd[i]))

        def after_tile(t):
            blk, j = divmod(t, 4)
            i = blk % 2
            if j == 3:
                P.dma("pool", hs_d[blk], hTb[i], reads=[bhTa[i], bhTd[i]], writes=[bhs[blk]])
                if blk + 2 < NBLK:
                    P.dma("sp", xb[i], xv[:, 4 * blk + 8:4 * blk + 12, :], writes=[bxb[i]])
            if j in (1, 3):
                cb = 8 + 2 * blk + (j // 2)
                ada_cols(cb)

        norm_pipeline(tiles, xn, bxn, junk, bjunk, after=after_tile)
        ada_finish(32, 96)
        P.op("dve", STT(A2, mod[:, 64:80], 1.0, cs("n2w"), ALU.add, ALU.mult), reads=[bK, bC], writes=[bK])
        P.barrier()
    A.off = base_off

    bms = P.buf("ms")

    if "B" in phases:
        wbuf = [A.alloc([KC, ML_W], BF16) for _ in range(2)]
        bwb = [P.buf(f"wb{i}") for i in range(2)]
        hbuf = [A.alloc([KC, 512], BF16) for _ in range(2)]
        bhb = [P.buf(f"hb{i}") for i in range(2)]
        mob = [A.alloc([2, 512], BF16) for _ in range(2)]
        bmoa = [P.buf(f"moa{i}") for i in range(2)]
        xc = [A.alloc([515], F32) for _ in range(2)]
        bxc = [P.buf(f"xc{i}") for i in range(2)]
        ycv = [A.alloc([512], F32) for _ in range(2)]
        bycv = [P.buf(f"ycv{i}") for i in range(2)]
        sgm = [A.alloc([512], F32) for _ in range(2)]
        bsgm = [P.buf(f"sgm{i}") for i in range(2)]
        qkT = [[A.alloc([512], BF16) for _ in range(2)] for _ in range(2)]
        bqkT = [[P.buf(f"qkT{p}_{i}") for i in range(2)] for p in range(2)]
        Gp = [A.alloc([96], F32) for _ in range(2)]
        bGp = [P.buf(f"G{p}") for p in range(2)]
        vaug = [A.alloc([257], BF16) for _ in range(2)]
        bva = [P.buf(f"vaug{i}") for i in range(2)]
        vw = [A.alloc([257], BF16) for _ in range(2)]
        bvw = [P.buf(f"vw{i}") for i in range(2)]
        ktm = [A.alloc([128], BF16) for _ in range(2)]
        bktm = [P.buf(f"ktm{i}") for i in range(2)]
        ATs = [A.alloc([128], BF16) for _ in range(2)]
        bATs = [P.buf(f"ATs{i}") for i in range(2)]
        C32 = A.alloc([257], F32)
        bC32 = P.buf("C32")
        Cb = [A.alloc([257], BF16) for _ in range(2)]
        bCb = [P.buf(f"Cb{i}") for i in range(2)]
        dds = A.alloc([64 * 8], F32)
        junk2 = A.alloc([256], BF16)
        bjunk2 = P.buf("junk2")
        hn = [A.alloc([256], F32) for _ in range(2)]
        bhn = [P.buf(f"hn{i}") for i in range(2)]
        sig = [A.alloc([256], F32) for _ in range(2)]
        bsig = [P.buf(f"sig{i}") for i in range(2)]
        hm = [A.alloc([256], BF16) for _ in range(2)]
        bhm = [P.buf(f"hm{i}") for i in range(2)]
        KT = [A.alloc([NTILE * 128], BF16) for _ in range(2)]
        bKT = [P.buf(f"KT{i}") for i in range(2)]
        Vh = A.alloc([2, NTILE, 128], BF16)
        bVh = P.buf("Vh")
        QT = [A.alloc([512], BF16) for _ in range(2)]
        bQT = [P.buf(f"QT{i}") for i in range(2)]
        sqb = [A.alloc([8, 64], F32) for _ in range(2)]
        bsqb = [P.buf(f"sqb{i}") for i in range(2)]
        rs = [A.alloc([8], F32) for _ in range(2)]
        brs = [P.buf(f"rs{i}") for i in range(2)]
        t1 = [A.alloc([8, 64], F32) for _ in range(2)]
        bt1 = [P.buf(f"t1{i}") for i in range(2)]
        qkb = [A.alloc([8, 64], BF16) for _ in range(2)]
        bqkb = [P.buf(f"qkb{i}") for i in range(2)]
        rt = [A.alloc([4, 8, 8], F32) for _ in range(2)]
        brt = [P.buf(f"rt{i}") for i in range(2)]
        PT = [A.alloc([2, 512], BF16) for _ in range(3)]
        bPT = [P.buf(f"PT{i}") for i in range(3)]
        Pacc = [A.alloc([512], F32) for _ in range(2)]
        bPacc = [P.buf(f"Pacc{i}") for i in range(2)]
        fz = [A.alloc([512], F32) for _ in range(5)]
        bfz = [P.buf(f"fz{i}") for i in range(5)]
        def interleave(g1, g2):
            gens = [g for g in (g1, g2) if g is not None]
            while gens:
                for g in list(gens):
                    try:
                        next(g)
                    except StopIteration:
                        gens.remove(g)

        P.op("dve", MSET(dds, 0.0), writes=[bK])
        for i in range(2):
            P.op("dve", MSET(vaug[i][:, 256:257], 1.0), writes=[bva[i]])

        wml = cs("nw").rearrange("p (h v) -> p h v", h=4)
        cw = cs("cw").rearrange("p (h q t) -> p h q t", h=4, q=2)
        cbv = cs("cb").rearrange("p (h q) -> p h q", h=4)
        gbv = cs("gb").rearrange("p (h g) -> p h g", h=4)
        wqk = cs("wqk").rearrange("p (g d) -> p g d", g=8)
        LNS = math.log(128.0 ** -0.5)

        npass = 8
        pass_cols = [(hh * ML_W, ML_W) for hh in range(4)] + [(4 * ML_W + hp * DA_W, DA_W) for hp in range(4)]

        def load_w(pi):
            o, n = pass_cols[pi]
            P.dma("pool", wbuf[pi % 2][:, :, 0:n], win_d[:, :, o:o + n], writes=[bwb[pi % 2]])

        hcount = [0]

        def load_h(blk):
            i = hcount[0] % 2
            hcount[0] += 1
            P.dma("sp", hbuf[i], hs_d[blk], reads=[bhs[blk]], writes=[bhb[i]])
            return i

        load_w(0)
        cur_tile_head = [0]
        mo_cnt = [0]
        KB = os.environ.get("KDBG_B", "")
        for pi in range(npass):
            if pi + 1 < npass:
                load_w(pi + 1)
            if (KB == "ml" and pi >= 4) or (KB == "da" and pi < 4) or (KB == "ml1" and pi >= 1) or (KB == "da1" and pi != 4):
                continue
            W = wbuf[pi % 2]
            bW = bwb[pi % 2]
            nxt = load_h(0)
            if pi < 4:
                hh = pi
                P.op("dve", MSET(C32, 0.0), writes=[bC32])
                hmap = {}

                def ml_prep(blk, hi):
                    H = hbuf[hi]
                    bH = bhb[hi]
                    cur = blk >= 4
                    pb = blk % 2
                    for qk in (1, 0):
                        if qk == 0 and blk < 3:
                            continue
                        for kc in range(KC):
                            P.op("pe", MM(ps[:, qk, :], W[:, kc, qk * 128:(qk + 1) * 128], H[:, kc, :], kc == 0, kc == KC - 1),
                                 reads=[bW, bH], writes=[bps[qk]])
                        yield
                        first = (blk == 0) if qk == 1 else (blk == 3)
                        if first:
                            P.op("dve", MSET(xc[qk][:, 0:3], 0.0), writes=[bxc[qk]])
                        elif blk == 4:
                            P.op("dve", TS(xc[qk][:, 0:3], xc[qk][:, 512:515], flag[:, 0:1], None, ALU.mult),
                                 reads=[bxc[qk], bC], writes=[bxc[qk]])
                        else:
                            P.op("dve", CP(xc[qk][:, 0:3], xc[qk][:, 512:515]), reads=[bxc[qk]], writes=[bxc[qk]])
                        P.op("act", lambda e, qk=qk: e.copy(out=xc[qk][:, 3:515], in_=ps[:, qk, :]), reads=[bps[qk]], writes=[bxc[qk]])
                        yield
                        if qk == 0 and blk < 4:
                            continue
                        P.op("dve", TS(ycv[qk], xc[qk][:, 0:512], cw[:, hh, qk, 0:1], cbv[:, hh, qk:qk + 1], ALU.mult, ALU.add),
                             reads=[bxc[qk], bC], writes=[bycv[qk]])
                        yield
                        for tp in range(1, 4):
                            P.op("dve", STT(ycv[qk], xc[qk][:, tp:tp + 512], cw[:, hh, qk, tp:tp + 1], ycv[qk], ALU.mult, ALU.add),
                                 reads=[bxc[qk], bC, bycv[qk]], writes=[bycv[qk]])
                            yield
                        P.op("act", ACTF(sgm[qk], ycv[qk], AF.Exp, scale=-1.0), reads=[bycv[qk]], writes=[bsgm[qk]])
                        P.op("act", ACTF(sgm[qk], sgm[qk], AF.Ln, bias=1.0), reads=[bsgm[qk]], writes=[bsgm[qk]])
                        yield
                        P.op("act", ACTF(sgm[qk], sgm[qk], AF.Exp, scale=-1.0), reads=[bsgm[qk]], writes=[bsgm[qk]])
                        P.op("dve", TT(qkT[pb][qk], ycv[qk], sgm[qk], ALU.mult), reads=[bycv[qk], bsgm[qk]], writes=[bqkT[pb][qk]])
                        yield
                    for j in range(4):
                        for kc in range(KC):
                            P.op("pe", MM(ps[:, 5, 384 + 2 * j:386 + 2 * j], H[:, kc, j * 128:(j + 1) * 128], W[:, kc, 768:770], kc == 0, kc == KC - 1),
                                 reads=[bW, bH], writes=[bps[5]])
                        yield
                    G = Gp[pb]
                    bG = bGp[pb]
                    gsb = G[:, 0:8].rearrange("p (j g) -> p j g", j=4)
                    th = G[:, 8:16].rearrange("p (j g) -> p j g", j=4)
                    e1, l1, li, nBe, tq, argw, argf = (G[:, 16 + 4 * i:20 + 4 * i] for i in range(7))
                    wv, fpv, eB = (G[:, 48 + 4 * i:52 + 4 * i] for i in range(3))
                    P.op("dve", TT(gsb, ps[:, 5, 384:392].rearrange("p (j g) -> p j g", j=4),
                                   gbv[:, hh:hh + 1, :].to_broadcast([128, 4, 2]), ALU.add), reads=[bps[5], bC], writes=[bG])
                    P.op("act", ACTF(th, gsb, AF.Exp, scale=2.0 / 15.0), reads=[bG], writes=[bG])
                    yield
                    P.op("act", ACTF(th, th, AF.Ln, bias=1.0), reads=[bG], writes=[bG])
                    P.op("act", ACTF(th, th, AF.Exp, scale=-1.0), reads=[bG], writes=[bG])
                    yield
                    P.op("dve", TS(th, th, -2.0, 1.0, ALU.mult, ALU.add), reads=[bG], writes=[bG])
                    P.op("act", ACTF(e1, th[:, :, 1], AF.Exp, scale=-15.0), reads=[bG], writes=[bG])
                    yield
                    P.op("act", ACTF(l1, e1, AF.Ln, bias=1.0), reads=[bG], writes=[bG])
                    P.op("dve", TS(li, th[:, :, 0], 15.0, None, ALU.mult), reads=[bG], writes=[bG])
                    yield
                    P.op("pe", MM(ps[:, 5, 400:404], tri, l1), reads=[bG, bC], writes=[bps[5]])
                    P.op("pe", MM(ps[:, 5, 416:420], onesf, l1), reads=[bG, bK], writes=[bps[5]])
                    yield
                    P.op("dve", CP(nBe, ps[:, 5, 416:420]), reads=[bps[5]], writes=[bG])
                    P.op("dve", TT(tq, li, nBe, ALU.subtract), reads=[bG], writes=[bG])
                    yield
                    P.op("dve", TT(argw, tq, ps[:, 5, 400:404], ALU.add), reads=[bG, bps[5]], writes=[bG])
                    P.op("dve", TT(argf, nBe, ps[:, 5, 400:404], ALU.subtract), reads=[bG, bps[5]], writes=[bG])
                    yield
                    P.op("act", ACTF(wv, argw, AF.Exp), reads=[bG], writes=[bG])
                    P.op("act", ACTF(fpv, argf, AF.Exp, bias=LNS), reads=[bG], writes=[bG])
                    P.op("act", ACTF(eB, nBe, AF.Exp, scale=-1.0), reads=[bG], writes=[bG])
                    yield

                def ml_tiles(blk, hi):
                    H = hbuf[hi]
                    bH = bhb[hi]
                    cur = blk >= 4
                    pb = blk % 2
                    G = Gp[pb]
                    bG = bGp[pb]
                    wv, fpv, eB = (G[:, 48 + 4 * i:52 + 4 * i] for i in range(3))
                    qT = qkT[pb][0]
                    kT = qkT[pb][1]
                    bqT = bqkT[pb][0]
                    bkT = bqkT[pb][1]
                    nv = 512 if cur else 256
                    mi = mo_cnt[0] % 2

                    def emit_vo(j):
                        vb_ = 2 + (j % 2)
                        for kc in range(KC):
                            P.op("pe", MM(ps[:, vb_, 0:nv], H[:, kc, j * 128:(j + 1) * 128], W[:, kc, 256:256 + nv], kc == 0, kc == KC - 1),
                                 reads=[bW, bH], writes=[bps[vb_]])

                    def tile(j):
                        t = blk * 4 + j
                        vb = 2 + (j % 2)
                        nb = 6 + (j % 2)
                        a = t % 2
                        P.op("dve", TS(vw[a][:, 0:256], ps[:, vb, 0:256], wv[:, j:j + 1], None, ALU.mult), reads=[bps[vb], bG], writes=[bvw[a]])
                        P.op("dve", CP(vw[a][:, 256:257], wv[:, j:j + 1]), reads=[bG], writes=[bvw[a]])
                        P.op("pe", TR(psbf(5)[:, 2 + a, :], kT[:, j * 128:(j + 1) * 128], identb), reads=[bkT, bK], writes=[bps[5]])
                        P.op("act", lambda e, a=a: e.copy(out=ktm[a], in_=psbf(5)[:, 2 + a, :]), reads=[bps[5]], writes=[bktm[a]])
                        yield
                        if cur:
                            u = cur_tile_head[0]
                            cur_tile_head[0] += 1
                            dd = dds[:, 8 * u:8 * u + 8]
                            bdd = P.buf(f"dd{u}")
                            P.op("act", lambda e, a=a, vb=vb: e.copy(out=vaug[a][:, 0:256], in_=ps[:, vb, 0:256]), reads=[bps[vb]], writes=[bva[a]])
                            P.op("act", ACTF(sig[a], ps[:, vb, 256:512], AF.Exp, scale=-1.0), reads=[bps[vb]], writes=[bsig[a]])
                            P.op("pe", MM(ps[:, 5, 0:128], kT[:, j * 128:(j + 1) * 128], qT[:, j * 128:(j + 1) * 128]),
                                 reads=[bqT, bkT], writes=[bps[5]])
                            P.op("dve", STT(ATs[a], ps[:, 5, 0:128], wv[:, j:j + 1], tri, ALU.mult, ALU.mult), reads=[bps[5], bG, bC], writes=[bATs[a]])
                            yield
                        if blk == 4 and j == 0:
                            P.op("dve", TS(C32, C32, flag[:, 0:1], None, ALU.mult), reads=[bC32, bC], writes=[bC32])
                        if cur:
                            P.op("dve", TS(Cb[a], C32, eB[:, j:j + 1], None, ALU.mult), reads=[bC32, bG], writes=[bCb[a]])
                            P.op("pe", MM(ps[:, nb, 0:257], ATs[a], vaug[a], True, False), reads=[bATs[a], bva[a]], writes=[bps[nb]])
                            P.op("pe", MM(ps[:, nb, 0:257], qT[:, j * 128:(j + 1) * 128], Cb[a], False, True), reads=[bqT, bCb[a]], writes=[bps[nb]])
                        P.op("pe", MM(ps[:, 4, 0:257], ktm[a], vw[a]), reads=[bktm[a], bvw[a]], writes=[bps[4]])
                        P.op("dve", STT(C32, C32, eB[:, j:j + 1], ps[:, 4, 0:257], ALU.mult, ALU.add), reads=[bC32, bG, bps[4]], writes=[bC32])
                        yield
                        if not cur:
                            return
                        P.op("dve", TT(dd[:, 0:1], ps[:, nb, 256:257], fpv[:, j:j + 1], ALU.mult), reads=[bps[nb], bG], writes=[bdd])
                        P.op("dve", TS(dd[:, 1:2], dd[:, 0:1], -1.0, 1.0, ALU.mult, ALU.max), reads=[bdd], writes=[bdd])
                        yield
                        P.op("dve", TT(dd[:, 1:2], dd[:, 1:2], dd[:, 0:1], ALU.max), reads=[bdd], writes=[bdd])
                        P.op("dve", RCP(dd[:, 1:2], dd[:, 1:2]), reads=[bdd], writes=[bdd])
                        yield
                        P.op("dve", TT(dd[:, 2:3], fpv[:, j:j + 1], dd[:, 1:2], ALU.mult), reads=[bdd, bG], writes=[bdd])
                        P.op("act", ACTF(junk2, ps[:, nb, 0:256], AF.Square, scale=dd[:, 2:3], accum=dd[:, 3:4]), reads=[bps[nb], bdd], writes=[bjunk2, bdd])
                        yield
                        P.op("act", ACTF(dd[:, 4:5], dd[:, 3:4], AF.Ln, scale=1.0 / 256.0, bias=EPS), reads=[bdd], writes=[bdd])
                        P.op("act", ACTF(dd[:, 4:5], dd[:, 4:5], AF.Exp, scale=-0.5), reads=[bdd], writes=[bdd])
                        yield
                        P.op("dve", TT(dd[:, 5:6], dd[:, 2:3], dd[:, 4:5], ALU.mult), reads=[bdd], writes=[bdd])
                        yield
                        P.op("dve", STT(hn[a], ps[:, nb, 0:256], dd[:, 5:6], wml[:, hh, :], ALU.mult, ALU.mult), reads=[bps[nb], bdd, bC], writes=[bhn[a]])
                        P.op("act", ACTF(sig[a], sig[a], AF.Ln, bias=1.0), reads=[bsig[a]], writes=[bsig[a]])
                        yield
                        P.op("act", ACTF(sig[a], sig[a], AF.Exp, scale=-1.0), reads=[bsig[a]], writes=[bsig[a]])
                        yield
                        P.op("dve", TT(hm[a], hn[a], sig[a], ALU.mult), reads=[bhn[a], bsig[a]], writes=[bhm[a]])
                        yield
                        for i2 in range(2):
                            P.op("pe", TR(psbf(5)[:, 4 + i2, :], hm[a][:, i2 * 128:(i2 + 1) * 128], identb), reads=[bhm[a], bK], writes=[bps[5]])
                        P.op("act", lambda e, mi=mi, j=j: e.copy(out=mob[mi][:, :, j * 128:(j + 1) * 128], in_=psbf(5)[:, 4:6, :]),
                             reads=[bps[5]], writes=[bmoa[mi]], relaxed=True)
                        yield

                    emit_vo(0)
                    yield
                    act_t = []
                    for j in range(4):
                        if j + 1 < 4:
                            emit_vo(j + 1)
                            yield
                        act_t.append(tile(j))
                        if len(act_t) == 2:
                            done0 = False
                            while not done0:
                                for g in list(act_t):
                                    try:
                                        next(g)
                                    except StopIteration:
                                        if g is act_t[0]:
                                            done0 = True
                                        act_t.remove(g)
                                yield
                        else:
                            for _ in range(3):
                                try:
                                    next(act_t[0])
                                except StopIteration:
                                    act_t.pop()
                                    break
                            yield
                    for g in act_t:
                        for _ in g:
                            yield
                    if cur:
                        cb4 = blk - 4
                        P.dma("sp", ms_d[2 * hh:2 * hh + 2, :, cb4 * 512:(cb4 + 1) * 512].rearrange("c p n -> p c n"), mob[mi],
                              reads=[bmoa[mi]], writes=[bms])
                        mo_cnt[0] += 1

                hmap[0] = nxt
                for _ in ml_prep(0, hmap[0]):
                    pass
                for blk in range(NBLK):
                    g2 = None
                    if blk + 1 < NBLK:
                        hmap[blk + 1] = load_h(blk + 1)
                        g2 = ml_prep(blk + 1, hmap[blk + 1])
                    interleave(ml_tiles(blk, hmap[blk]), g2)
            else:
                hp = pi - 4
                fin2q = []
                for blk in range(NBLK):
                    while fin2q:
                        fin2q.pop(0)()
                    hi = nxt
                    if blk + 1 < NBLK:
                        nxt = load_h(blk + 1)
                    H = hbuf[hi]
                    bH = bhb[hi]
                    cur = blk >= 4
                    c0 = 0 if cur else 128

                    def proj_mm(j):
                        b0 = 2 * (j % 2)
                        for hd in range(2):
                            for kc in range(KC):
                                P.op("pe", MM(ps[:, b0 + hd, c0:384], H[:, kc, j * 128:(j + 1) * 128], W[:, kc, hd * 384 + c0:hd * 384 + 384],
                                              kc == 0, kc == KC - 1), reads=[bW, bH], writes=[bps[b0 + hd]])

                    def tile_gen(j):
                        t = blk * 4 + j
                        b0 = 2 * (j % 2)
                        bb = [bps[b0], bps[b0 + 1]]
                        sl = t % 2
                        P.op("act", lambda e, t=t, b0=b0: e.copy(out=Vh[:, :, t, :], in_=ps[:, b0:b0 + 2, 256:384]), reads=bb, writes=[bVh], relaxed=True)
                        pq = ps[:, b0:b0 + 2, 0:256].rearrange("p h (g d) -> p h g d", g=4)
                        sq4 = sqb[sl].rearrange("p (h g) d -> p h g d", h=2)
                        t14 = t1[sl].rearrange("p (h g) d -> p h g d", h=2)
                        P.op("act", ACTF(sq4, pq, AF.Square), reads=bb, writes=[bsqb[sl]])
                        yield
                        P.op("dve", lambda e, sl=sl: e.tensor_reduce(out=rs[sl], in_=sqb[sl], axis=AX.X, op=ALU.add), reads=[bsqb[sl]], writes=[brs[sl]])
                        yield
                        P.op("act", ACTF(rs[sl], rs[sl], AF.Ln, scale=1.0 / 64.0, bias=EPS), reads=[brs[sl]], writes=[brs[sl]])
                        P.op("act", ACTF(rs[sl], rs[sl], AF.Exp, scale=-0.5), reads=[brs[sl]], writes=[brs[sl]])
                        yield
                        P.op("dve", TT(t14, pq, rs[sl].rearrange("p (h g) -> p h g", h=2)[:, :, :, None].to_broadcast([128, 2, 4, 64]), ALU.mult),
                             reads=bb + [brs[sl]], writes=[bt1[sl]])
                        yield
                        P.op("dve", TT(t1[sl], t1[sl], wqk, ALU.mult), reads=[bt1[sl], bC], writes=[bt1[sl]])
                        yield
                        P.op("act", lambda e, sl=sl: e.copy(out=qkb[sl], in_=t1[sl]), reads=[bt1[sl]], writes=[bqkb[sl]])
                        cosb = cosT[:, t:t + 1, :].to_broadcast([128, 8, 8])
                        sinb = sinT[:, t:t + 1, :].to_broadcast([128, 8, 8])
                        x1 = t1[sl][:, :, 0:8]
                        x2 = t1[sl][:, :, 8:16]
                        R_ = rt[sl]
                        P.op("dve", TT(R_[:, 0], x1, cosb, ALU.mult), reads=[bt1[sl], bK], writes=[brt[sl]])
                        P.op("dve", TT(R_[:, 1], x2, sinb, ALU.mult), reads=[bt1[sl], bK], writes=[brt[sl]], relaxed=True)
                        yield
                        P.op("dve", TT(R_[:, 2], x2, cosb, ALU.mult), reads=[bt1[sl], bK], writes=[brt[sl]], relaxed=True)
                        P.op("dve", TT(R_[:, 3], x1, sinb, ALU.mult), reads=[bt1[sl], bK], writes=[brt[sl]], relaxed=True)
                        yield
                        P.op("dve", TT(qkb[sl][:, :, 0:8], R_[:, 0], R_[:, 1], ALU.subtract), reads=[brt[sl]], writes=[bqkb[sl]])
                        P.op("dve", TT(qkb[sl][:, :, 8:16], R_[:, 2], R_[:, 3], ALU.add), reads=[brt[sl]], writes=[bqkb[sl]], relaxed=True)
                        yield
                        qk2 = qkb[sl].rearrange("p g d -> p (g d)")
                        for hd in range(2):
                            P.op("pe", TR(psbf(4)[:, 2 * hd, :], qk2[:, hd * 256 + 128:hd * 256 + 256], identb), reads=[bqkb[sl], bK], writes=[bps[4]])
                            if cur:
                                P.op("pe", TR(psbf(4)[:, 2 * hd + 1, :], qk2[:, hd * 256:hd * 256 + 128], identb), reads=[bqkb[sl], bK], writes=[bps[4]])
                        yield
                        for hd in range(2):
                            P.op("act", lambda e, hd=hd, t=t: e.copy(out=KT[hd][:, t * 128:(t + 1) * 128], in_=psbf(4)[:, 2 * hd, :]),
                                 reads=[bps[4]], writes=[bKT[hd]], relaxed=True)
                            if cur:
                                P.op("act", lambda e, hd=hd, j=j: e.copy(out=QT[hd][:, j * 128:(j + 1) * 128], in_=psbf(4)[:, 2 * hd + 1, :]),
                                     reads=[bps[4]], writes=[bQT[hd]], relaxed=True)
                        yield

                    proj_mm(0)
                    act_g = []
                    for j in range(4):
                        if j + 1 < 4:
                            proj_mm(j + 1)
                        act_g.append(tile_gen(j))
                        if len(act_g) == 2:
                            done0 = False
                            nstep = 0
                            while not done0:
                                for g in list(act_g):
                                    try:
                                        next(g)
                                    except StopIteration:
                                        if g is act_g[0]:
                                            done0 = True
                                        act_g.remove(g)
                                nstep += 1
                        elif j == 0:
                            for _ in range(4):
                                next(act_g[0])
                    for g in act_g:
                        for _ in g:
                            pass
                    if not cur:
                        continue
                    cb4 = blk - 4
                    ndone = 16 + 4 * cb4
                    for hd in range(2):
                        steps = [(c, 0, None) for c in range(ndone)] + [(ndone + jd, jd * 128, jd) for jd in (3, 2, 1, 0)]
                        nst = len(steps)

                        def emit_S(n):
                            c, q0, jd = steps[n]
                            sb0 = 4 + 2 * (n % 2)
                            for m in range(2):
                                P.op("pe", MM(ps[:, sb0 + m, q0:512], KT[hd][m * 64:(m + 1) * 64, c * 128:(c + 1) * 128], QT[hd][m * 64:(m + 1) * 64, q0:512]),
                                     reads=[bKT[hd], bQT[hd]], writes=[bps[sb0 + m]])

                        def emit_E(n):
                            c, q0, jd = steps[n]
                            sb0 = 4 + 2 * (n % 2)
                            pt = n % 3
                            P.op("act", ACTF(PT[pt][:, :, q0:512], ps[:, sb0:sb0 + 2, q0:512], AF.Exp, scale=0.125, bias=(pbias[:, 0:1] if c < 16 else None)),
                                 reads=[bps[sb0], bps[sb0 + 1], bC], writes=[bPT[pt]])
                            if jd is not None:
                                P.op("dve", MSET(PT[pt][64:128, :, q0:q0 + 64], 0.0), reads=[bPT[pt]], writes=[bPT[pt]])

                        def emit_PV(n):
                            c, q0, jd = steps[n]
                            pt = n % 3
                            for m in range(2):
                                P.op("pe", MM(ps[:, m, q0:512], Vh[:, hd, c, :], PT[pt][:, m, q0:512], c == 0, jd == 0),
                                     reads=[bVh, bPT[pt]], writes=[bps[m]])
                            if c == 0:
                                P.op("dve", CP(Pacc[0], PT[pt][:, 0, :]), reads=[bPT[pt]], writes=[bPacc[0]])
                            else:
                                P.op("dve", TT(Pacc[0][:, q0:512], Pacc[0][:, q0:512], PT[pt][:, 0, q0:512], ALU.add), reads=[bPT[pt], bPacc[0]], writes=[bPacc[0]])
                            P.op("pe", MM(ps[:, 3, q0:512], onesb, PT[pt][:, 1, q0:512], c == 0, jd == 0), reads=[bPT[pt], bK], writes=[bps[3]])

                        emit_S(0)
                        emit_S(1)
                        for n in range(nst):
                            emit_E(n)
                            if n + 2 < nst:
                                emit_S(n + 2)
                            emit_PV(n)
                            if fin2q and n >= 2:
                                fin2q.pop(0)()
                        while fin2q:
                            fin2q.pop(0)()
                        P.op("pe", MM(ps[:, 2, :], onesf, Pacc[0]), reads=[bPacc[0], bK], writes=[bps[2]])
                        P.op("dve", RCP(fz[0], ps[:, 2, :]), reads=[bps[2]], writes=[bfz[0]])
                        P.op("dve", TT(fz[1], ps[:, 0, :], fz[0], ALU.mult), reads=[bps[0], bfz[0]], writes=[bfz[1]])
                        P.op("dve", RCP(fz[2], ps[:, 3, :]), reads=[bps[3]], writes=[bfz[2]])
                        P.op("dve", TT(fz[3], ps[:, 1, :], fz[2], ALU.mult), reads=[bps[1], bfz[2]], writes=[bfz[3]])

                        def f2a():
                            P.op("dve", STT(fz[1], fz[3], neglam[:, 0:1], fz[1], ALU.mult, ALU.add), reads=[bfz[3], bfz[1], bK], writes=[bfz[1]])

                        def f2b():
                            P.op("act", ACTF(fz[4], fz[1], AF.Square), reads=[bfz[1]], writes=[bfz[4]])

                        def f2c():
                            P.op("pe", MM(ps[:, 2, :], onesf, fz[4]), reads=[bfz[4], bK], writes=[bps[2]])

                        def f2d():
                            P.op("act", ACTF(fz[0], ps[:, 2, :], AF.Ln, scale=1.0 / 128.0, bias=EPS), reads=[bps[2]], writes=[bfz[0]])

                        def f2e():
                            P.op("act", ACTF(fz[0], fz[0], AF.Exp, scale=-0.5), reads=[bfz[0]], writes=[bfz[0]])

                        def f2f(hd=hd, cb4=cb4):
                            mi = mo_cnt[0] % 2
                            mo_cnt[0] += 1
                            P.op("dve", STT(mob[mi][:, 0, :], fz[1], subw_s[:, 0:1], fz[0], ALU.mult, ALU.mult), reads=[bfz[1], bfz[0], bK], writes=[bmoa[mi]])
                            ch = 8 + 2 * hp + hd
                            P.dma("sp", ms_d[ch, :, cb4 * 512:(cb4 + 1) * 512], mob[mi][:, 0, :], reads=[bmoa[mi]], writes=[bms])

                        fin2q.extend([f2a, f2b, f2c, f2d, f2e, f2f])
                while fin2q:
                    fin2q.pop(0)()
        P.barrier()
    A.off = base_off

    bout = P.buf("out")
    if "C" in phases:
        h2T = A.alloc([KC, 1024], BF16)
        bh2a, bh2d = P.buf("h2a"), P.buf("h2d")
        wgu = [A.alloc([KC, 256], BF16) for _ in range(2)]
        bwgu = [P.buf(f"wgu{i}") for i in range(2)]
        wdb = [A.alloc([NFF, 128], BF16) for _ in range(2)]
        bwd = [P.buf(f"wd{i}") for i in range(2)]
        tmpf = [A.alloc([2, 512], F32) for _ in range(2)]
        btmp = [P.buf(f"tmp{i}") for i in range(2)]
        un0 = A.off
        mxb = [A.alloc([KC, 512], BF16) for _ in range(2)]
        bmx = [P.buf(f"mx{i}") for i in range(2)]
        xsb = A.alloc([8, D], F32)
        bxsb = P.buf("xsb")
        xn = [A.alloc([D], BF16) for _ in range(2)]
        bxn = [P.buf(f"xn{i}") for i in range(2)]
        junk = A.alloc([D], BF16)
        bjunk = P.buf("junk")
        un1 = A.off
        A.off = un0
        actT = A.alloc([NFF, 1024], BF16)
        bact = P.buf("actT")
        xs = [A.alloc([8, 128], F32) for _ in range(2)]
        bxs = [P.buf(f"xs{i}") for i in range(2)]
        sg = [A.alloc([512], F32) for _ in range(2)]
        bsg = [P.buf(f"sg{i}") for i in range(2)]
        A.off = max(A.off, un1)
        wob = [wgu[i][:, :, 0:128] for i in range(2)]
        out_v = out_d.rearrange("(t p) d -> p t d", p=128)
        tcount = [0]
        wcount = [0]
        for TB in range(2):
            for sbk in range(2):
                g512 = TB * 2 + sbk
                P.dma("sp", mxb[sbk], ms_d[:, :, g512 * 512:(g512 + 1) * 512].rearrange("c p n -> p c n"), reads=[bms], writes=[bmx[sbk]])
                P.dma("sp", xsb[:, 4 * sbk:4 * sbk + 4, :], xv[:, 16 + 4 * g512:16 + 4 * g512 + 4, :], writes=[bxsb])
            pend = None
            for m in range(16):
                wi = wcount[0] % 2
                wcount[0] += 1
                P.dma("pool", wob[wi], wo_d[m], writes=[bwgu[wi]])
                pb = 2 * (m % 2)
                for sbk in range(2):
                    for kc in range(KC):
                        P.op("pe", MM(ps[:, pb + sbk, :], wob[wi][:, kc, :], mxb[sbk][:, kc, :], kc == 0, kc == KC - 1),
                             reads=[bwgu[wi], bmx[sbk]], writes=[bps[pb + sbk]])
                ti = tcount[0] % 2
                tcount[0] += 1
                P.op("act", ACTF(tmpf[ti], ps[:, pb:pb + 2, :], AF.Copy, scale=gate1[:, m:m + 1]), reads=[bps[pb], bps[pb + 1], bK], writes=[btmp[ti]])
                if pend is not None:
                    pend()

                def pend(m=m, ti=ti):
                    tb0 = 4 + 2 * ti
                    for j in range(8):
                        P.op("pe", TR(ps[:, tb0 + j // 4, (j % 4) * 128:(j % 4 + 1) * 128], tmpf[ti][:, j // 4, (j % 4) * 128:(j % 4 + 1) * 128], identf),
                             reads=[btmp[ti], bC], writes=[bps[tb0 + j // 4]])
                    xslice = xsb[:, :, m * 128:(m + 1) * 128]
                    P.op("dve", TT(xslice, ps[:, tb0:tb0 + 2, :].rearrange("p b (j f) -> p (b j) f", j=4), xslice, ALU.add),
                         reads=[bps[tb0], bps[tb0 + 1], bxsb], writes=[bxsb])
            pend()
            ntiles = []
            for j in range(8):
                col = j * 128
                ntiles.append((xsb[:, j, :], bxsb, A2, Sh2, (lambda kc, col=col: h2T[:, kc, col:col + 128]), bh2a, bh2d))
            norm_pipeline(ntiles, xn, bxn, junk, bjunk)
            P.dma("sp", out_v[:, 8 * TB:8 * TB + 8, :], xsb, reads=[bxsb], writes=[bout])
            P.barrier()
            for jf in range(NFF):
                wi = wcount[0] % 2
                wcount[0] += 1
                P.dma("pool", wgu[wi], wgu_d[jf], writes=[bwgu[wi]])
                for sb in range(2):
                    for gu in range(2):
                        ob = 2 * (sb % 2) + gu + (4 if jf % 2 else 0)
                        for kc in range(KC):
                            P.op("pe", MM(ps[:, ob, :], wgu[wi][:, kc, gu * 128:(gu + 1) * 128], h2T[:, kc, sb * 512:(sb + 1) * 512], kc == 0, kc == KC - 1),
                                 reads=[bwgu[wi], bh2a, bh2d], writes=[bps[ob]])
                    ob = 2 * (sb % 2) + (4 if jf % 2 else 0)
                    si = tcount[0] % 2
                    tcount[0] += 1
                    P.op("act", ACTF(sg[si], ps[:, ob, :], AF.Silu), reads=[bps[ob]], writes=[bsg[si]])
                    P.op("dve", TT(actT[:, jf, sb * 512:(sb + 1) * 512], sg[si], ps[:, ob + 1, :], ALU.mult), reads=[bsg[si], bps[ob + 1]], writes=[bact], relaxed=True)
            pend = None
            for m in range(16):
                wi = m % 2
                P.dma("pool", wdb[wi], wd_d[m], writes=[bwd[wi]])
                xi = m % 2
                P.dma("sp", xs[xi], out_v[:, 8 * TB:8 * TB + 8, m * 128:(m + 1) * 128], reads=[bout], writes=[bxs[xi]], sem_buf=bxs[xi])
                for sb in range(2):
                    ob = sb
                    for fc in range(NFF):
                        P.op("pe", MM(ps[:, ob, :], wdb[wi][:, fc, :], actT[:, fc, sb * 512:(sb + 1) * 512], fc == 0, fc == NFF - 1),
                             reads=[bwd[wi], bact], writes=[bps[ob]])
                    ti = tcount[0] % 2
                    tcount[0] += 1
                    P.op("act", ACTF(tmpf[ti][:, 0, :], ps[:, ob, :], AF.Copy, scale=gate2[:, m:m + 1]), reads=[bps[ob], bK], writes=[btmp[ti]])
                    if pend is not None:
                        pend()

                    def pend(m=m, sb=sb, ti=ti, xi=xi):
                        tb = 2 + ti
                        for j in range(4):
                            P.op("pe", TR(ps[:, tb, j * 128:(j + 1) * 128], tmpf[ti][:, 0, j * 128:(j + 1) * 128], identf), reads=[btmp[ti], bC], writes=[bps[tb]])
                        xslice = xs[xi][:, sb * 4:(sb + 1) * 4, :]
                        P.op("dve", TT(xslice, ps[:, tb, :].rearrange("p (j f) -> p j f", j=4), xslice, ALU.add), reads=[bps[tb], bxs[xi]], writes=[bxs[xi]])
                        if sb == 1:
                            bo2 = P.buf(f"o2_{TB}_{m}")
                            P.dma("sp", out_v[:, 8 * TB:8 * TB + 8, m * 128:(m + 1) * 128], xs[xi], reads=[bxs[xi]], writes=[bout, bo2], sem_buf=bo2)
            pend()
            P.barrier()
    P.barrier()
    P.emit()
    return nc, P


def _prep_shared(inp):
    f = np.float32
    w_in = np.asarray(inp["w_in"], f)[0]
    cols = []
    for hh in range(4):
        cols += list(range(hh * 128, hh * 128 + 128)) + list(range(512 + hh * 128, 512 + hh * 128 + 128))
        cols += list(range(1024 + hh * 256, 1024 + hh * 256 + 256)) + list(range(2048 + hh * 256, 2048 + hh * 256 + 256))
        cols += [3072 + hh, 3076 + hh]
    for hp in range(4):
        for hd in range(2):
            h = 2 * hp + hd
            cols += list(range(3080 + h * 128, 3080 + h * 128 + 128)) + list(range(4104 + h * 128, 4104 + h * 128 + 128))
            cols += list(range(5128 + h * 128, 5128 + h * 128 + 128))
    cols = np.asarray(cols)
    sh = {}
    sh["w_in_l"] = np.ascontiguousarray(w_in.reshape(KC, 128, N_IN).transpose(1, 0, 2)[:, :, cols])
    sh["w_ada_l"] = np.ascontiguousarray(np.asarray(inp["w_ada"], f)[0].reshape(KC, 128, 24, 512).transpose(2, 1, 0, 3))
    sh["wo_l"] = np.ascontiguousarray(np.asarray(inp["w_out"], f)[0].reshape(KC, 128, 16, 128).transpose(2, 1, 0, 3))
    wgu = np.asarray(inp["w_gate_up"], f)[0].reshape(KC, 128, 2, NFF, 128)
    sh["wgu_l"] = np.ascontiguousarray(wgu.transpose(3, 1, 0, 2, 4).reshape(NFF, 128, KC, 256))
    sh["wd_l"] = np.ascontiguousarray(np.asarray(inp["w_down"], f)[0].reshape(NFF, 128, 16, 128).transpose(2, 1, 0, 3))
    return sh


def _prep_cst(inp, b, half):
    f = np.float32
    c = np.zeros((128, NCST), f)

    def put(name, arr):
        o, n = CO[name]
        c[:, o:o + n] = np.asarray(arr, f).reshape(128, n)

    rep = lambda v: np.broadcast_to(np.asarray(v, f).reshape(1, -1), (128, np.asarray(v).size))
    put("c_l", np.asarray(inp["c"], f)[b].reshape(KC, 128).T)
    put("b_l", np.asarray(inp["b_ada"], f)[0].reshape(96, 128).T)
    put("n1w", np.asarray(inp["norm1_w"], f)[0].reshape(KC, 128).T)
    put("n2w", np.asarray(inp["norm2_w"], f)[0].reshape(KC, 128).T)
    cw = np.asarray(inp["mlstm_conv_w"], f)[0].reshape(4, 2, 4, 128)
    put("cw", cw.transpose(3, 2, 1, 0).reshape(128, 32))
    cb = np.asarray(inp["mlstm_conv_b"], f)[0].reshape(2, 4, 128)
    put("cb", cb.transpose(2, 1, 0).reshape(128, 8))
    gb = np.asarray(inp["mlstm_gate_b"], f)[0].reshape(2, 4)
    put("gb", rep(gb.T.reshape(-1)))
    put("nw", rep(np.asarray(inp["mlstm_norm_w"], f)[0].reshape(-1)))
    qw = np.asarray(inp["q_norm_w"], f)[0]
    kw = np.asarray(inp["k_norm_w"], f)[0]
    put("wqk", rep(np.concatenate([qw, qw, kw, kw, qw, qw, kw, kw])))
    put("lamv", rep(np.concatenate([np.asarray(inp[k], f)[0] for k in ("lambda_q1", "lambda_k1", "lambda_q2", "lambda_k2")])))
    put("subw", np.asarray(inp["subln_w"], f)[0].reshape(128, 1))
    put("flag", np.full((128, 1), float(half), f))
    put("pbias", np.full((128, 1), 0.0 if half else -30000.0, f))
    invf = (np.float32(500000.0) ** (-np.arange(0, 16, 2, dtype=np.float32) / np.float32(16))).astype(f)
    put("invf", rep(invf))
    put("ident", np.eye(128, dtype=f))
    put("tri", np.triu(np.ones((128, 128), f)))
    return c


def _in_maps(inp):
    sh = _prep_shared(inp)
    x = np.asarray(inp["x"], np.float32)
    pos = np.asarray(inp["positions"], np.int32)
    maps = []
    for core in range(8):
        b, half = core // 2, core % 2
        m = dict(sh)
        m["x"] = np.ascontiguousarray(np.concatenate([x[b, 0:TOK], x[b, half * TOK:(half + 1) * TOK]], axis=0))
        pp = np.concatenate([pos[b, 0:TOK], pos[b, half * TOK:(half + 1) * TOK]])
        m["pos"] = np.ascontiguousarray(pp.reshape(NTILE, 128).T)
        m["cst"] = _prep_cst(inp, b, half)
        maps.append(m)
    return maps


_NC = None


def kernel(**inputs):
    global _NC
    if _NC is None:
        _NC = build()[0]
    maps = _in_maps(inputs)
    res = run_bass_kernel_spmd(_NC, maps, core_ids=list(range(8)))
    out = np.empty((4, 2 * TOK, D), np.float32)
    for core in range(8):
        b, half = core // 2, core % 2
        out[b, half * TOK:(half + 1) * TOK] = res.results[core]["out"]
    return out
```

```python
import math
import os
import numpy as np
import concourse.bass as bass
import concourse.mybir as mybir
from concourse.bass_utils import run_bass_kernel_spmd

F32 = mybir.dt.float32
BF16 = mybir.dt.bfloat16
I32 = mybir.dt.int32
AF = mybir.ActivationFunctionType
ALU = mybir.AluOpType
AX = mybir.AxisListType

SEM_EPOCH = 16000
import os as _os
SAME_ENGINE_SYNC = _os.environ.get("K_SES", "1") == "1"


class Buf:
    __slots__ = ("name", "w", "rd", "dsem", "dcnt", "excl")

    def __init__(self, name, excl=False):
        self.name = name
        self.excl = excl
        self.w = None
        self.rd = {}
        self.dsem = None
        self.dcnt = 0


class Op:
    __slots__ = ("eng", "fn", "waits", "need_sig", "sigidx", "dma_sem")

    def __init__(self, eng, fn):
        self.eng = eng
        self.fn = fn
        self.waits = []
        self.need_sig = False
        self.sigidx = None
        self.dma_sem = None


class Prog:
    ENGS = ("pe", "act", "dve", "pool", "sp")

    def __init__(self, nc):
        self.nc = nc
        self.ops = {e: [] for e in self.ENGS}
        self.nsem = 0
        self.bufs = []

    def buf(self, name, excl=False):
        b = Buf(name, excl)
        self.bufs.append(b)
        return b

    def _newsem(self, name):
        self.nsem += 1
        return self.nc.alloc_semaphore(f"s{self.nsem}_{name}")

    def _collect(self, op, reads, writes, relaxed):
        toks = []
        for b in reads:
            if b.w is not None:
                toks.append((b.w, False))
            if b.excl:
                for k, t in b.rd.items():
                    if k != op.eng:
                        toks.append((t, False))
        for b in writes:
            if b.w is not None:
                toks.append((b.w, True))
            for t in b.rd.values():
                toks.append((t, False))
        for t, waw in toks:
            if t[0] == "op":
                p = t[1]
                if p is op:
                    continue
                if p.eng == op.eng and (p.eng == "pe" or not SAME_ENGINE_SYNC or (waw and relaxed)):
                    continue
                p.need_sig = True
            op.waits.append(t)

    def op(self, eng, fn, reads=(), writes=(), relaxed=False):
        o = Op(eng, fn)
        self._collect(o, reads, writes, relaxed)
        tok = ("op", o)
        for b in reads:
            b.rd[eng] = tok
        for b in writes:
            b.w = tok
            b.rd = {}
        self.ops[eng].append(o)
        return o

    def dma(self, eng, out_ap, in_ap, reads=(), writes=(), sem_buf=None):
        sb = sem_buf or (writes[0] if writes else reads[0])
        if sb.dsem is None:
            sb.dsem = self._newsem(sb.name)
        sb.dcnt += 16
        sem, cnt = sb.dsem, sb.dcnt

        def fn(e, out_ap=out_ap, in_ap=in_ap):
            return e.dma_start(out=out_ap, in_=in_ap)

        o = Op(eng, fn)
        o.dma_sem = sem
        self._collect(o, reads, writes, False)
        tok = ("sem", sem, cnt)
        for b in reads:
            b.rd[("d", id(sem))] = tok
        for b in writes:
            b.w = tok
            b.rd = {}
        self.ops[eng].append(o)
        return o

    def barrier(self):
        toks = []
        for b in self.bufs:
            if b.w is not None:
                toks.append(b.w)
            toks.extend(b.rd.values())
        for e in self.ENGS:
            for o in reversed(self.ops[e]):
                if o.fn is not None and o.dma_sem is None:
                    toks.append(("op", o))
                    break
        for e in self.ENGS:
            o = Op(e, None)
            for t in toks:
                if t[0] == "op":
                    if t[1].eng == e:
                        continue
                    t[1].need_sig = True
                o.waits.append(t)
            self.ops[e].append(o)
        for b in self.bufs:
            b.w = None
            b.rd = {}

    def final_wait(self, eng, bufs):
        o = Op(eng, None)
        for b in bufs:
            if b.w is not None:
                o.waits.append(b.w)
                if b.w[0] == "op":
                    b.w[1].need_sig = True
        self.ops[eng].append(o)

    def emit(self):
        nc = self.nc
        eng_sems = {}
        for e in self.ENGS:
            n = 0
            for o in self.ops[e]:
                if o.need_sig and o.dma_sem is None and o.fn is not None:
                    o.sigidx = n
                    n += 1
            eng_sems[e] = [self._newsem(f"{e}{i}") for i in range((n + SEM_EPOCH - 1) // SEM_EPOCH)]
        self.stats = {e: len(self.ops[e]) for e in self.ENGS}
        self.stats["nsem"] = self.nsem

        def resolve(t):
            if t[0] == "sem":
                return t[1], t[2]
            p = t[1]
            assert p.sigidx is not None, "waiting on op without signal"
            return eng_sems[p.eng][p.sigidx // SEM_EPOCH], p.sigidx % SEM_EPOCH + 1

        def run(e, h):
            waited = {}
            for o in self.ops[e]:
                need = {}
                for t in o.waits:
                    s, v = resolve(t)
                    k = id(s)
                    if waited.get(k, 0) >= v:
                        continue
                    if k not in need or need[k][1] < v:
                        need[k] = (s, v)
                for k, (s, v) in need.items():
                    h.wait_ge(s, v)
                    waited[k] = v
                if o.fn is None:
                    continue
                ins = o.fn(h)
                if o.dma_sem is not None:
                    ins.then_inc(o.dma_sem, 16)
                elif o.need_sig:
                    ins.then_inc(eng_sems[e][o.sigidx // SEM_EPOCH], 1)

        with nc.Block() as block:
            @block.tensor
            def _(h):
                run("pe", h)

            @block.scalar
            def _(h):
                run("act", h)

            @block.vector
            def _(h):
                run("dve", h)

            @block.gpsimd
            def _(h):
                run("pool", h)

            @block.sync
            def _(h):
                run("sp", h)


def MM(out, lhsT, rhs, start=True, stop=True):
    return lambda e: e.matmul(out, lhsT=lhsT, rhs=rhs, start=start, stop=stop)


def TR(out, in_, ident):
    return lambda e: e.transpose(out=out, in_=in_, identity=ident)


def ACTF(out, in_, func, scale=None, bias=None, accum=None):
    kw = {}
    if scale is not None:
        kw["scale"] = scale
    if bias is not None:
        kw["bias"] = bias
    if accum is not None:
        kw["accum_out"] = accum
    return lambda e: e.activation(out=out, in_=in_, func=func, **kw)


def TS(out, in0, s1, s2=None, op0=ALU.mult, op1=None):
    if op1 is None:
        return lambda e: e.tensor_scalar(out=out, in0=in0, scalar1=s1, scalar2=None, op0=op0)
    return lambda e: e.tensor_scalar(out=out, in0=in0, scalar1=s1, scalar2=s2, op0=op0, op1=op1)


def TT(out, in0, in1, op):
    return lambda e: e.tensor_tensor(out=out, in0=in0, in1=in1, op=op)


def STT(out, in0, scalar, in1, op0, op1):
    return lambda e: e.scalar_tensor_tensor(out=out, in0=in0, scalar=scalar, in1=in1, op0=op0, op1=op1)


def CP(out, in_):
    return lambda e: e.tensor_copy(out=out, in_=in_)


def MSET(ap, v):
    return lambda e: e.memset(ap, v)


def RCP(out, in_):
    return lambda e: e.reciprocal(out=out, in_=in_)


D = 2048
KC = 16
TOK = 2048
NTILE = 32
NBLK = 8
DFF = 5632
NFF = 44
N_IN = 6152
EPS = 1e-6
LAMBDA_INIT = 0.8 - 0.6 * math.exp(-0.3 * 0)
ML_W = 770
DA_W = 768
ARENA_BYTES = 211968

_CST = [("c_l", 16), ("b_l", 96), ("n1w", 16), ("n2w", 16), ("cw", 32), ("cb", 8), ("gb", 8), ("nw", 1024),
        ("wqk", 512), ("lamv", 256), ("subw", 1), ("flag", 1), ("pbias", 1), ("invf", 8), ("ident", 128),
        ("tri", 128)]
CO = {}
_o = 0
for _n, _s in _CST:
    CO[_n] = (_o, _s)
    _o += _s
NCST = _o


class Arena:
    def __init__(self, nc):
        self.t = nc.alloc_sbuf_tensor("arena", [128, ARENA_BYTES // 4], F32)
        self.off = 0

    def alloc(self, shape, dtype, parts=128, p0=0):
        n = int(np.prod(shape))
        esz = 2 if dtype == BF16 else 4
        nb = (n * esz + 31) // 32 * 32
        o = self.off
        self.off += nb
        assert self.off <= ARENA_BYTES, f"SBUF arena overflow {self.off}"
        v = self.t[p0:p0 + parts, o // 4:(o + nb) // 4]
        if dtype != F32:
            v = v.bitcast(dtype)
        v = v[:, 0:n]
        if len(shape) > 1:
            names = [f"a{i}" for i in range(len(shape))]
            kw = {nm: int(s) for nm, s in zip(names, shape)}
            v = v.rearrange("p (" + " ".join(names) + ") -> p " + " ".join(names), **kw)
        return v


def build(dbg=False, phases="0ABC"):
    nc = bass.Bass("TRN2", target_bir_lowering=False)
    P = Prog(nc)
    A = Arena(nc)

    def din(name, shape, dt=F32):
        return nc.dram_tensor(name, list(shape), dt, kind="ExternalInput").ap()

    def dscr(name, shape, dt):
        if dbg:
            return nc.dram_tensor(name, list(shape), dt, kind="ExternalOutput").ap()
        return nc.dram_tensor(name, list(shape), dt).ap()

    x_d = din("x", [2 * TOK, D])
    pos_d = din("pos", [128, NTILE], I32)
    cst_d = din("cst", [128, NCST])
    wada_d = din("w_ada_l", [24, 128, KC, 512])
    win_d = din("w_in_l", [128, KC, N_IN])
    wo_d = din("wo_l", [16, 128, KC, 128])
    wgu_d = din("wgu_l", [NFF, 128, KC, 256])
    wd_d = din("wd_l", [16, 128, NFF, 128])
    out_d = nc.dram_tensor("out", [TOK, D], F32, kind="ExternalOutput").ap()
    hs_d = dscr("hs", [NBLK, 128, KC, 512], BF16)
    ms_d = dscr("ms", [16, 128, TOK], BF16)
    modd = dscr("modd", [1, 6 * D], F32)

    ps = nc.alloc_psum_tensor("ps", [128, 8, 512], F32)
    bps = [P.buf(f"ps{i}", excl=True) for i in range(8)]

    def psbf(bank):
        return ps[:, bank, :].bitcast(BF16).rearrange("p (a b) -> p a b", a=8)

    CST = A.alloc([NCST], F32)
    bC = P.buf("cst")
    bK = P.buf("derived")
    P.dma("sp", CST, cst_d, writes=[bC])

    def cs(name):
        o, n = CO[name]
        return CST[:, o:o + n]

    identf = cs("ident")
    tri = cs("tri")
    flag = cs("flag")
    pbias = cs("pbias")
    posi = A.alloc([NTILE], I32)
    P.dma("sp", posi, pos_d, writes=[bK])
    identb = A.alloc([128], BF16)
    onesf = A.alloc([128], F32)
    onesb = A.alloc([128], BF16)
    mod = A.alloc([96], F32)
    A1 = A.alloc([16], F32)
    A2 = A.alloc([16], F32)
    Sh1 = mod[:, 0:16]
    Sh2 = mod[:, 48:64]
    gate1 = mod[:, 32:48]
    gate2 = mod[:, 80:96]
    cosT = A.alloc([NTILE, 8], F32)
    sinT = A.alloc([NTILE, 8], F32)
    stats = A.alloc([128], F32)
    neglam = A.alloc([1], F32)
    subw_s = A.alloc([1], F32)
    sc = A.alloc([16], F32)
    P.op("dve", CP(identb, identf), reads=[bC], writes=[bK])
    P.op("dve", MSET(onesf, 1.0), writes=[bK])
    P.op("dve", MSET(onesb, 1.0), writes=[bK])
    P.op("dve", MSET(stats, 0.0), writes=[bK])
    rp = A.off
    posf = A.alloc([NTILE], F32)
    ang = A.alloc([NTILE, 8], F32)
    ang2 = A.alloc([NTILE, 8], F32)
    ki = A.alloc([NTILE, 8], I32)
    kf = A.alloc([NTILE, 8], F32)
    P.op("dve", CP(posf, posi), reads=[bK], writes=[bK])
    P.op("dve", TT(ang, posf[:, :, None].to_broadcast([128, NTILE, 8]),
                   cs("invf")[:, None, :].to_broadcast([128, NTILE, 8]), ALU.mult), reads=[bK, bC], writes=[bK])
    for (dst, shift) in ((sinT, 0.0), (cosT, math.pi / 2)):
        P.op("dve", TS(ang2, ang, shift, None, ALU.add), reads=[bK], writes=[bK])
        P.op("dve", TS(ki, ang2, 1.0 / (2 * math.pi), None, ALU.mult), reads=[bK], writes=[bK])
        P.op("dve", CP(kf, ki), reads=[bK], writes=[bK])
        P.op("dve", STT(ang2, kf, -2 * math.pi, ang2, ALU.mult, ALU.add), reads=[bK], writes=[bK])
        P.op("dve", TS(ang2, ang2, -3.1415925, 3.1415925, ALU.max, ALU.min), reads=[bK], writes=[bK])
        P.op("act", ACTF(dst, ang2, AF.Sin), reads=[bK], writes=[bK])
    lamv = cs("lamv").rearrange("p (a b) -> p a b", a=4)
    lt = A.alloc([2, 64], F32)
    ls = A.alloc([2], F32)
    P.op("dve", TT(lt[:, 0, :], lamv[:, 0, :], lamv[:, 1, :], ALU.mult), reads=[bC], writes=[bK])
    P.op("dve", TT(lt[:, 1, :], lamv[:, 2, :], lamv[:, 3, :], ALU.mult), reads=[bC, bK], writes=[bK])
    P.op("dve", lambda e: e.tensor_reduce(out=ls, in_=lt, axis=AX.X, op=ALU.add), reads=[bK], writes=[bK])
    P.op("act", ACTF(ls, ls, AF.Exp), reads=[bK], writes=[bK])
    P.op("dve", STT(neglam, ls[:, 1:2], -LAMBDA_INIT, ls[:, 0:1], ALU.add, ALU.subtract), reads=[bK], writes=[bK])
    P.op("dve", TS(subw_s, cs("subw"), 1.0 - LAMBDA_INIT, None, ALU.mult), reads=[bC], writes=[bK])
    P.op("act", ACTF(sc, cs("c_l"), AF.Silu), reads=[bC], writes=[bK])
    base_off = A.off

    norm_uid = [0]

    def norm_s1(xsrc, bx, xn, bxn, junk, bjunk):
        uid = norm_uid[0]
        norm_uid[0] += 1
        st = stats[:, 2 * uid:2 * uid + 2]
        bst = P.buf(f"st{uid}")
        k = uid % 2
        P.op("act", ACTF(junk, xsrc, AF.Square, accum=st[:, 0:1]), reads=[bx, bK], writes=[bjunk, bst], relaxed=True)
        P.op("act", ACTF(st[:, 1:2], st[:, 0:1], AF.Ln, scale=1.0 / D, bias=EPS), reads=[bst], writes=[bst])
        P.op("act", ACTF(st[:, 1:2], st[:, 1:2], AF.Exp, scale=-0.5), reads=[bst], writes=[bst])
        P.op("dve", TS(xn[k], xsrc, st[:, 1:2], None, ALU.mult), reads=[bx, bst], writes=[bxn[k]])
        return k

    def norm_s2(k, Av, Shv, dst_of_kc, bdst_act, bdst_dve, xn, bxn):
        for hf in range(2):
            bank = 4 + 2 * k + hf
            pv = psbf(bank)
            for q in range(8):
                kc = hf * 8 + q
                P.op("pe", TR(pv[:, q, :], xn[k][:, kc * 128:(kc + 1) * 128], identb), reads=[bxn[k], bK], writes=[bps[bank]])
            for q in range(8):
                kc = hf * 8 + q
                if hf == 0 and q < 6:
                    P.op("act", ACTF(dst_of_kc(kc), pv[:, q, :], AF.Identity, scale=Av[:, kc:kc + 1], bias=Shv[:, kc:kc + 1]),
                         reads=[bps[bank], bK], writes=[bdst_act], relaxed=True)
                else:
                    P.op("dve", TS(dst_of_kc(kc), pv[:, q, :], Av[:, kc:kc + 1], Shv[:, kc:kc + 1], ALU.mult, ALU.add),
                         reads=[bps[bank], bK], writes=[bdst_dve], relaxed=True)

    def norm_pipeline(tiles, xn, bxn, junk, bjunk, after=None):
        ks = {}
        ks[0] = norm_s1(tiles[0][0], tiles[0][1], xn, bxn, junk, bjunk)
        for t, tl in enumerate(tiles):
            if t + 1 < len(tiles):
                ks[t + 1] = norm_s1(tiles[t + 1][0], tiles[t + 1][1], xn, bxn, junk, bjunk)
            norm_s2(ks[t], tl[2], tl[3], tl[4], tl[5], tl[6], xn, bxn)
            if after is not None:
                after(t)

    xv = x_d.rearrange("(t p) d -> p t d", p=128)
    bmd = P.buf("modd")
    bhs1 = P.buf("hs")
    bhs = [bhs1] * NBLK

    if "A" in phases:
        mrow = [A.alloc([512], F32, parts=1) for _ in range(2)]
        bmr = [P.buf(f"mrow{i}") for i in range(2)]
        wab = [A.alloc([KC, 512], F32) for _ in range(2)]
        bwa = [P.buf(f"wab{i}") for i in range(2)]
        modT = A.alloc([128], F32, parts=96)
        bmT = P.buf("modT")

        def ada_cols(cb):
            i = cb % 2
            P.dma("sp", wab[i], wada_d[cb], writes=[bwa[i]])
            for kc in range(KC):
                P.op("pe", MM(ps[0:1, i, :], sc[:, kc:kc + 1], wab[i][:, kc, :], kc == 0, kc == KC - 1),
                     reads=[bwa[i], bK], writes=[bps[i]])
            P.op("act", lambda e, i=i: e.copy(out=mrow[i], in_=ps[0:1, i, :]), reads=[bps[i]], writes=[bmr[i]])
            P.dma("pool", modd[:, cb * 512:(cb + 1) * 512], mrow[i], reads=[bmr[i]], writes=[bmd])

        def ada_finish(j0, j1):
            n = j1 - j0
            P.dma("sp", modT[0:n, :], modd[:, j0 * 128:j1 * 128].rearrange("o (j p) -> (o j) p", p=128), reads=[bmd], writes=[bmT])
            P.op("pe", TR(ps[:, 2, 0:n], modT[0:n, :], identf[0:n, 0:n]), reads=[bmT, bC], writes=[bps[2]])
            P.op("dve", TT(mod[:, j0:j1], ps[:, 2, 0:n], cs("b_l")[:, j0:j1], ALU.add), reads=[bps[2], bC], writes=[bK])

        for cb in range(8):
            ada_cols(cb)
        ada_finish(0, 32)
        P.op("dve", STT(A1, mod[:, 16:32], 1.0, cs("n1w"), ALU.add, ALU.mult), reads=[bK, bC], writes=[bK])

        xb = [A.alloc([4, D], F32) for _ in range(2)]
        bxb = [P.buf(f"xb{i}") for i in range(2)]
        xn = [A.alloc([D], BF16) for _ in range(2)]
        bxn = [P.buf(f"xn{i}") for i in range(2)]
        junk = A.alloc([D], BF16)
        bjunk = P.buf("junk")
        hTb = [A.alloc([KC, 512], BF16) for _ in range(2)]
        bhTa = [P.buf(f"hTa{i}") for i in range(2)]
        bhTd = [P.buf(f"hTd{i}") for i in range(2)]
        P.dma("sp", xb[0], xv[:, 0:4, :], writes=[bxb[0]])
        P.dma("sp", xb[1], xv[:, 4:8, :], writes=[bxb[1]])
        tiles = []
        for blk in range(NBLK):
            i = blk % 2
            for j in range(4):
                tiles.append((xb[i][:, j, :], bxb[i], A1, Sh1, (lambda kc, i=i, j=j: hTb[i][:, kc, j * 128:(j + 1) * 128]), bhTa[i], bhTd[i]))

        def after_tile(t):
            blk, j = divmod(t, 4)
            i = blk % 2
            if j == 3:
                P.dma("pool", hs_d[blk], hTb[i], reads=[bhTa[i], bhTd[i]], writes=[bhs[blk]])
                if blk + 2 < NBLK:
                    P.dma("sp", xb[i], xv[:, 4 * blk + 8:4 * blk + 12, :], writes=[bxb[i]])

        norm_pipeline(tiles, xn, bxn, junk, bjunk, after=after_tile)
        P.barrier()
    A.off = base_off

    bms = P.buf("ms")

    if "B" in phases:
        wbuf = [A.alloc([KC, ML_W], BF16) for _ in range(2)]
        bwb = [P.buf(f"wb{i}") for i in range(2)]
        hbuf = [A.alloc([KC, 512], BF16) for _ in range(2)]
        bhb = [P.buf(f"hb{i}") for i in range(2)]
        mob = [A.alloc([2, 512], BF16) for _ in range(2)]
        bmoa = [P.buf(f"moa{i}") for i in range(2)]
        xc = [A.alloc([515], F32) for _ in range(2)]
        bxc = [P.buf(f"xc{i}") for i in range(2)]
        ycv = [A.alloc([512], F32) for _ in range(2)]
        bycv = [P.buf(f"ycv{i}") for i in range(2)]
        sgm = [A.alloc([512], F32) for _ in range(2)]
        bsgm = [P.buf(f"sgm{i}") for i in range(2)]
        qkT = [[A.alloc([512], BF16) for _ in range(2)] for _ in range(2)]
        bqkT = [[P.buf(f"qkT{p}_{i}") for i in range(2)] for p in range(2)]
        Gp = [A.alloc([96], F32) for _ in range(2)]
        bGp = [P.buf(f"G{p}") for p in range(2)]
        vaug = [A.alloc([257], BF16) for _ in range(2)]
        bva = [P.buf(f"vaug{i}") for i in range(2)]
        vw = [A.alloc([257], BF16) for _ in range(2)]
        bvw = [P.buf(f"vw{i}") for i in range(2)]
        ktm = [A.alloc([128], BF16) for _ in range(2)]
        bktm = [P.buf(f"ktm{i}") for i in range(2)]
        ATs = [A.alloc([128], BF16) for _ in range(2)]
        bATs = [P.buf(f"ATs{i}") for i in range(2)]
        C32 = A.alloc([257], F32)
        bC32 = P.buf("C32")
        Cb = [A.alloc([257], BF16) for _ in range(2)]
        bCb = [P.buf(f"Cb{i}") for i in range(2)]
        dds = A.alloc([64 * 8], F32)
        junk2 = A.alloc([256], BF16)
        bjunk2 = P.buf("junk2")
        hn = [A.alloc([256], F32) for _ in range(2)]
        bhn = [P.buf(f"hn{i}") for i in range(2)]
        sig = [A.alloc([256], F32) for _ in range(2)]
        bsig = [P.buf(f"sig{i}") for i in range(2)]
        hm = [A.alloc([256], BF16) for _ in range(2)]
        bhm = [P.buf(f"hm{i}") for i in range(2)]
        KT = [A.alloc([NTILE * 128], BF16) for _ in range(2)]
        bKT = [P.buf(f"KT{i}") for i in range(2)]
        Vh = A.alloc([2, NTILE, 128], BF16)
        bVh = P.buf("Vh")
        QT = [A.alloc([512], BF16) for _ in range(2)]
        bQT = [P.buf(f"QT{i}") for i in range(2)]
        sqb = [A.alloc([8, 64], F32) for _ in range(2)]
        bsqb = [P.buf(f"sqb{i}") for i in range(2)]
        rs = [A.alloc([8], F32) for _ in range(2)]
        brs = [P.buf(f"rs{i}") for i in range(2)]
        t1 = [A.alloc([8, 64], F32) for _ in range(2)]
        bt1 = [P.buf(f"t1{i}") for i in range(2)]
        qkb = [A.alloc([8, 64], BF16) for _ in range(2)]
        bqkb = [P.buf(f"qkb{i}") for i in range(2)]
        rt = [A.alloc([4, 8, 8], F32) for _ in range(2)]
        brt = [P.buf(f"rt{i}") for i in range(2)]
        PT = [A.alloc([2, 512], BF16) for _ in range(3)]
        bPT = [P.buf(f"PT{i}") for i in range(3)]
        Pacc = [A.alloc([512], F32) for _ in range(2)]
        bPacc = [P.buf(f"Pacc{i}") for i in range(2)]
        fz = [A.alloc([512], F32) for _ in range(5)]
        bfz = [P.buf(f"fz{i}") for i in range(5)]
        wbg = A.alloc([KC, 256], BF16)
        bwbg = P.buf("wbg")
        mrb = A.alloc([256], F32, parts=1)
        bmrb = P.buf("mrb")
        scb = A.alloc([16], BF16)
        modT2 = A.alloc([128], F32, parts=64)
        bmT2 = P.buf("modT2")
        P.op("dve", CP(scb, sc), reads=[bK], writes=[bK])
        wada_v = wada_d.rearrange("c p k (h n) -> c h p k n", h=2)

        def ada_bg():
            for c in range(8, 24):
                for hf in range(2):
                    P.dma("pool", wbg, wada_v[c, hf], writes=[bwbg])
                    yield
                    for kc in range(KC):
                        P.op("pe", MM(ps[0:1, 2, 0:256], scb[:, kc:kc + 1], wbg[:, kc, :], kc == 0, kc == KC - 1),
                             reads=[bwbg, bK], writes=[bps[2]])
                    P.op("act", lambda e: e.copy(out=mrb, in_=ps[0:1, 2, 0:256]), reads=[bps[2]], writes=[bmrb])
                    col = c * 512 + hf * 256
                    P.dma("sp", modd[:, col:col + 256], mrb, reads=[bmrb], writes=[bmd])
                    yield

        g_ada = ada_bg()

        def ada_step(n=2):
            for _ in range(n):
                try:
                    next(g_ada)
                except StopIteration:
                    return

        def interleave(g1, g2):
            gens = [g for g in (g1, g2) if g is not None]
            while gens:
                for g in list(gens):
                    try:
                        next(g)
                    except StopIteration:
                        gens.remove(g)

        P.op("dve", MSET(dds, 0.0), writes=[bK])
        for i in range(2):
            P.op("dve", MSET(vaug[i][:, 256:257], 1.0), writes=[bva[i]])

        wml = cs("nw").rearrange("p (h v) -> p h v", h=4)
        cw = cs("cw").rearrange("p (h q t) -> p h q t", h=4, q=2)
        cbv = cs("cb").rearrange("p (h q) -> p h q", h=4)
        gbv = cs("gb").rearrange("p (h g) -> p h g", h=4)
        wqk = cs("wqk").rearrange("p (g d) -> p g d", g=8)
        LNS = math.log(128.0 ** -0.5)

        npass = 8
        pass_cols = [(hh * ML_W, ML_W) for hh in range(4)] + [(4 * ML_W + hp * DA_W, DA_W) for hp in range(4)]

        def load_w(pi):
            o, n = pass_cols[pi]
            P.dma("pool", wbuf[pi % 2][:, :, 0:n], win_d[:, :, o:o + n], writes=[bwb[pi % 2]])

        hcount = [0]

        def load_h(blk):
            i = hcount[0] % 2
            hcount[0] += 1
            P.dma("sp", hbuf[i], hs_d[blk], reads=[bhs[blk]], writes=[bhb[i]])
            return i

        load_w(0)
        cur_tile_head = [0]
        mo_cnt = [0]
        KB = os.environ.get("KDBG_B", "")
        for pi in range(npass):
            if pi + 1 < npass:
                load_w(pi + 1)
            if (KB == "ml" and pi >= 4) or (KB == "da" and pi < 4) or (KB == "ml1" and pi >= 1) or (KB == "da1" and pi != 4):
                continue
            W = wbuf[pi % 2]
            bW = bwb[pi % 2]
            nxt = load_h(0)
            if pi < 4:
                hh = pi
                P.op("dve", MSET(C32, 0.0), writes=[bC32])
                hmap = {}

                def ml_prep(blk, hi):
                    H = hbuf[hi]
                    bH = bhb[hi]
                    cur = blk >= 4
                    pb = blk % 2
                    for qk in (1, 0):
                        if qk == 0 and blk < 3:
                            continue
                        for kc in range(KC):
                            P.op("pe", MM(ps[:, qk, :], W[:, kc, qk * 128:(qk + 1) * 128], H[:, kc, :], kc == 0, kc == KC - 1),
                                 reads=[bW, bH], writes=[bps[qk]])
                        yield
                        first = (blk == 0) if qk == 1 else (blk == 3)
                        if first:
                            P.op("dve", MSET(xc[qk][:, 0:3], 0.0), writes=[bxc[qk]])
                        elif blk == 4:
                            P.op("dve", TS(xc[qk][:, 0:3], xc[qk][:, 512:515], flag[:, 0:1], None, ALU.mult),
                                 reads=[bxc[qk], bC], writes=[bxc[qk]])
                        else:
                            P.op("dve", CP(xc[qk][:, 0:3], xc[qk][:, 512:515]), reads=[bxc[qk]], writes=[bxc[qk]])
                        P.op("act", lambda e, qk=qk: e.copy(out=xc[qk][:, 3:515], in_=ps[:, qk, :]), reads=[bps[qk]], writes=[bxc[qk]])
                        yield
                        if qk == 0 and blk < 4:
                            continue
                        P.op("dve", TS(ycv[qk], xc[qk][:, 0:512], cw[:, hh, qk, 0:1], cbv[:, hh, qk:qk + 1], ALU.mult, ALU.add),
                             reads=[bxc[qk], bC], writes=[bycv[qk]])
                        yield
                        for tp in range(1, 4):
                            P.op("dve", STT(ycv[qk], xc[qk][:, tp:tp + 512], cw[:, hh, qk, tp:tp + 1], ycv[qk], ALU.mult, ALU.add),
                                 reads=[bxc[qk], bC, bycv[qk]], writes=[bycv[qk]])
                            yield
                        P.op("act", ACTF(sgm[qk], ycv[qk], AF.Exp, scale=-1.0), reads=[bycv[qk]], writes=[bsgm[qk]])
                        P.op("act", ACTF(sgm[qk], sgm[qk], AF.Ln, bias=1.0), reads=[bsgm[qk]], writes=[bsgm[qk]])
                        yield
                        P.op("act", ACTF(sgm[qk], sgm[qk], AF.Exp, scale=-1.0), reads=[bsgm[qk]], writes=[bsgm[qk]])
                        P.op("dve", TT(qkT[pb][qk], ycv[qk], sgm[qk], ALU.mult), reads=[bycv[qk], bsgm[qk]], writes=[bqkT[pb][qk]])
                        yield
                    for j in range(4):
                        for kc in range(KC):
                            P.op("pe", MM(ps[:, 5, 384 + 2 * j:386 + 2 * j], H[:, kc, j * 128:(j + 1) * 128], W[:, kc, 768:770], kc == 0, kc == KC - 1),
                                 reads=[bW, bH], writes=[bps[5]])
                        yield
                    G = Gp[pb]
                    bG = bGp[pb]
                    gsb = G[:, 0:8].rearrange("p (j g) -> p j g", j=4)
                    th = G[:, 8:16].rearrange("p (j g) -> p j g", j=4)
                    e1, l1, li, nBe, tq, argw, argf = (G[:, 16 + 4 * i:20 + 4 * i] for i in range(7))
                    wv, fpv, eB = (G[:, 48 + 4 * i:52 + 4 * i] for i in range(3))
                    P.op("dve", TT(gsb, ps[:, 5, 384:392].rearrange("p (j g) -> p j g", j=4),
                                   gbv[:, hh:hh + 1, :].to_broadcast([128, 4, 2]), ALU.add), reads=[bps[5], bC], writes=[bG])
                    P.op("act", ACTF(th, gsb, AF.Exp, scale=2.0 / 15.0), reads=[bG], writes=[bG])
                    yield
                    P.op("act", ACTF(th, th, AF.Ln, bias=1.0), reads=[bG], writes=[bG])
                    P.op("act", ACTF(th, th, AF.Exp, scale=-1.0), reads=[bG], writes=[bG])
                    yield
                    P.op("dve", TS(th, th, -2.0, 1.0, ALU.mult, ALU.add), reads=[bG], writes=[bG])
                    P.op("act", ACTF(e1, th[:, :, 1], AF.Exp, scale=-15.0), reads=[bG], writes=[bG])
                    yield
                    P.op("act", ACTF(l1, e1, AF.Ln, bias=1.0), reads=[bG], writes=[bG])
                    P.op("dve", TS(li, th[:, :, 0], 15.0, None, ALU.mult), reads=[bG], writes=[bG])
                    yield
                    P.op("pe", MM(ps[:, 5, 400:404], tri, l1), reads=[bG, bC], writes=[bps[5]])
                    P.op("pe", MM(ps[:, 5, 416:420], onesf, l1), reads=[bG, bK], writes=[bps[5]])
                    yield
                    P.op("dve", CP(nBe, ps[:, 5, 416:420]), reads=[bps[5]], writes=[bG])
                    P.op("dve", TT(tq, li, nBe, ALU.subtract), reads=[bG], writes=[bG])
                    yield
                    P.op("dve", TT(argw, tq, ps[:, 5, 400:404], ALU.add), reads=[bG, bps[5]], writes=[bG])
                    P.op("dve", TT(argf, nBe, ps[:, 5, 400:404], ALU.subtract), reads=[bG, bps[5]], writes=[bG])
                    yield
                    P.op("act", ACTF(wv, argw, AF.Exp), reads=[bG], writes=[bG])
                    P.op("act", ACTF(fpv, argf, AF.Exp, bias=LNS), reads=[bG], writes=[bG])
                    P.op("act", ACTF(eB, nBe, AF.Exp, scale=-1.0), reads=[bG], writes=[bG])
                    yield

                def ml_tiles(blk, hi):
                    H = hbuf[hi]
                    bH = bhb[hi]
                    cur = blk >= 4
                    pb = blk % 2
                    G = Gp[pb]
                    bG = bGp[pb]
                    wv, fpv, eB = (G[:, 48 + 4 * i:52 + 4 * i] for i in range(3))
                    qT = qkT[pb][0]
                    kT = qkT[pb][1]
                    bqT = bqkT[pb][0]
                    bkT = bqkT[pb][1]
                    nv = 512 if cur else 256
                    mi = mo_cnt[0] % 2

                    def emit_vo(j):
                        vb_ = 2 + (j % 2)
                        for kc in range(KC):
                            P.op("pe", MM(ps[:, vb_, 0:nv], H[:, kc, j * 128:(j + 1) * 128], W[:, kc, 256:256 + nv], kc == 0, kc == KC - 1),
                                 reads=[bW, bH], writes=[bps[vb_]])

                    def tile(j):
                        t = blk * 4 + j
                        vb = 2 + (j % 2)
                        nb = 6 + (j % 2)
                        a = t % 2
                        P.op("dve", TS(vw[a][:, 0:256], ps[:, vb, 0:256], wv[:, j:j + 1], None, ALU.mult), reads=[bps[vb], bG], writes=[bvw[a]])
                        P.op("dve", CP(vw[a][:, 256:257], wv[:, j:j + 1]), reads=[bG], writes=[bvw[a]])
                        P.op("pe", TR(psbf(5)[:, 2 + a, :], kT[:, j * 128:(j + 1) * 128], identb), reads=[bkT, bK], writes=[bps[5]])
                        P.op("act", lambda e, a=a: e.copy(out=ktm[a], in_=psbf(5)[:, 2 + a, :]), reads=[bps[5]], writes=[bktm[a]])
                        yield
                        if cur:
                            u = cur_tile_head[0]
                            cur_tile_head[0] += 1
                            dd = dds[:, 8 * u:8 * u + 8]
                            bdd = P.buf(f"dd{u}")
                            P.op("act", lambda e, a=a, vb=vb: e.copy(out=vaug[a][:, 0:256], in_=ps[:, vb, 0:256]), reads=[bps[vb]], writes=[bva[a]])
                            P.op("act", ACTF(sig[a], ps[:, vb, 256:512], AF.Exp, scale=-1.0), reads=[bps[vb]], writes=[bsig[a]])
                            P.op("pe", MM(ps[:, 5, 0:128], kT[:, j * 128:(j + 1) * 128], qT[:, j * 128:(j + 1) * 128]),
                                 reads=[bqT, bkT], writes=[bps[5]])
                            P.op("dve", STT(ATs[a], ps[:, 5, 0:128], wv[:, j:j + 1], tri, ALU.mult, ALU.mult), reads=[bps[5], bG, bC], writes=[bATs[a]])
                            yield
                        if blk == 4 and j == 0:
                            P.op("dve", TS(C32, C32, flag[:, 0:1], None, ALU.mult), reads=[bC32, bC], writes=[bC32])
                        if cur:
                            P.op("dve", TS(Cb[a], C32, eB[:, j:j + 1], None, ALU.mult), reads=[bC32, bG], writes=[bCb[a]])
                            P.op("pe", MM(ps[:, nb, 0:257], ATs[a], vaug[a], True, False), reads=[bATs[a], bva[a]], writes=[bps[nb]])
                            P.op("pe", MM(ps[:, nb, 0:257], qT[:, j * 128:(j + 1) * 128], Cb[a], False, True), reads=[bqT, bCb[a]], writes=[bps[nb]])
                        P.op("pe", MM(ps[:, 4, 0:257], ktm[a], vw[a]), reads=[bktm[a], bvw[a]], writes=[bps[4]])
                        P.op("dve", STT(C32, C32, eB[:, j:j + 1], ps[:, 4, 0:257], ALU.mult, ALU.add), reads=[bC32, bG, bps[4]], writes=[bC32])
                        yield
                        if not cur:
                            return
                        P.op("dve", TT(dd[:, 0:1], ps[:, nb, 256:257], fpv[:, j:j + 1], ALU.mult), reads=[bps[nb], bG], writes=[bdd])
                        P.op("dve", TS(dd[:, 1:2], dd[:, 0:1], -1.0, 1.0, ALU.mult, ALU.max), reads=[bdd], writes=[bdd])
                        yield
                        P.op("dve", TT(dd[:, 1:2], dd[:, 1:2], dd[:, 0:1], ALU.max), reads=[bdd], writes=[bdd])
                        P.op("dve", RCP(dd[:, 1:2], dd[:, 1:2]), reads=[bdd], writes=[bdd])
                        yield
                        P.op("dve", TT(dd[:, 2:3], fpv[:, j:j + 1], dd[:, 1:2], ALU.mult), reads=[bdd, bG], writes=[bdd])
                        P.op("act", ACTF(junk2, ps[:, nb, 0:256], AF.Square, scale=dd[:, 2:3], accum=dd[:, 3:4]), reads=[bps[nb], bdd], writes=[bjunk2, bdd])
                        yield
                        P.op("act", ACTF(dd[:, 4:5], dd[:, 3:4], AF.Ln, scale=1.0 / 256.0, bias=EPS), reads=[bdd], writes=[bdd])
                        P.op("act", ACTF(dd[:, 4:5], dd[:, 4:5], AF.Exp, scale=-0.5), reads=[bdd], writes=[bdd])
                        yield
                        P.op("dve", TT(dd[:, 5:6], dd[:, 2:3], dd[:, 4:5], ALU.mult), reads=[bdd], writes=[bdd])
                        yield
                        P.op("dve", STT(hn[a], ps[:, nb, 0:256], dd[:, 5:6], wml[:, hh, :], ALU.mult, ALU.mult), reads=[bps[nb], bdd, bC], writes=[bhn[a]])
                        P.op("act", ACTF(sig[a], sig[a], AF.Ln, bias=1.0), reads=[bsig[a]], writes=[bsig[a]])
                        yield
                        P.op("act", ACTF(sig[a], sig[a], AF.Exp, scale=-1.0), reads=[bsig[a]], writes=[bsig[a]])
                        yield
                        P.op("dve", TT(hm[a], hn[a], sig[a], ALU.mult), reads=[bhn[a], bsig[a]], writes=[bhm[a]])
                        yield
                        for i2 in range(2):
                            P.op("pe", TR(psbf(5)[:, 4 + i2, :], hm[a][:, i2 * 128:(i2 + 1) * 128], identb), reads=[bhm[a], bK], writes=[bps[5]])
                        P.op("act", lambda e, mi=mi, j=j: e.copy(out=mob[mi][:, :, j * 128:(j + 1) * 128], in_=psbf(5)[:, 4:6, :]),
                             reads=[bps[5]], writes=[bmoa[mi]], relaxed=True)
                        yield

                    emit_vo(0)
                    yield
                    act_t = []
                    for j in range(4):
                        if j + 1 < 4:
                            emit_vo(j + 1)
                            yield
                        act_t.append(tile(j))
                        if len(act_t) == 2:
                            done0 = False
                            while not done0:
                                for g in list(act_t):
                                    try:
                                        next(g)
                                    except StopIteration:
                                        if g is act_t[0]:
                                            done0 = True
                                        act_t.remove(g)
                                yield
                        else:
                            for _ in range(3):
                                try:
                                    next(act_t[0])
                                except StopIteration:
                                    act_t.pop()
                                    break
                            yield
                    for g in act_t:
                        for _ in g:
                            yield
                    if cur:
                        cb4 = blk - 4
                        P.dma("sp", ms_d[2 * hh:2 * hh + 2, :, cb4 * 512:(cb4 + 1) * 512].rearrange("c p n -> p c n"), mob[mi],
                              reads=[bmoa[mi]], writes=[bms])
                        mo_cnt[0] += 1

                hmap[0] = nxt
                for _ in ml_prep(0, hmap[0]):
                    pass
                for blk in range(NBLK):
                    g2 = None
                    if blk + 1 < NBLK:
                        hmap[blk + 1] = load_h(blk + 1)
                        g2 = ml_prep(blk + 1, hmap[blk + 1])
                    interleave(ml_tiles(blk, hmap[blk]), g2)
            else:
                hp = pi - 4
                fin2q = []
                for blk in range(NBLK):
                    ada_step()
                    while fin2q:
                        fin2q.pop(0)()
                    hi = nxt
                    if blk + 1 < NBLK:
                        nxt = load_h(blk + 1)
                    H = hbuf[hi]
                    bH = bhb[hi]
                    cur = blk >= 4
                    c0 = 0 if cur else 128

                    def proj_mm(j):
                        b0 = 2 * (j % 2)
                        for hd in range(2):
                            for kc in range(KC):
                                P.op("pe", MM(ps[:, b0 + hd, c0:384], H[:, kc, j * 128:(j + 1) * 128], W[:, kc, hd * 384 + c0:hd * 384 + 384],
                                              kc == 0, kc == KC - 1), reads=[bW, bH], writes=[bps[b0 + hd]])

                    def tile_gen(j):
                        t = blk * 4 + j
                        b0 = 2 * (j % 2)
                        bb = [bps[b0], bps[b0 + 1]]
                        sl = t % 2
                        P.op("act", lambda e, t=t, b0=b0: e.copy(out=Vh[:, :, t, :], in_=ps[:, b0:b0 + 2, 256:384]), reads=bb, writes=[bVh], relaxed=True)
                        pq = ps[:, b0:b0 + 2, 0:256].rearrange("p h (g d) -> p h g d", g=4)
                        sq4 = sqb[sl].rearrange("p (h g) d -> p h g d", h=2)
                        t14 = t1[sl].rearrange("p (h g) d -> p h g d", h=2)
                        P.op("act", ACTF(sq4, pq, AF.Square), reads=bb, writes=[bsqb[sl]])
                        yield
                        P.op("dve", lambda e, sl=sl: e.tensor_reduce(out=rs[sl], in_=sqb[sl], axis=AX.X, op=ALU.add), reads=[bsqb[sl]], writes=[brs[sl]])
                        yield
                        P.op("act", ACTF(rs[sl], rs[sl], AF.Ln, scale=1.0 / 64.0, bias=EPS), reads=[brs[sl]], writes=[brs[sl]])
                        P.op("act", ACTF(rs[sl], rs[sl], AF.Exp, scale=-0.5), reads=[brs[sl]], writes=[brs[sl]])
                        yield
                        P.op("dve", TT(t14, pq, rs[sl].rearrange("p (h g) -> p h g", h=2)[:, :, :, None].to_broadcast([128, 2, 4, 64]), ALU.mult),
                             reads=bb + [brs[sl]], writes=[bt1[sl]])
                        yield
                        P.op("dve", TT(t1[sl], t1[sl], wqk, ALU.mult), reads=[bt1[sl], bC], writes=[bt1[sl]])
                        yield
                        P.op("act", lambda e, sl=sl: e.copy(out=qkb[sl], in_=t1[sl]), reads=[bt1[sl]], writes=[bqkb[sl]])
                        cosb = cosT[:, t:t + 1, :].to_broadcast([128, 8, 8])
                        sinb = sinT[:, t:t + 1, :].to_broadcast([128, 8, 8])
                        x1 = t1[sl][:, :, 0:8]
                        x2 = t1[sl][:, :, 8:16]
                        R_ = rt[sl]
                        P.op("dve", TT(R_[:, 0], x1, cosb, ALU.mult), reads=[bt1[sl], bK], writes=[brt[sl]])
                        P.op("dve", TT(R_[:, 1], x2, sinb, ALU.mult), reads=[bt1[sl], bK], writes=[brt[sl]], relaxed=True)
                        yield
                        P.op("dve", TT(R_[:, 2], x2, cosb, ALU.mult), reads=[bt1[sl], bK], writes=[brt[sl]], relaxed=True)
                        P.op("dve", TT(R_[:, 3], x1, sinb, ALU.mult), reads=[bt1[sl], bK], writes=[brt[sl]], relaxed=True)
                        yield
                        P.op("dve", TT(qkb[sl][:, :, 0:8], R_[:, 0], R_[:, 1], ALU.subtract), reads=[brt[sl]], writes=[bqkb[sl]])
                        P.op("dve", TT(qkb[sl][:, :, 8:16], R_[:, 2], R_[:, 3], ALU.add), reads=[brt[sl]], writes=[bqkb[sl]], relaxed=True)
                        yield
                        qk2 = qkb[sl].rearrange("p g d -> p (g d)")
                        for hd in range(2):
                            P.op("pe", TR(psbf(4)[:, 2 * hd, :], qk2[:, hd * 256 + 128:hd * 256 + 256], identb), reads=[bqkb[sl], bK], writes=[bps[4]])
                            if cur:
                                P.op("pe", TR(psbf(4)[:, 2 * hd + 1, :], qk2[:, hd * 256:hd * 256 + 128], identb), reads=[bqkb[sl], bK], writes=[bps[4]])
                        yield
                        for hd in range(2):
                            P.op("act", lambda e, hd=hd, t=t: e.copy(out=KT[hd][:, t * 128:(t + 1) * 128], in_=psbf(4)[:, 2 * hd, :]),
                                 reads=[bps[4]], writes=[bKT[hd]], relaxed=True)
                            if cur:
                                P.op("act", lambda e, hd=hd, j=j: e.copy(out=QT[hd][:, j * 128:(j + 1) * 128], in_=psbf(4)[:, 2 * hd + 1, :]),
                                     reads=[bps[4]], writes=[bQT[hd]], relaxed=True)
                        yield

                    proj_mm(0)
                    act_g = []
                    for j in range(4):
                        if j + 1 < 4:
                            proj_mm(j + 1)
                        act_g.append(tile_gen(j))
                        if len(act_g) == 2:
                            done0 = False
                            nstep = 0
                            while not done0:
                                for g in list(act_g):
                                    try:
                                        next(g)
                                    except StopIteration:
                                        if g is act_g[0]:
                                            done0 = True
                                        act_g.remove(g)
                                nstep += 1
                        elif j == 0:
                            for _ in range(4):
                                next(act_g[0])
                    for g in act_g:
                        for _ in g:
                            pass
                    if not cur:
                        continue
                    cb4 = blk - 4
                    ndone = 16 + 4 * cb4
                    for hd in range(2):
                        steps = [(c, 0, None) for c in range(ndone)] + [(ndone + jd, jd * 128, jd) for jd in (3, 2, 1, 0)]
                        nst = len(steps)

                        def emit_S(n):
                            c, q0, jd = steps[n]
                            sb0 = 4 + 2 * (n % 2)
                            for m in range(2):
                                P.op("pe", MM(ps[:, sb0 + m, q0:512], KT[hd][m * 64:(m + 1) * 64, c * 128:(c + 1) * 128], QT[hd][m * 64:(m + 1) * 64, q0:512]),
                                     reads=[bKT[hd], bQT[hd]], writes=[bps[sb0 + m]])

                        def emit_E(n):
                            c, q0, jd = steps[n]
                            sb0 = 4 + 2 * (n % 2)
                            pt = n % 3
                            P.op("act", ACTF(PT[pt][:, :, q0:512], ps[:, sb0:sb0 + 2, q0:512], AF.Exp, scale=0.125, bias=(pbias[:, 0:1] if c < 16 else None)),
                                 reads=[bps[sb0], bps[sb0 + 1], bC], writes=[bPT[pt]])
                            if jd is not None:
                                P.op("dve", MSET(PT[pt][64:128, :, q0:q0 + 64], 0.0), reads=[bPT[pt]], writes=[bPT[pt]])

                        def emit_PV(n):
                            c, q0, jd = steps[n]
                            pt = n % 3
                            for m in range(2):
                                P.op("pe", MM(ps[:, m, q0:512], Vh[:, hd, c, :], PT[pt][:, m, q0:512], c == 0, jd == 0),
                                     reads=[bVh, bPT[pt]], writes=[bps[m]])
                            if c == 0:
                                P.op("dve", CP(Pacc[0], PT[pt][:, 0, :]), reads=[bPT[pt]], writes=[bPacc[0]])
                            else:
                                P.op("dve", TT(Pacc[0][:, q0:512], Pacc[0][:, q0:512], PT[pt][:, 0, q0:512], ALU.add), reads=[bPT[pt], bPacc[0]], writes=[bPacc[0]])
                            P.op("pe", MM(ps[:, 3, q0:512], onesb, PT[pt][:, 1, q0:512], c == 0, jd == 0), reads=[bPT[pt], bK], writes=[bps[3]])

                        emit_S(0)
                        emit_S(1)
                        for n in range(nst):
                            emit_E(n)
                            if n + 2 < nst:
                                emit_S(n + 2)
                            emit_PV(n)
                            if fin2q and n >= 2:
                                fin2q.pop(0)()
                        while fin2q:
                            fin2q.pop(0)()
                        P.op("pe", MM(ps[:, 2, :], onesf, Pacc[0]), reads=[bPacc[0], bK], writes=[bps[2]])
                        P.op("dve", RCP(fz[0], ps[:, 2, :]), reads=[bps[2]], writes=[bfz[0]])
                        P.op("dve", TT(fz[1], ps[:, 0, :], fz[0], ALU.mult), reads=[bps[0], bfz[0]], writes=[bfz[1]])
                        P.op("dve", RCP(fz[2], ps[:, 3, :]), reads=[bps[3]], writes=[bfz[2]])
                        P.op("dve", TT(fz[3], ps[:, 1, :], fz[2], ALU.mult), reads=[bps[1], bfz[2]], writes=[bfz[3]])

                        def f2a():
                            P.op("dve", STT(fz[1], fz[3], neglam[:, 0:1], fz[1], ALU.mult, ALU.add), reads=[bfz[3], bfz[1], bK], writes=[bfz[1]])

                        def f2b():
                            P.op("act", ACTF(fz[4], fz[1], AF.Square), reads=[bfz[1]], writes=[bfz[4]])

                        def f2c():
                            P.op("pe", MM(ps[:, 2, :], onesf, fz[4]), reads=[bfz[4], bK], writes=[bps[2]])

                        def f2d():
                            P.op("act", ACTF(fz[0], ps[:, 2, :], AF.Ln, scale=1.0 / 128.0, bias=EPS), reads=[bps[2]], writes=[bfz[0]])

                        def f2e():
                            P.op("act", ACTF(fz[0], fz[0], AF.Exp, scale=-0.5), reads=[bfz[0]], writes=[bfz[0]])

                        def f2f(hd=hd, cb4=cb4):
                            mi = mo_cnt[0] % 2
                            mo_cnt[0] += 1
                            P.op("dve", STT(mob[mi][:, 0, :], fz[1], subw_s[:, 0:1], fz[0], ALU.mult, ALU.mult), reads=[bfz[1], bfz[0], bK], writes=[bmoa[mi]])
                            ch = 8 + 2 * hp + hd
                            P.dma("sp", ms_d[ch, :, cb4 * 512:(cb4 + 1) * 512], mob[mi][:, 0, :], reads=[bmoa[mi]], writes=[bms])

                        fin2q.extend([f2a, f2b, f2c, f2d, f2e, f2f])
                while fin2q:
                    fin2q.pop(0)()
        for _ in g_ada:
            pass
        P.dma("sp", modT2, modd[:, 32 * 128:96 * 128].rearrange("o (j p) -> (o j) p", p=128), reads=[bmd], writes=[bmT2])
        P.op("pe", TR(ps[:, 2, 0:64], modT2, identf[0:64, 0:64]), reads=[bmT2, bC], writes=[bps[2]])
        P.op("dve", TT(mod[:, 32:96], ps[:, 2, 0:64], cs("b_l")[:, 32:96], ALU.add), reads=[bps[2], bC], writes=[bK])
        P.op("dve", STT(A2, mod[:, 64:80], 1.0, cs("n2w"), ALU.add, ALU.mult), reads=[bK, bC], writes=[bK])
        P.barrier()
    A.off = base_off

    bout = P.buf("out")
    if "C" in phases:
        h2T = A.alloc([KC, 1024], BF16)
        bh2a, bh2d = P.buf("h2a"), P.buf("h2d")
        wgu = [A.alloc([KC, 256], BF16) for _ in range(2)]
        bwgu = [P.buf(f"wgu{i}") for i in range(2)]
        wdb = [A.alloc([NFF, 128], BF16) for _ in range(2)]
        bwd = [P.buf(f"wd{i}") for i in range(2)]
        tmpf = [A.alloc([2, 512], F32) for _ in range(2)]
        btmp = [P.buf(f"tmp{i}") for i in range(2)]
        un0 = A.off
        mxb = [A.alloc([KC, 512], BF16) for _ in range(2)]
        bmx = [P.buf(f"mx{i}") for i in range(2)]
        xsb = A.alloc([8, D], F32)
        bxsb = P.buf("xsb")
        xn = [A.alloc([D], BF16) for _ in range(2)]
        bxn = [P.buf(f"xn{i}") for i in range(2)]
        junk = A.alloc([D], BF16)
        bjunk = P.buf("junk")
        un1 = A.off
        A.off = un0
        actT = A.alloc([NFF, 1024], BF16)
        bact = P.buf("actT")
        xs = [A.alloc([8, 128], F32) for _ in range(2)]
        bxs = [P.buf(f"xs{i}") for i in range(2)]
        sg = [A.alloc([512], F32) for _ in range(2)]
        bsg = [P.buf(f"sg{i}") for i in range(2)]
        A.off = max(A.off, un1)
        wob = [wgu[i][:, :, 0:128] for i in range(2)]
        out_v = out_d.rearrange("(t p) d -> p t d", p=128)
        tcount = [0]
        wcount = [0]
        for TB in range(2):
            for sbk in range(2):
                g512 = TB * 2 + sbk
                P.dma("sp", mxb[sbk], ms_d[:, :, g512 * 512:(g512 + 1) * 512].rearrange("c p n -> p c n"), reads=[bms], writes=[bmx[sbk]])
                P.dma("sp", xsb[:, 4 * sbk:4 * sbk + 4, :], xv[:, 16 + 4 * g512:16 + 4 * g512 + 4, :], writes=[bxsb])
            pend = None
            for m in range(16):
                wi = wcount[0] % 2
                wcount[0] += 1
                P.dma("pool", wob[wi], wo_d[m], writes=[bwgu[wi]])
                pb = 2 * (m % 2)
                for sbk in range(2):
                    for kc in range(KC):
                        P.op("pe", MM(ps[:, pb + sbk, :], wob[wi][:, kc, :], mxb[sbk][:, kc, :], kc == 0, kc == KC - 1),
                             reads=[bwgu[wi], bmx[sbk]], writes=[bps[pb + sbk]])
                ti = tcount[0] % 2
                tcount[0] += 1
                P.op("act", ACTF(tmpf[ti], ps[:, pb:pb + 2, :], AF.Copy, scale=gate1[:, m:m + 1]), reads=[bps[pb], bps[pb + 1], bK], writes=[btmp[ti]])
                if pend is not None:
                    pend()

                def pend(m=m, ti=ti):
                    tb0 = 4 + 2 * ti
                    for j in range(8):
                        P.op("pe", TR(ps[:, tb0 + j // 4, (j % 4) * 128:(j % 4 + 1) * 128], tmpf[ti][:, j // 4, (j % 4) * 128:(j % 4 + 1) * 128], identf),
                             reads=[btmp[ti], bC], writes=[bps[tb0 + j // 4]])
                    xslice = xsb[:, :, m * 128:(m + 1) * 128]
                    P.op("dve", TT(xslice, ps[:, tb0:tb0 + 2, :].rearrange("p b (j f) -> p (b j) f", j=4), xslice, ALU.add),
                         reads=[bps[tb0], bps[tb0 + 1], bxsb], writes=[bxsb])
            pend()
            ntiles = []
            for j in range(8):
                col = j * 128
                ntiles.append((xsb[:, j, :], bxsb, A2, Sh2, (lambda kc, col=col: h2T[:, kc, col:col + 128]), bh2a, bh2d))
            norm_pipeline(ntiles, xn, bxn, junk, bjunk)
            P.dma("sp", out_v[:, 8 * TB:8 * TB + 8, :], xsb, reads=[bxsb], writes=[bout])
            P.barrier()
            for jf in range(NFF):
                wi = wcount[0] % 2
                wcount[0] += 1
                P.dma("pool", wgu[wi], wgu_d[jf], writes=[bwgu[wi]])
                for sb in range(2):
                    for gu in range(2):
                        ob = 2 * (sb % 2) + gu + (4 if jf % 2 else 0)
                        for kc in range(KC):
                            P.op("pe", MM(ps[:, ob, :], wgu[wi][:, kc, gu * 128:(gu + 1) * 128], h2T[:, kc, sb * 512:(sb + 1) * 512], kc == 0, kc == KC - 1),
                                 reads=[bwgu[wi], bh2a, bh2d], writes=[bps[ob]])
                    ob = 2 * (sb % 2) + (4 if jf % 2 else 0)
                    si = tcount[0] % 2
                    tcount[0] += 1
                    P.op("act", ACTF(sg[si], ps[:, ob, :], AF.Silu), reads=[bps[ob]], writes=[bsg[si]])
                    P.op("dve", TT(actT[:, jf, sb * 512:(sb + 1) * 512], sg[si], ps[:, ob + 1, :], ALU.mult), reads=[bsg[si], bps[ob + 1]], writes=[bact], relaxed=True)
            pend = None
            for m in range(16):
                wi = m % 2
                P.dma("pool", wdb[wi], wd_d[m], writes=[bwd[wi]])
                xi = m % 2
                P.dma("sp", xs[xi], out_v[:, 8 * TB:8 * TB + 8, m * 128:(m + 1) * 128], reads=[bout], writes=[bxs[xi]], sem_buf=bxs[xi])
                for sb in range(2):
                    ob = sb
                    for fc in range(NFF):
                        P.op("pe", MM(ps[:, ob, :], wdb[wi][:, fc, :], actT[:, fc, sb * 512:(sb + 1) * 512], fc == 0, fc == NFF - 1),
                             reads=[bwd[wi], bact], writes=[bps[ob]])
                    ti = tcount[0] % 2
                    tcount[0] += 1
                    P.op("act", ACTF(tmpf[ti][:, 0, :], ps[:, ob, :], AF.Copy, scale=gate2[:, m:m + 1]), reads=[bps[ob], bK], writes=[btmp[ti]])
                    if pend is not None:
                        pend()

                    def pend(m=m, sb=sb, ti=ti, xi=xi):
                        tb = 2 + ti
                        for j in range(4):
                            P.op("pe", TR(ps[:, tb, j * 128:(j + 1) * 128], tmpf[ti][:, 0, j * 128:(j + 1) * 128], identf), reads=[btmp[ti], bC], writes=[bps[tb]])
                        xslice = xs[xi][:, sb * 4:(sb + 1) * 4, :]
                        P.op("dve", TT(xslice, ps[:, tb, :].rearrange("p (j f) -> p j f", j=4), xslice, ALU.add), reads=[bps[tb], bxs[xi]], writes=[bxs[xi]])
                        if sb == 1:
                            bo2 = P.buf(f"o2_{TB}_{m}")
                            P.dma("sp", out_v[:, 8 * TB:8 * TB + 8, m * 128:(m + 1) * 128], xs[xi], reads=[bxs[xi]], writes=[bout, bo2], sem_buf=bo2)
            pend()
            P.barrier()
    P.barrier()
    P.emit()
    return nc, P


def _prep_shared(inp):
    f = np.float32
    w_in = np.asarray(inp["w_in"], f)[0]
    cols = []
    for hh in range(4):
        cols += list(range(hh * 128, hh * 128 + 128)) + list(range(512 + hh * 128, 512 + hh * 128 + 128))
        cols += list(range(1024 + hh * 256, 1024 + hh * 256 + 256)) + list(range(2048 + hh * 256, 2048 + hh * 256 + 256))
        cols += [3072 + hh, 3076 + hh]
    for hp in range(4):
        for hd in range(2):
            h = 2 * hp + hd
            cols += list(range(3080 + h * 128, 3080 + h * 128 + 128)) + list(range(4104 + h * 128, 4104 + h * 128 + 128))
            cols += list(range(5128 + h * 128, 5128 + h * 128 + 128))
    cols = np.asarray(cols)
    sh = {}
    sh["w_in_l"] = np.ascontiguousarray(w_in.reshape(KC, 128, N_IN).transpose(1, 0, 2)[:, :, cols])
    sh["w_ada_l"] = np.ascontiguousarray(np.asarray(inp["w_ada"], f)[0].reshape(KC, 128, 24, 512).transpose(2, 1, 0, 3))
    sh["wo_l"] = np.ascontiguousarray(np.asarray(inp["w_out"], f)[0].reshape(KC, 128, 16, 128).transpose(2, 1, 0, 3))
    wgu = np.asarray(inp["w_gate_up"], f)[0].reshape(KC, 128, 2, NFF, 128)
    sh["wgu_l"] = np.ascontiguousarray(wgu.transpose(3, 1, 0, 2, 4).reshape(NFF, 128, KC, 256))
    sh["wd_l"] = np.ascontiguousarray(np.asarray(inp["w_down"], f)[0].reshape(NFF, 128, 16, 128).transpose(2, 1, 0, 3))
    return sh


def _prep_cst(inp, b, half):
    f = np.float32
    c = np.zeros((128, NCST), f)

    def put(name, arr):
        o, n = CO[name]
        c[:, o:o + n] = np.asarray(arr, f).reshape(128, n)

    rep = lambda v: np.broadcast_to(np.asarray(v, f).reshape(1, -1), (128, np.asarray(v).size))
    put("c_l", np.asarray(inp["c"], f)[b].reshape(KC, 128).T)
    put("b_l", np.asarray(inp["b_ada"], f)[0].reshape(96, 128).T)
    put("n1w", np.asarray(inp["norm1_w"], f)[0].reshape(KC, 128).T)
    put("n2w", np.asarray(inp["norm2_w"], f)[0].reshape(KC, 128).T)
    cw = np.asarray(inp["mlstm_conv_w"], f)[0].reshape(4, 2, 4, 128)
    put("cw", cw.transpose(3, 2, 1, 0).reshape(128, 32))
    cb = np.asarray(inp["mlstm_conv_b"], f)[0].reshape(2, 4, 128)
    put("cb", cb.transpose(2, 1, 0).reshape(128, 8))
    gb = np.asarray(inp["mlstm_gate_b"], f)[0].reshape(2, 4)
    put("gb", rep(gb.T.reshape(-1)))
    put("nw", rep(np.asarray(inp["mlstm_norm_w"], f)[0].reshape(-1)))
    qw = np.asarray(inp["q_norm_w"], f)[0]
    kw = np.asarray(inp["k_norm_w"], f)[0]
    put("wqk", rep(np.concatenate([qw, qw, kw, kw, qw, qw, kw, kw])))
    put("lamv", rep(np.concatenate([np.asarray(inp[k], f)[0] for k in ("lambda_q1", "lambda_k1", "lambda_q2", "lambda_k2")])))
    put("subw", np.asarray(inp["subln_w"], f)[0].reshape(128, 1))
    put("flag", np.full((128, 1), float(half), f))
    put("pbias", np.full((128, 1), 0.0 if half else -30000.0, f))
    invf = (np.float32(500000.0) ** (-np.arange(0, 16, 2, dtype=np.float32) / np.float32(16))).astype(f)
    put("invf", rep(invf))
    put("ident", np.eye(128, dtype=f))
    put("tri", np.triu(np.ones((128, 128), f)))
    return c


def _in_maps(inp):
    sh = _prep_shared(inp)
    x = np.asarray(inp["x"], np.float32)
    pos = np.asarray(inp["positions"], np.int32)
    maps = []
    for core in range(8):
        b, half = core // 2, core % 2
        m = dict(sh)
        m["x"] = np.ascontiguousarray(np.concatenate([x[b, 0:TOK], x[b, half * TOK:(half + 1) * TOK]], axis=0))
        pp = np.concatenate([pos[b, 0:TOK], pos[b, half * TOK:(half + 1) * TOK]])
        m["pos"] = np.ascontiguousarray(pp.reshape(NTILE, 128).T)
        m["cst"] = _prep_cst(inp, b, half)
        maps.append(m)
    return maps


_NC = None


def kernel(**inputs):
    global _NC
    if _NC is None:
        _NC = build()[0]
    maps = _in_maps(inputs)
    res = run_bass_kernel_spmd(_NC, maps, core_ids=list(range(8)))
    out = np.empty((4, 2 * TOK, D), np.float32)
    for core in range(8):
        b, half = core // 2, core % 2
        out[b, half * TOK:(half + 1) * TOK] = res.results[core]["out"]
    return out
```

```python
import math
import os
import numpy as np
import concourse.bass as bass
import concourse.mybir as mybir
from concourse.bass_utils import run_bass_kernel_spmd

F32 = mybir.dt.float32
BF16 = mybir.dt.bfloat16
I32 = mybir.dt.int32
AF = mybir.ActivationFunctionType
ALU = mybir.AluOpType
AX = mybir.AxisListType

SEM_EPOCH = 16000
import os as _os
SAME_ENGINE_SYNC = _os.environ.get("K_SES", "1") == "1"


class Buf:
    __slots__ = ("name", "w", "rd", "dsem", "dcnt", "excl")

    def __init__(self, name, excl=False):
        self.name = name
        self.excl = excl
        self.w = None
        self.rd = {}
        self.dsem = None
        self.dcnt = 0


class Op:
    __slots__ = ("eng", "fn", "waits", "need_sig", "sigidx", "dma_sem")

    def __init__(self, eng, fn):
        self.eng = eng
        self.fn = fn
        self.waits = []
        self.need_sig = False
        self.sigidx = None
        self.dma_sem = None


class Prog:
    ENGS = ("pe", "act", "dve", "pool", "sp")

    def __init__(self, nc):
        self.nc = nc
        self.ops = {e: [] for e in self.ENGS}
        self.nsem = 0
        self.bufs = []

    def buf(self, name, excl=False):
        b = Buf(name, excl)
        self.bufs.append(b)
        return b

    def _newsem(self, name):
        self.nsem += 1
        return self.nc.alloc_semaphore(f"s{self.nsem}_{name}")

    def _collect(self, op, reads, writes, relaxed):
        toks = []
        for b in reads:
            if b.w is not None:
                toks.append((b.w, False))
            if b.excl:
                for k, t in b.rd.items():
                    if k != op.eng:
                        toks.append((t, False))
        for b in writes:
            if b.w is not None:
                toks.append((b.w, True))
            for t in b.rd.values():
                toks.append((t, False))
        for t, waw in toks:
            if t[0] == "op":
                p = t[1]
                if p is op:
                    continue
                if p.eng == op.eng and (p.eng == "pe" or not SAME_ENGINE_SYNC or (waw and relaxed)):
                    continue
                p.need_sig = True
            op.waits.append(t)

    def op(self, eng, fn, reads=(), writes=(), relaxed=False):
        o = Op(eng, fn)
        self._collect(o, reads, writes, relaxed)
        tok = ("op", o)
        for b in reads:
            b.rd[eng] = tok
        for b in writes:
            b.w = tok
            b.rd = {}
        self.ops[eng].append(o)
        return o

    def dma(self, eng, out_ap, in_ap, reads=(), writes=(), sem_buf=None):
        sb = sem_buf or (writes[0] if writes else reads[0])
        if sb.dsem is None:
            sb.dsem = self._newsem(sb.name)
        sb.dcnt += 16
        sem, cnt = sb.dsem, sb.dcnt

        def fn(e, out_ap=out_ap, in_ap=in_ap):
            return e.dma_start(out=out_ap, in_=in_ap)

        o = Op(eng, fn)
        o.dma_sem = sem
        self._collect(o, reads, writes, False)
        tok = ("sem", sem, cnt)
        for b in reads:
            b.rd[("d", id(sem))] = tok
        for b in writes:
            b.w = tok
            b.rd = {}
        self.ops[eng].append(o)
        return o

    def barrier(self):
        toks = []
        for b in self.bufs:
            if b.w is not None:
                toks.append(b.w)
            toks.extend(b.rd.values())
        for e in self.ENGS:
            for o in reversed(self.ops[e]):
                if o.fn is not None and o.dma_sem is None:
                    toks.append(("op", o))
                    break
        for e in self.ENGS:
            o = Op(e, None)
            for t in toks:
                if t[0] == "op":
                    if t[1].eng == e:
                        continue
                    t[1].need_sig = True
                o.waits.append(t)
            self.ops[e].append(o)
        for b in self.bufs:
            b.w = None
            b.rd = {}

    def final_wait(self, eng, bufs):
        o = Op(eng, None)
        for b in bufs:
            if b.w is not None:
                o.waits.append(b.w)
                if b.w[0] == "op":
                    b.w[1].need_sig = True
        self.ops[eng].append(o)

    def emit(self):
        nc = self.nc
        eng_sems = {}
        for e in self.ENGS:
            n = 0
            for o in self.ops[e]:
                if o.need_sig and o.dma_sem is None and o.fn is not None:
                    o.sigidx = n
                    n += 1
            eng_sems[e] = [self._newsem(f"{e}{i}") for i in range((n + SEM_EPOCH - 1) // SEM_EPOCH)]
        self.stats = {e: len(self.ops[e]) for e in self.ENGS}
        self.stats["nsem"] = self.nsem

        def resolve(t):
            if t[0] == "sem":
                return t[1], t[2]
            p = t[1]
            assert p.sigidx is not None, "waiting on op without signal"
            return eng_sems[p.eng][p.sigidx // SEM_EPOCH], p.sigidx % SEM_EPOCH + 1

        def run(e, h):
            waited = {}
            for o in self.ops[e]:
                need = {}
                for t in o.waits:
                    s, v = resolve(t)
                    k = id(s)
                    if waited.get(k, 0) >= v:
                        continue
                    if k not in need or need[k][1] < v:
                        need[k] = (s, v)
                for k, (s, v) in need.items():
                    h.wait_ge(s, v)
                    waited[k] = v
                if o.fn is None:
                    continue
                ins = o.fn(h)
                if o.dma_sem is not None:
                    ins.then_inc(o.dma_sem, 16)
                elif o.need_sig:
                    ins.then_inc(eng_sems[e][o.sigidx // SEM_EPOCH], 1)

        with nc.Block() as block:
            @block.tensor
            def _(h):
                run("pe", h)

            @block.scalar
            def _(h):
                run("act", h)

            @block.vector
            def _(h):
                run("dve", h)

            @block.gpsimd
            def _(h):
                run("pool", h)

            @block.sync
            def _(h):
                run("sp", h)


def MM(out, lhsT, rhs, start=True, stop=True):
    return lambda e: e.matmul(out, lhsT=lhsT, rhs=rhs, start=start, stop=stop)


def TR(out, in_, ident):
    return lambda e: e.transpose(out=out, in_=in_, identity=ident)


def ACTF(out, in_, func, scale=None, bias=None, accum=None):
    kw = {}
    if scale is not None:
        kw["scale"] = scale
    if bias is not None:
        kw["bias"] = bias
    if accum is not None:
        kw["accum_out"] = accum
    return lambda e: e.activation(out=out, in_=in_, func=func, **kw)


def TS(out, in0, s1, s2=None, op0=ALU.mult, op1=None):
    if op1 is None:
        return lambda e: e.tensor_scalar(out=out, in0=in0, scalar1=s1, scalar2=None, op0=op0)
    return lambda e: e.tensor_scalar(out=out, in0=in0, scalar1=s1, scalar2=s2, op0=op0, op1=op1)


def TT(out, in0, in1, op):
    return lambda e: e.tensor_tensor(out=out, in0=in0, in1=in1, op=op)


def STT(out, in0, scalar, in1, op0, op1):
    return lambda e: e.scalar_tensor_tensor(out=out, in0=in0, scalar=scalar, in1=in1, op0=op0, op1=op1)


def CP(out, in_):
    return lambda e: e.tensor_copy(out=out, in_=in_)


def MSET(ap, v):
    return lambda e: e.memset(ap, v)


def RCP(out, in_):
    return lambda e: e.reciprocal(out=out, in_=in_)


D = 2048
KC = 16
TOK = 2048
NTILE = 32
NBLK = 8
DFF = 5632
NFF = 44
N_IN = 6152
EPS = 1e-6
LAMBDA_INIT = 0.8 - 0.6 * math.exp(-0.3 * 0)
ML_W = 770
DA_W = 768
ARENA_BYTES = 211968

_CST = [("c_l", 16), ("b_l", 96), ("n1w", 16), ("n2w", 16), ("cw", 32), ("cb", 8), ("gb", 8), ("nw", 1024),
        ("wqk", 512), ("lamv", 256), ("subw", 1), ("flag", 1), ("pbias", 1), ("invf", 8), ("ident", 128),
        ("tri", 128)]
CO = {}
_o = 0
for _n, _s in _CST:
    CO[_n] = (_o, _s)
    _o += _s
NCST = _o


class Arena:
    def __init__(self, nc):
        self.t = nc.alloc_sbuf_tensor("arena", [128, ARENA_BYTES // 4], F32)
        self.off = 0

    def alloc(self, shape, dtype, parts=128, p0=0):
        n = int(np.prod(shape))
        esz = 2 if dtype == BF16 else 4
        nb = (n * esz + 31) // 32 * 32
        o = self.off
        self.off += nb
        assert self.off <= ARENA_BYTES, f"SBUF arena overflow {self.off}"
        v = self.t[p0:p0 + parts, o // 4:(o + nb) // 4]
        if dtype != F32:
            v = v.bitcast(dtype)
        v = v[:, 0:n]
        if len(shape) > 1:
            names = [f"a{i}" for i in range(len(shape))]
            kw = {nm: int(s) for nm, s in zip(names, shape)}
            v = v.rearrange("p (" + " ".join(names) + ") -> p " + " ".join(names), **kw)
        return v


def build(dbg=False, phases="0ABC"):
    nc = bass.Bass("TRN2", target_bir_lowering=False)
    P = Prog(nc)
    A = Arena(nc)

    def din(name, shape, dt=F32):
        return nc.dram_tensor(name, list(shape), dt, kind="ExternalInput").ap()

    def dscr(name, shape, dt):
        if dbg:
            return nc.dram_tensor(name, list(shape), dt, kind="ExternalOutput").ap()
        return nc.dram_tensor(name, list(shape), dt).ap()

    x_d = din("x", [2 * TOK, D])
    pos_d = din("pos", [128, NTILE], I32)
    cst_d = din("cst", [128, NCST])
    wada_d = din("w_ada_l", [24, 128, KC, 512])
    win_d = din("w_in_l", [128, KC, N_IN])
    wo_d = din("wo_l", [16, 128, KC, 128])
    wgu_d = din("wgu_l", [NFF, 128, KC, 256])
    wd_d = din("wd_l", [16, 128, NFF, 128])
    out_d = nc.dram_tensor("out", [TOK, D], F32, kind="ExternalOutput").ap()
    hs_d = dscr("hs", [NBLK, 128, KC, 512], BF16)
    ms_d = dscr("ms", [16, 128, TOK], BF16)
    modd = dscr("modd", [1, 6 * D], F32)

    ps = nc.alloc_psum_tensor("ps", [128, 8, 512], F32)
    bps = [P.buf(f"ps{i}", excl=True) for i in range(8)]

    def psbf(bank):
        return ps[:, bank, :].bitcast(BF16).rearrange("p (a b) -> p a b", a=8)

    CST = A.alloc([NCST], F32)
    bC = P.buf("cst")
    bK = P.buf("derived")
    P.dma("sp", CST, cst_d, writes=[bC])

    def cs(name):
        o, n = CO[name]
        return CST[:, o:o + n]

    identf = cs("ident")
    tri = cs("tri")
    flag = cs("flag")
    pbias = cs("pbias")
    posi = A.alloc([NTILE], I32)
    P.dma("sp", posi, pos_d, writes=[bK])
    identb = A.alloc([128], BF16)
    onesf = A.alloc([128], F32)
    onesb = A.alloc([128], BF16)
    mod = A.alloc([96], F32)
    A1 = A.alloc([16], F32)
    A2 = A.alloc([16], F32)
    Sh1 = mod[:, 0:16]
    Sh2 = mod[:, 48:64]
    gate1 = mod[:, 32:48]
    gate2 = mod[:, 80:96]
    cosT = A.alloc([NTILE, 8], F32)
    sinT = A.alloc([NTILE, 8], F32)
    stats = A.alloc([128], F32)
    neglam = A.alloc([1], F32)
    subw_s = A.alloc([1], F32)
    sc = A.alloc([16], F32)
    P.op("dve", CP(identb, identf), reads=[bC], writes=[bK])
    P.op("dve", MSET(onesf, 1.0), writes=[bK])
    P.op("dve", MSET(onesb, 1.0), writes=[bK])
    P.op("dve", MSET(stats, 0.0), writes=[bK])
    rp = A.off
    posf = A.alloc([NTILE], F32)
    ang = A.alloc([NTILE, 8], F32)
    ang2 = A.alloc([NTILE, 8], F32)
    ki = A.alloc([NTILE, 8], I32)
    kf = A.alloc([NTILE, 8], F32)
    P.op("dve", CP(posf, posi), reads=[bK], writes=[bK])
    P.op("dve", TT(ang, posf[:, :, None].to_broadcast([128, NTILE, 8]),
                   cs("invf")[:, None, :].to_broadcast([128, NTILE, 8]), ALU.mult), reads=[bK, bC], writes=[bK])
    for (dst, shift) in ((sinT, 0.0), (cosT, math.pi / 2)):
        P.op("dve", TS(ang2, ang, shift, None, ALU.add), reads=[bK], writes=[bK])
        P.op("dve", TS(ki, ang2, 1.0 / (2 * math.pi), None, ALU.mult), reads=[bK], writes=[bK])
        P.op("dve", CP(kf, ki), reads=[bK], writes=[bK])
        P.op("dve", STT(ang2, kf, -2 * math.pi, ang2, ALU.mult, ALU.add), reads=[bK], writes=[bK])
        P.op("dve", TS(ang2, ang2, -3.1415925, 3.1415925, ALU.max, ALU.min), reads=[bK], writes=[bK])
        P.op("act", ACTF(dst, ang2, AF.Sin), reads=[bK], writes=[bK])
    lamv = cs("lamv").rearrange("p (a b) -> p a b", a=4)
    lt = A.alloc([2, 64], F32)
    ls = A.alloc([2], F32)
    P.op("dve", TT(lt[:, 0, :], lamv[:, 0, :], lamv[:, 1, :], ALU.mult), reads=[bC], writes=[bK])
    P.op("dve", TT(lt[:, 1, :], lamv[:, 2, :], lamv[:, 3, :], ALU.mult), reads=[bC, bK], writes=[bK])
    P.op("dve", lambda e: e.tensor_reduce(out=ls, in_=lt, axis=AX.X, op=ALU.add), reads=[bK], writes=[bK])
    P.op("act", ACTF(ls, ls, AF.Exp), reads=[bK], writes=[bK])
    P.op("dve", STT(neglam, ls[:, 1:2], -LAMBDA_INIT, ls[:, 0:1], ALU.add, ALU.subtract), reads=[bK], writes=[bK])
    P.op("dve", TS(subw_s, cs("subw"), 1.0 - LAMBDA_INIT, None, ALU.mult), reads=[bC], writes=[bK])
    P.op("act", ACTF(sc, cs("c_l"), AF.Silu), reads=[bC], writes=[bK])
    base_off = A.off

    norm_uid = [0]

    def norm_s1(xsrc, bx, xn, bxn, junk, bjunk):
        uid = norm_uid[0]
        norm_uid[0] += 1
        st = stats[:, 2 * uid:2 * uid + 2]
        bst = P.buf(f"st{uid}")
        k = uid % 2
        P.op("act", ACTF(junk, xsrc, AF.Square, accum=st[:, 0:1]), reads=[bx, bK], writes=[bjunk, bst], relaxed=True)
        P.op("act", ACTF(st[:, 1:2], st[:, 0:1], AF.Ln, scale=1.0 / D, bias=EPS), reads=[bst], writes=[bst])
        P.op("act", ACTF(st[:, 1:2], st[:, 1:2], AF.Exp, scale=-0.5), reads=[bst], writes=[bst])
        P.op("dve", TS(xn[k], xsrc, st[:, 1:2], None, ALU.mult), reads=[bx, bst], writes=[bxn[k]])
        return k

    def norm_s2(k, Av, Shv, dst_of_kc, bdst_act, bdst_dve, xn, bxn):
        for hf in range(2):
            bank = 4 + 2 * k + hf
            pv = psbf(bank)
            for q in range(8):
                kc = hf * 8 + q
                P.op("pe", TR(pv[:, q, :], xn[k][:, kc * 128:(kc + 1) * 128], identb), reads=[bxn[k], bK], writes=[bps[bank]])
            for q in range(8):
                kc = hf * 8 + q
                if hf == 0 and q < 6:
                    P.op("act", ACTF(dst_of_kc(kc), pv[:, q, :], AF.Identity, scale=Av[:, kc:kc + 1], bias=Shv[:, kc:kc + 1]),
                         reads=[bps[bank], bK], writes=[bdst_act], relaxed=True)
                else:
                    P.op("dve", TS(dst_of_kc(kc), pv[:, q, :], Av[:, kc:kc + 1], Shv[:, kc:kc + 1], ALU.mult, ALU.add),
                         reads=[bps[bank], bK], writes=[bdst_dve], relaxed=True)

    def norm_pipeline(tiles, xn, bxn, junk, bjunk, after=None):
        ks = {}
        ks[0] = norm_s1(tiles[0][0], tiles[0][1], xn, bxn, junk, bjunk)
        for t, tl in enumerate(tiles):
            if t + 1 < len(tiles):
                ks[t + 1] = norm_s1(tiles[t + 1][0], tiles[t + 1][1], xn, bxn, junk, bjunk)
            norm_s2(ks[t], tl[2], tl[3], tl[4], tl[5], tl[6], xn, bxn)
            if after is not None:
                after(t)

    xv = x_d.rearrange("(t p) d -> p t d", p=128)
    bmd = P.buf("modd")
    bhs1 = P.buf("hs")
    bhs = [bhs1] * NBLK

    if "A" in phases:
        mrow = [A.alloc([512], F32, parts=1) for _ in range(2)]
        bmr = [P.buf(f"mrow{i}") for i in range(2)]
        wab = [A.alloc([KC, 512], F32) for _ in range(2)]
        bwa = [P.buf(f"wab{i}") for i in range(2)]
        modT = A.alloc([128], F32, parts=96)
        bmT = P.buf("modT")

        def ada_cols(cb):
            i = cb % 2
            P.dma("sp", wab[i], wada_d[cb], writes=[bwa[i]])
            for kc in range(KC):
                P.op("pe", MM(ps[0:1, i, :], sc[:, kc:kc + 1], wab[i][:, kc, :], kc == 0, kc == KC - 1),
                     reads=[bwa[i], bK], writes=[bps[i]])
            P.op("act", lambda e, i=i: e.copy(out=mrow[i], in_=ps[0:1, i, :]), reads=[bps[i]], writes=[bmr[i]])
            P.dma("pool", modd[:, cb * 512:(cb + 1) * 512], mrow[i], reads=[bmr[i]], writes=[bmd])

        def ada_finish(j0, j1):
            n = j1 - j0
            P.dma("sp", modT[0:n, :], modd[:, j0 * 128:j1 * 128].rearrange("o (j p) -> (o j) p", p=128), reads=[bmd], writes=[bmT])
            P.op("pe", TR(ps[:, 2, 0:n], modT[0:n, :], identf[0:n, 0:n]), reads=[bmT, bC], writes=[bps[2]])
            P.op("dve", TT(mod[:, j0:j1], ps[:, 2, 0:n], cs("b_l")[:, j0:j1], ALU.add), reads=[bps[2], bC], writes=[bK])

        for cb in range(8):
            ada_cols(cb)
        ada_finish(0, 32)
        P.op("dve", STT(A1, mod[:, 16:32], 1.0, cs("n1w"), ALU.add, ALU.mult), reads=[bK, bC], writes=[bK])

        xb = [A.alloc([4, D], F32) for _ in range(2)]
        bxb = [P.buf(f"xb{i}") for i in range(2)]
        xn = [A.alloc([D], BF16) for _ in range(2)]
        bxn = [P.buf(f"xn{i}") for i in range(2)]
        junk = A.alloc([D], BF16)
        bjunk = P.buf("junk")
        hTb = [A.alloc([KC, 512], BF16) for _ in range(2)]
        bhTa = [P.buf(f"hTa{i}") for i in range(2)]
        bhTd = [P.buf(f"hTd{i}") for i in range(2)]
        P.dma("sp", xb[0], xv[:, 0:4, :], writes=[bxb[0]])
        P.dma("sp", xb[1], xv[:, 4:8, :], writes=[bxb[1]])
        tiles = []
        for blk in range(NBLK):
            i = blk % 2
            for j in range(4):
                tiles.append((xb[i][:, j, :], bxb[i], A1, Sh1, (lambda kc, i=i, j=j: hTb[i][:, kc, j * 128:(j + 1) * 128]), bhTa[i], bhTd[i]))

        def after_tile(t):
            blk, j = divmod(t, 4)
            i = blk % 2
            if j == 3:
                P.dma("pool", hs_d[blk], hTb[i], reads=[bhTa[i], bhTd[i]], writes=[bhs[blk]])
                if blk + 2 < NBLK:
                    P.dma("sp", xb[i], xv[:, 4 * blk + 8:4 * blk + 12, :], writes=[bxb[i]])

        norm_pipeline(tiles, xn, bxn, junk, bjunk, after=after_tile)
        P.barrier()
    A.off = base_off

    bms = P.buf("ms")

    if "B" in phases:
        wbuf = [A.alloc([KC, ML_W], BF16) for _ in range(2)]
        bwb = [P.buf(f"wb{i}") for i in range(2)]
        hbuf = [A.alloc([KC, 512], BF16) for _ in range(2)]
        bhb = [P.buf(f"hb{i}") for i in range(2)]
        mob = [A.alloc([2, 512], BF16) for _ in range(2)]
        bmoa = [P.buf(f"moa{i}") for i in range(2)]
        xc = [A.alloc([515], F32) for _ in range(2)]
        bxc = [P.buf(f"xc{i}") for i in range(2)]
        ycv = [A.alloc([512], F32) for _ in range(2)]
        bycv = [P.buf(f"ycv{i}") for i in range(2)]
        sgm = [A.alloc([512], F32) for _ in range(2)]
        bsgm = [P.buf(f"sgm{i}") for i in range(2)]
        qkT = [[A.alloc([512], BF16) for _ in range(2)] for _ in range(2)]
        bqkT = [[P.buf(f"qkT{p}_{i}") for i in range(2)] for p in range(2)]
        Gp = [A.alloc([96], F32) for _ in range(2)]
        bGp = [P.buf(f"G{p}") for p in range(2)]
        vaug = [A.alloc([257], BF16) for _ in range(2)]
        bva = [P.buf(f"vaug{i}") for i in range(2)]
        vw = [A.alloc([257], BF16) for _ in range(2)]
        bvw = [P.buf(f"vw{i}") for i in range(2)]
        ktm = [A.alloc([128], BF16) for _ in range(2)]
        bktm = [P.buf(f"ktm{i}") for i in range(2)]
        ATs = [A.alloc([128], BF16) for _ in range(2)]
        bATs = [P.buf(f"ATs{i}") for i in range(2)]
        C32 = A.alloc([257], F32)
        bC32 = P.buf("C32")
        Cb = [A.alloc([257], BF16) for _ in range(2)]
        bCb = [P.buf(f"Cb{i}") for i in range(2)]
        dds = A.alloc([64 * 8], F32)
        junk2 = A.alloc([256], BF16)
        bjunk2 = P.buf("junk2")
        hn = [A.alloc([256], F32) for _ in range(2)]
        bhn = [P.buf(f"hn{i}") for i in range(2)]
        sig = [A.alloc([256], F32) for _ in range(2)]
        bsig = [P.buf(f"sig{i}") for i in range(2)]
        hm = [A.alloc([256], BF16) for _ in range(2)]
        bhm = [P.buf(f"hm{i}") for i in range(2)]
        KT = [A.alloc([NTILE * 128], BF16) for _ in range(2)]
        bKT = [P.buf(f"KT{i}") for i in range(2)]
        Vh = A.alloc([2, NTILE, 128], BF16)
        bVh = P.buf("Vh")
        QT = [A.alloc([512], BF16) for _ in range(2)]
        bQT = [P.buf(f"QT{i}") for i in range(2)]
        sqb = [A.alloc([8, 64], F32) for _ in range(2)]
        bsqb = [P.buf(f"sqb{i}") for i in range(2)]
        rs = [A.alloc([8], F32) for _ in range(2)]
        brs = [P.buf(f"rs{i}") for i in range(2)]
        t1 = [A.alloc([8, 64], F32) for _ in range(2)]
        bt1 = [P.buf(f"t1{i}") for i in range(2)]
        qkb = [A.alloc([8, 64], BF16) for _ in range(2)]
        bqkb = [P.buf(f"qkb{i}") for i in range(2)]
        rt = [A.alloc([4, 8, 8], F32) for _ in range(2)]
        brt = [P.buf(f"rt{i}") for i in range(2)]
        PT = [A.alloc([2, 512], BF16) for _ in range(3)]
        bPT = [P.buf(f"PT{i}") for i in range(3)]
        Pacc = [A.alloc([512], F32) for _ in range(2)]
        bPacc = [P.buf(f"Pacc{i}") for i in range(2)]
        fz = [A.alloc([512], F32) for _ in range(5)]
        bfz = [P.buf(f"fz{i}") for i in range(5)]
        wbg = A.alloc([KC, 256], BF16)
        bwbg = P.buf("wbg")
        mrb = A.alloc([256], F32, parts=1)
        bmrb = P.buf("mrb")
        scb = A.alloc([16], BF16)
        modT2 = A.alloc([128], F32, parts=64)
        bmT2 = P.buf("modT2")
        P.op("dve", CP(scb, sc), reads=[bK], writes=[bK])
        wada_v = wada_d.rearrange("c p k (h n) -> c h p k n", h=2)

        def ada_bg():
            for c in range(8, 24):
                for hf in range(2):
                    P.dma("pool", wbg, wada_v[c, hf], writes=[bwbg])
                    yield
                    for kc in range(KC):
                        P.op("pe", MM(ps[0:1, 2, 0:256], scb[:, kc:kc + 1], wbg[:, kc, :], kc == 0, kc == KC - 1),
                             reads=[bwbg, bK], writes=[bps[2]])
                    P.op("act", lambda e: e.copy(out=mrb, in_=ps[0:1, 2, 0:256]), reads=[bps[2]], writes=[bmrb])
                    col = c * 512 + hf * 256
                    P.dma("sp", modd[:, col:col + 256], mrb, reads=[bmrb], writes=[bmd])
                    yield

        g_ada = ada_bg()

        def ada_step(n=2):
            for _ in range(n):
                try:
                    next(g_ada)
                except StopIteration:
                    return

        def interleave(g1, g2):
            gens = [g for g in (g1, g2) if g is not None]
            while gens:
                for g in list(gens):
                    try:
                        next(g)
                    except StopIteration:
                        gens.remove(g)

        P.op("dve", MSET(dds, 0.0), writes=[bK])
        for i in range(2):
            P.op("dve", MSET(vaug[i][:, 256:257], 1.0), writes=[bva[i]])

        wml = cs("nw").rearrange("p (h v) -> p h v", h=4)
        cw = cs("cw").rearrange("p (h q t) -> p h q t", h=4, q=2)
        cbv = cs("cb").rearrange("p (h q) -> p h q", h=4)
        gbv = cs("gb").rearrange("p (h g) -> p h g", h=4)
        wqk = cs("wqk").rearrange("p (g d) -> p g d", g=8)
        LNS = math.log(128.0 ** -0.5)

        npass = 8
        pass_cols = [(hh * ML_W, ML_W) for hh in range(4)] + [(4 * ML_W + hp * DA_W, DA_W) for hp in range(4)]

        def load_w(pi):
            o, n = pass_cols[pi]
            P.dma("pool", wbuf[pi % 2][:, :, 0:n], win_d[:, :, o:o + n], writes=[bwb[pi % 2]])

        hcount = [0]

        def load_h(blk):
            i = hcount[0] % 2
            hcount[0] += 1
            P.dma("sp", hbuf[i], hs_d[blk], reads=[bhs[blk]], writes=[bhb[i]])
            return i

        load_w(0)
        cur_tile_head = [0]
        mo_cnt = [0]
        KB = os.environ.get("KDBG_B", "")
        for pi in range(npass):
            if pi + 1 < npass:
                load_w(pi + 1)
            if (KB == "ml" and pi >= 4) or (KB == "da" and pi < 4) or (KB == "ml1" and pi >= 1) or (KB == "da1" and pi != 4):
                continue
            W = wbuf[pi % 2]
            bW = bwb[pi % 2]
            nxt = load_h(0)
            if pi < 4:
                hh = pi
                P.op("dve", MSET(C32, 0.0), writes=[bC32])
                hmap = {}

                def ml_prep(blk, hi):
                    H = hbuf[hi]
                    bH = bhb[hi]
                    cur = blk >= 4
                    pb = blk % 2
                    for qk in (1, 0):
                        if qk == 0 and blk < 3:
                            continue
                        for kc in range(KC):
                            P.op("pe", MM(ps[:, qk, :], W[:, kc, qk * 128:(qk + 1) * 128], H[:, kc, :], kc == 0, kc == KC - 1),
                                 reads=[bW, bH], writes=[bps[qk]])
                        yield
                        first = (blk == 0) if qk == 1 else (blk == 3)
                        if first:
                            P.op("dve", MSET(xc[qk][:, 0:3], 0.0), writes=[bxc[qk]])
                        elif blk == 4:
                            P.op("dve", TS(xc[qk][:, 0:3], xc[qk][:, 512:515], flag[:, 0:1], None, ALU.mult),
                                 reads=[bxc[qk], bC], writes=[bxc[qk]])
                        else:
                            P.op("dve", CP(xc[qk][:, 0:3], xc[qk][:, 512:515]), reads=[bxc[qk]], writes=[bxc[qk]])
                        P.op("act", lambda e, qk=qk: e.copy(out=xc[qk][:, 3:515], in_=ps[:, qk, :]), reads=[bps[qk]], writes=[bxc[qk]])
                        yield
                        if qk == 0 and blk < 4:
                            continue
                        P.op("dve", TS(ycv[qk], xc[qk][:, 0:512], cw[:, hh, qk, 0:1], cbv[:, hh, qk:qk + 1], ALU.mult, ALU.add),
                             reads=[bxc[qk], bC], writes=[bycv[qk]])
                        yield
                        for tp in range(1, 4):
                            P.op("dve", STT(ycv[qk], xc[qk][:, tp:tp + 512], cw[:, hh, qk, tp:tp + 1], ycv[qk], ALU.mult, ALU.add),
                                 reads=[bxc[qk], bC, bycv[qk]], writes=[bycv[qk]])
                            yield
                        P.op("act", ACTF(sgm[qk], ycv[qk], AF.Exp, scale=-1.0), reads=[bycv[qk]], writes=[bsgm[qk]])
                        P.op("act", ACTF(sgm[qk], sgm[qk], AF.Ln, bias=1.0), reads=[bsgm[qk]], writes=[bsgm[qk]])
                        yield
                        P.op("act", ACTF(sgm[qk], sgm[qk], AF.Exp, scale=-1.0), reads=[bsgm[qk]], writes=[bsgm[qk]])
                        P.op("dve", TT(qkT[pb][qk], ycv[qk], sgm[qk], ALU.mult), reads=[bycv[qk], bsgm[qk]], writes=[bqkT[pb][qk]])
                        yield
                    for j in range(4):
                        for kc in range(KC):
                            P.op("pe", MM(ps[:, 5, 384 + 2 * j:386 + 2 * j], H[:, kc, j * 128:(j + 1) * 128], W[:, kc, 768:770], kc == 0, kc == KC - 1),
                                 reads=[bW, bH], writes=[bps[5]])
                        yield
                    G = Gp[pb]
                    bG = bGp[pb]
                    gsb = G[:, 0:8].rearrange("p (j g) -> p j g", j=4)
                    th = G[:, 8:16].rearrange("p (j g) -> p j g", j=4)
                    e1, l1, li, nBe, tq, argw, argf = (G[:, 16 + 4 * i:20 + 4 * i] for i in range(7))
                    wv, fpv, eB = (G[:, 48 + 4 * i:52 + 4 * i] for i in range(3))
                    P.op("dve", TT(gsb, ps[:, 5, 384:392].rearrange("p (j g) -> p j g", j=4),
                                   gbv[:, hh:hh + 1, :].to_broadcast([128, 4, 2]), ALU.add), reads=[bps[5], bC], writes=[bG])
                    P.op("act", ACTF(th, gsb, AF.Exp, scale=2.0 / 15.0), reads=[bG], writes=[bG])
                    yield
                    P.op("act", ACTF(th, th, AF.Ln, bias=1.0), reads=[bG], writes=[bG])
                    P.op("act", ACTF(th, th, AF.Exp, scale=-1.0), reads=[bG], writes=[bG])
                    yield
                    P.op("dve", TS(th, th, -2.0, 1.0, ALU.mult, ALU.add), reads=[bG], writes=[bG])
                    P.op("act", ACTF(e1, th[:, :, 1], AF.Exp, scale=-15.0), reads=[bG], writes=[bG])
                    yield
                    P.op("act", ACTF(l1, e1, AF.Ln, bias=1.0), reads=[bG], writes=[bG])
                    P.op("dve", TS(li, th[:, :, 0], 15.0, None, ALU.mult), reads=[bG], writes=[bG])
                    yield
                    P.op("pe", MM(ps[:, 5, 400:404], tri, l1), reads=[bG, bC], writes=[bps[5]])
                    P.op("pe", MM(ps[:, 5, 416:420], onesf, l1), reads=[bG, bK], writes=[bps[5]])
                    yield
                    P.op("dve", CP(nBe, ps[:, 5, 416:420]), reads=[bps[5]], writes=[bG])
                    P.op("dve", TT(tq, li, nBe, ALU.subtract), reads=[bG], writes=[bG])
                    yield
                    P.op("dve", TT(argw, tq, ps[:, 5, 400:404], ALU.add), reads=[bG, bps[5]], writes=[bG])
                    P.op("dve", TT(argf, nBe, ps[:, 5, 400:404], ALU.subtract), reads=[bG, bps[5]], writes=[bG])
                    yield
                    P.op("act", ACTF(wv, argw, AF.Exp), reads=[bG], writes=[bG])
                    P.op("act", ACTF(fpv, argf, AF.Exp, bias=LNS), reads=[bG], writes=[bG])
                    P.op("act", ACTF(eB, nBe, AF.Exp, scale=-1.0), reads=[bG], writes=[bG])
                    yield

                def ml_tiles(blk, hi):
                    H = hbuf[hi]
                    bH = bhb[hi]
                    cur = blk >= 4
                    pb = blk % 2
                    G = Gp[pb]
                    bG = bGp[pb]
                    wv, fpv, eB = (G[:, 48 + 4 * i:52 + 4 * i] for i in range(3))
                    qT = qkT[pb][0]
                    kT = qkT[pb][1]
                    bqT = bqkT[pb][0]
                    bkT = bqkT[pb][1]
                    nv = 512 if cur else 256
                    mi = mo_cnt[0] % 2

                    def emit_vo(j):
                        vb_ = 2 + (j % 2)
                        for kc in range(KC):
                            P.op("pe", MM(ps[:, vb_, 0:nv], H[:, kc, j * 128:(j + 1) * 128], W[:, kc, 256:256 + nv], kc == 0, kc == KC - 1),
                                 reads=[bW, bH], writes=[bps[vb_]])

                    def tile(j):
                        t = blk * 4 + j
                        vb = 2 + (j % 2)
                        nb = 6 + (j % 2)
                        a = t % 2
                        P.op("dve", TS(vw[a][:, 0:256], ps[:, vb, 0:256], wv[:, j:j + 1], None, ALU.mult), reads=[bps[vb], bG], writes=[bvw[a]])
                        P.op("dve", CP(vw[a][:, 256:257], wv[:, j:j + 1]), reads=[bG], writes=[bvw[a]])
                        P.op("pe", TR(psbf(5)[:, 2 + a, :], kT[:, j * 128:(j + 1) * 128], identb), reads=[bkT, bK], writes=[bps[5]])
                        P.op("act", lambda e, a=a: e.copy(out=ktm[a], in_=psbf(5)[:, 2 + a, :]), reads=[bps[5]], writes=[bktm[a]])
                        yield
                        if cur:
                            u = cur_tile_head[0]
                            cur_tile_head[0] += 1
                            dd = dds[:, 8 * u:8 * u + 8]
                            bdd = P.buf(f"dd{u}")
                            P.op("act", lambda e, a=a, vb=vb: e.copy(out=vaug[a][:, 0:256], in_=ps[:, vb, 0:256]), reads=[bps[vb]], writes=[bva[a]])
                            P.op("act", ACTF(sig[a], ps[:, vb, 256:512], AF.Exp, scale=-1.0), reads=[bps[vb]], writes=[bsig[a]])
                            P.op("pe", MM(ps[:, 5, 0:128], kT[:, j * 128:(j + 1) * 128], qT[:, j * 128:(j + 1) * 128]),
                                 reads=[bqT, bkT], writes=[bps[5]])
                            P.op("dve", STT(ATs[a], ps[:, 5, 0:128], wv[:, j:j + 1], tri, ALU.mult, ALU.mult), reads=[bps[5], bG, bC], writes=[bATs[a]])
                            yield
                        if blk == 4 and j == 0:
                            P.op("dve", TS(C32, C32, flag[:, 0:1], None, ALU.mult), reads=[bC32, bC], writes=[bC32])
                        if cur:
                            P.op("dve", TS(Cb[a], C32, eB[:, j:j + 1], None, ALU.mult), reads=[bC32, bG], writes=[bCb[a]])
                            P.op("pe", MM(ps[:, nb, 0:257], ATs[a], vaug[a], True, False), reads=[bATs[a], bva[a]], writes=[bps[nb]])
                            P.op("pe", MM(ps[:, nb, 0:257], qT[:, j * 128:(j + 1) * 128], Cb[a], False, True), reads=[bqT, bCb[a]], writes=[bps[nb]])
                        P.op("pe", MM(ps[:, 4, 0:257], ktm[a], vw[a]), reads=[bktm[a], bvw[a]], writes=[bps[4]])
                        P.op("dve", STT(C32, C32, eB[:, j:j + 1], ps[:, 4, 0:257], ALU.mult, ALU.add), reads=[bC32, bG, bps[4]], writes=[bC32])
                        yield
                        if not cur:
                            return
                        P.op("dve", TT(dd[:, 0:1], ps[:, nb, 256:257], fpv[:, j:j + 1], ALU.mult), reads=[bps[nb], bG], writes=[bdd])
                        P.op("dve", TS(dd[:, 1:2], dd[:, 0:1], -1.0, 1.0, ALU.mult, ALU.max), reads=[bdd], writes=[bdd])
                        yield
                        P.op("dve", TT(dd[:, 1:2], dd[:, 1:2], dd[:, 0:1], ALU.max), reads=[bdd], writes=[bdd])
                        P.op("dve", RCP(dd[:, 1:2], dd[:, 1:2]), reads=[bdd], writes=[bdd])
                        yield
                        P.op("dve", TT(dd[:, 2:3], fpv[:, j:j + 1], dd[:, 1:2], ALU.mult), reads=[bdd, bG], writes=[bdd])
                        P.op("act", ACTF(junk2, ps[:, nb, 0:256], AF.Square, scale=dd[:, 2:3], accum=dd[:, 3:4]), reads=[bps[nb], bdd], writes=[bjunk2, bdd])
                        yield
                        P.op("act", ACTF(dd[:, 4:5], dd[:, 3:4], AF.Ln, scale=1.0 / 256.0, bias=EPS), reads=[bdd], writes=[bdd])
                        P.op("act", ACTF(dd[:, 4:5], dd[:, 4:5], AF.Exp, scale=-0.5), reads=[bdd], writes=[bdd])
                        yield
                        P.op("dve", TT(dd[:, 5:6], dd[:, 2:3], dd[:, 4:5], ALU.mult), reads=[bdd], writes=[bdd])
                        yield
                        P.op("dve", STT(hn[a], ps[:, nb, 0:256], dd[:, 5:6], wml[:, hh, :], ALU.mult, ALU.mult), reads=[bps[nb], bdd, bC], writes=[bhn[a]])
                        P.op("act", ACTF(sig[a], sig[a], AF.Ln, bias=1.0), reads=[bsig[a]], writes=[bsig[a]])
                        yield
                        P.op("act", ACTF(sig[a], sig[a], AF.Exp, scale=-1.0), reads=[bsig[a]], writes=[bsig[a]])
                        yield
                        P.op("dve", TT(hm[a], hn[a], sig[a], ALU.mult), reads=[bhn[a], bsig[a]], writes=[bhm[a]])
                        yield
                        for i2 in range(2):
                            P.op("pe", TR(psbf(5)[:, 4 + i2, :], hm[a][:, i2 * 128:(i2 + 1) * 128], identb), reads=[bhm[a], bK], writes=[bps[5]])
                        P.op("act", lambda e, mi=mi, j=j: e.copy(out=mob[mi][:, :, j * 128:(j + 1) * 128], in_=psbf(5)[:, 4:6, :]),
                             reads=[bps[5]], writes=[bmoa[mi]], relaxed=True)
                        yield

                    emit_vo(0)
                    yield
                    act_t = []
                    for j in range(4):
                        if j + 1 < 4:
                            emit_vo(j + 1)
                            yield
                        act_t.append(tile(j))
                        if len(act_t) == 2:
                            done0 = False
                            while not done0:
                                for g in list(act_t):
                                    try:
                                        next(g)
                                    except StopIteration:
                                        if g is act_t[0]:
                                            done0 = True
                                        act_t.remove(g)
                                yield
                        else:
                            for _ in range(3):
                                try:
                                    next(act_t[0])
                                except StopIteration:
                                    act_t.pop()
                                    break
                            yield
                    for g in act_t:
                        for _ in g:
                            yield
                    if cur:
                        cb4 = blk - 4
                        P.dma("sp", ms_d[2 * hh:2 * hh + 2, :, cb4 * 512:(cb4 + 1) * 512].rearrange("c p n -> p c n"), mob[mi],
                              reads=[bmoa[mi]], writes=[bms])
                        mo_cnt[0] += 1

                hmap[0] = nxt
                for _ in ml_prep(0, hmap[0]):
                    pass
                for blk in range(NBLK):
                    g2 = None
                    if blk + 1 < NBLK:
                        hmap[blk + 1] = load_h(blk + 1)
                        g2 = ml_prep(blk + 1, hmap[blk + 1])
                    interleave(ml_tiles(blk, hmap[blk]), g2)
            else:
                hp = pi - 4
                fin2q = []
                for blk in range(NBLK):
                    ada_step()
                    while fin2q:
                        fin2q.pop(0)()
                    hi = nxt
                    if blk + 1 < NBLK:
                        nxt = load_h(blk + 1)
                    H = hbuf[hi]
                    bH = bhb[hi]
                    cur = blk >= 4
                    c0 = 0 if cur else 128

                    def proj_mm(j):
                        b0 = 2 * (j % 2)
                        for hd in range(2):
                            for kc in range(KC):
                                P.op("pe", MM(ps[:, b0 + hd, c0:384], H[:, kc, j * 128:(j + 1) * 128], W[:, kc, hd * 384 + c0:hd * 384 + 384],
                                              kc == 0, kc == KC - 1), reads=[bW, bH], writes=[bps[b0 + hd]])

                    def tile_gen(j):
                        t = blk * 4 + j
                        b0 = 2 * (j % 2)
                        bb = [bps[b0], bps[b0 + 1]]
                        sl = t % 2
                        P.op("act", lambda e, t=t, b0=b0: e.copy(out=Vh[:, :, t, :], in_=ps[:, b0:b0 + 2, 256:384]), reads=bb, writes=[bVh], relaxed=True)
                        pq = ps[:, b0:b0 + 2, 0:256].rearrange("p h (g d) -> p h g d", g=4)
                        sq4 = sqb[sl].rearrange("p (h g) d -> p h g d", h=2)
                        t14 = t1[sl].rearrange("p (h g) d -> p h g d", h=2)
                        P.op("act", ACTF(sq4, pq, AF.Square), reads=bb, writes=[bsqb[sl]])
                        yield
                        P.op("dve", lambda e, sl=sl: e.tensor_reduce(out=rs[sl], in_=sqb[sl], axis=AX.X, op=ALU.add), reads=[bsqb[sl]], writes=[brs[sl]])
                        yield
                        P.op("act", ACTF(rs[sl], rs[sl], AF.Ln, scale=1.0 / 64.0, bias=EPS), reads=[brs[sl]], writes=[brs[sl]])
                        P.op("act", ACTF(rs[sl], rs[sl], AF.Exp, scale=-0.5), reads=[brs[sl]], writes=[brs[sl]])
                        yield
                        P.op("dve", TT(t14, pq, rs[sl].rearrange("p (h g) -> p h g", h=2)[:, :, :, None].to_broadcast([128, 2, 4, 64]), ALU.mult),
                             reads=bb + [brs[sl]], writes=[bt1[sl]])
                        yield
                        P.op("dve", TT(t1[sl], t1[sl], wqk, ALU.mult), reads=[bt1[sl], bC], writes=[bt1[sl]])
                        yield
                        P.op("act", lambda e, sl=sl: e.copy(out=qkb[sl], in_=t1[sl]), reads=[bt1[sl]], writes=[bqkb[sl]])
                        cosb = cosT[:, t:t + 1, :].to_broadcast([128, 8, 8])
                        sinb = sinT[:, t:t + 1, :].to_broadcast([128, 8, 8])
                        x1 = t1[sl][:, :, 0:8]
                        x2 = t1[sl][:, :, 8:16]
                        R_ = rt[sl]
                        P.op("dve", TT(R_[:, 0], x1, cosb, ALU.mult), reads=[bt1[sl], bK], writes=[brt[sl]])
                        P.op("dve", TT(R_[:, 1], x2, sinb, ALU.mult), reads=[bt1[sl], bK], writes=[brt[sl]], relaxed=True)
                        yield
                        P.op("dve", TT(R_[:, 2], x2, cosb, ALU.mult), reads=[bt1[sl], bK], writes=[brt[sl]], relaxed=True)
                        P.op("dve", TT(R_[:, 3], x1, sinb, ALU.mult), reads=[bt1[sl], bK], writes=[brt[sl]], relaxed=True)
                        yield
                        P.op("dve", TT(qkb[sl][:, :, 0:8], R_[:, 0], R_[:, 1], ALU.subtract), reads=[brt[sl]], writes=[bqkb[sl]])
                        P.op("dve", TT(qkb[sl][:, :, 8:16], R_[:, 2], R_[:, 3], ALU.add), reads=[brt[sl]], writes=[bqkb[sl]], relaxed=True)
                        yield
                        qk2 = qkb[sl].rearrange("p g d -> p (g d)")
                        for hd in range(2):
                            P.op("pe", TR(psbf(4)[:, 2 * hd, :], qk2[:, hd * 256 + 128:hd * 256 + 256], identb), reads=[bqkb[sl], bK], writes=[bps[4]])
                            if cur:
                                P.op("pe", TR(psbf(4)[:, 2 * hd + 1, :], qk2[:, hd * 256:hd * 256 + 128], identb), reads=[bqkb[sl], bK], writes=[bps[4]])
                        yield
                        for hd in range(2):
                            P.op("act", lambda e, hd=hd, t=t: e.copy(out=KT[hd][:, t * 128:(t + 1) * 128], in_=psbf(4)[:, 2 * hd, :]),
                                 reads=[bps[4]], writes=[bKT[hd]], relaxed=True)
                            if cur:
                                P.op("act", lambda e, hd=hd, j=j: e.copy(out=QT[hd][:, j * 128:(j + 1) * 128], in_=psbf(4)[:, 2 * hd + 1, :]),
                                     reads=[bps[4]], writes=[bQT[hd]], relaxed=True)
                        yield

                    proj_mm(0)
                    act_g = []
                    for j in range(4):
                        if j + 1 < 4:
                            proj_mm(j + 1)
                        act_g.append(tile_gen(j))
                        if len(act_g) == 2:
                            done0 = False
                            nstep = 0
                            while not done0:
                                for g in list(act_g):
                                    try:
                                        next(g)
                                    except StopIteration:
                                        if g is act_g[0]:
                                            done0 = True
                                        act_g.remove(g)
                                nstep += 1
                        elif j == 0:
                            for _ in range(4):
                                next(act_g[0])
                    for g in act_g:
                        for _ in g:
                            pass
                    if not cur:
                        continue
                    cb4 = blk - 4
                    ndone = 16 + 4 * cb4
                    for hd in range(2):
                        steps = [(c, 0, None) for c in range(ndone)] + [(ndone + jd, jd * 128, jd) for jd in (3, 2, 1, 0)]
                        nst = len(steps)

                        def emit_S(n):
                            c, q0, jd = steps[n]
                            sb0 = 4 + 2 * (n % 2)
                            for m in range(2):
                                P.op("pe", MM(ps[:, sb0 + m, q0:512], KT[hd][m * 64:(m + 1) * 64, c * 128:(c + 1) * 128], QT[hd][m * 64:(m + 1) * 64, q0:512]),
                                     reads=[bKT[hd], bQT[hd]], writes=[bps[sb0 + m]])

                        def emit_E(n):
                            c, q0, jd = steps[n]
                            sb0 = 4 + 2 * (n % 2)
                            pt = n % 3
                            P.op("act", ACTF(PT[pt][:, :, q0:512], ps[:, sb0:sb0 + 2, q0:512], AF.Exp, scale=0.125, bias=(pbias[:, 0:1] if c < 16 else None)),
                                 reads=[bps[sb0], bps[sb0 + 1], bC], writes=[bPT[pt]])
                            if jd is not None:
                                P.op("dve", MSET(PT[pt][64:128, :, q0:q0 + 64], 0.0), reads=[bPT[pt]], writes=[bPT[pt]])

                        def emit_PV(n):
                            c, q0, jd = steps[n]
                            pt = n % 3
                            for m in range(2):
                                P.op("pe", MM(ps[:, m, q0:512], Vh[:, hd, c, :], PT[pt][:, m, q0:512], c == 0, jd == 0),
                                     reads=[bVh, bPT[pt]], writes=[bps[m]])
                            if c == 0:
                                P.op("dve", CP(Pacc[0], PT[pt][:, 0, :]), reads=[bPT[pt]], writes=[bPacc[0]])
                            else:
                                P.op("dve", TT(Pacc[0][:, q0:512], Pacc[0][:, q0:512], PT[pt][:, 0, q0:512], ALU.add), reads=[bPT[pt], bPacc[0]], writes=[bPacc[0]])
                            P.op("pe", MM(ps[:, 3, q0:512], onesb, PT[pt][:, 1, q0:512], c == 0, jd == 0), reads=[bPT[pt], bK], writes=[bps[3]])

                        emit_S(0)
                        emit_S(1)
                        for n in range(nst):
                            emit_E(n)
                            if n + 2 < nst:
                                emit_S(n + 2)
                            emit_PV(n)
                            if fin2q and n >= 2:
                                fin2q.pop(0)()
                        while fin2q:
                            fin2q.pop(0)()
                        P.op("pe", MM(ps[:, 2, :], onesf, Pacc[0]), reads=[bPacc[0], bK], writes=[bps[2]])
                        P.op("act", ACTF(fz[0], ps[:, 2, :], AF.Ln), reads=[bps[2]], writes=[bfz[0]])
                        P.op("act", ACTF(fz[2], ps[:, 3, :], AF.Ln), reads=[bps[3]], writes=[bfz[2]])
                        P.op("act", ACTF(fz[0], fz[0], AF.Exp, scale=-1.0), reads=[bfz[0]], writes=[bfz[0]])
                        P.op("act", ACTF(fz[2], fz[2], AF.Exp, scale=-1.0), reads=[bfz[2]], writes=[bfz[2]])
                        P.op("dve", TT(fz[1], ps[:, 0, :], fz[0], ALU.mult), reads=[bps[0], bfz[0]], writes=[bfz[1]])
                        P.op("dve", TT(fz[3], ps[:, 1, :], fz[2], ALU.mult), reads=[bps[1], bfz[2]], writes=[bfz[3]])

                        def f2a():
                            P.op("dve", STT(fz[1], fz[3], neglam[:, 0:1], fz[1], ALU.mult, ALU.add), reads=[bfz[3], bfz[1], bK], writes=[bfz[1]])

                        def f2b():
                            P.op("act", ACTF(fz[4], fz[1], AF.Square), reads=[bfz[1]], writes=[bfz[4]])

                        def f2c():
                            P.op("pe", MM(ps[:, 2, :], onesf, fz[4]), reads=[bfz[4], bK], writes=[bps[2]])

                        def f2d():
                            P.op("act", ACTF(fz[0], ps[:, 2, :], AF.Ln, scale=1.0 / 128.0, bias=EPS), reads=[bps[2]], writes=[bfz[0]])

                        def f2e():
                            P.op("act", ACTF(fz[0], fz[0], AF.Exp, scale=-0.5), reads=[bfz[0]], writes=[bfz[0]])

                        def f2f(hd=hd, cb4=cb4):
                            mi = mo_cnt[0] % 2
                            mo_cnt[0] += 1
                            P.op("dve", STT(mob[mi][:, 0, :], fz[1], subw_s[:, 0:1], fz[0], ALU.mult, ALU.mult), reads=[bfz[1], bfz[0], bK], writes=[bmoa[mi]])
                            ch = 8 + 2 * hp + hd
                            P.dma("sp", ms_d[ch, :, cb4 * 512:(cb4 + 1) * 512], mob[mi][:, 0, :], reads=[bmoa[mi]], writes=[bms])

                        fin2q.extend([f2a, f2b, f2c, f2d, f2e, f2f])
                while fin2q:
                    fin2q.pop(0)()
        for _ in g_ada:
            pass
        P.dma("sp", modT2, modd[:, 32 * 128:96 * 128].rearrange("o (j p) -> (o j) p", p=128), reads=[bmd], writes=[bmT2])
        P.op("pe", TR(ps[:, 2, 0:64], modT2, identf[0:64, 0:64]), reads=[bmT2, bC], writes=[bps[2]])
        P.op("dve", TT(mod[:, 32:96], ps[:, 2, 0:64], cs("b_l")[:, 32:96], ALU.add), reads=[bps[2], bC], writes=[bK])
        P.op("dve", STT(A2, mod[:, 64:80], 1.0, cs("n2w"), ALU.add, ALU.mult), reads=[bK, bC], writes=[bK])
        P.barrier()
    A.off = base_off

    bout = P.buf("out")
    if "C" in phases:
        h2T = A.alloc([KC, 1024], BF16)
        bh2a, bh2d = P.buf("h2a"), P.buf("h2d")
        wgu = [A.alloc([KC, 256], BF16) for _ in range(2)]
        bwgu = [P.buf(f"wgu{i}") for i in range(2)]
        wdb = [A.alloc([NFF, 128], BF16) for _ in range(2)]
        bwd = [P.buf(f"wd{i}") for i in range(2)]
        tmpf = [A.alloc([2, 512], F32) for _ in range(2)]
        btmp = [P.buf(f"tmp{i}") for i in range(2)]
        un0 = A.off
        mxb = [A.alloc([KC, 512], BF16) for _ in range(2)]
        bmx = [P.buf(f"mx{i}") for i in range(2)]
        xsb = A.alloc([8, D], F32)
        bxsb = P.buf("xsb")
        xn = [A.alloc([D], BF16) for _ in range(2)]
        bxn = [P.buf(f"xn{i}") for i in range(2)]
        junk = A.alloc([D], BF16)
        bjunk = P.buf("junk")
        un1 = A.off
        A.off = un0
        actT = A.alloc([NFF, 1024], BF16)
        bact = P.buf("actT")
        xs = [A.alloc([8, 128], F32) for _ in range(2)]
        bxs = [P.buf(f"xs{i}") for i in range(2)]
        sg = [A.alloc([512], F32) for _ in range(2)]
        bsg = [P.buf(f"sg{i}") for i in range(2)]
        A.off = max(A.off, un1)
        wob = [wgu[i][:, :, 0:128] for i in range(2)]
        out_v = out_d.rearrange("(t p) d -> p t d", p=128)
        tcount = [0]
        wcount = [0]
        for TB in range(2):
            for sbk in range(2):
                g512 = TB * 2 + sbk
                P.dma("sp", mxb[sbk], ms_d[:, :, g512 * 512:(g512 + 1) * 512].rearrange("c p n -> p c n"), reads=[bms], writes=[bmx[sbk]])
                P.dma("sp", xsb[:, 4 * sbk:4 * sbk + 4, :], xv[:, 16 + 4 * g512:16 + 4 * g512 + 4, :], writes=[bxsb])
            pend = None
            for m in range(16):
                wi = wcount[0] % 2
                wcount[0] += 1
                P.dma("pool", wob[wi], wo_d[m], writes=[bwgu[wi]])
                pb = 2 * (m % 2)
                for sbk in range(2):
                    for kc in range(KC):
                        P.op("pe", MM(ps[:, pb + sbk, :], wob[wi][:, kc, :], mxb[sbk][:, kc, :], kc == 0, kc == KC - 1),
                             reads=[bwgu[wi], bmx[sbk]], writes=[bps[pb + sbk]])
                ti = tcount[0] % 2
                tcount[0] += 1
                P.op("act", ACTF(tmpf[ti], ps[:, pb:pb + 2, :], AF.Copy, scale=gate1[:, m:m + 1]), reads=[bps[pb], bps[pb + 1], bK], writes=[btmp[ti]])
                if pend is not None:
                    pend()

                def pend(m=m, ti=ti):
                    tb0 = 4 + 2 * ti
                    for j in range(8):
                        P.op("pe", TR(ps[:, tb0 + j // 4, (j % 4) * 128:(j % 4 + 1) * 128], tmpf[ti][:, j // 4, (j % 4) * 128:(j % 4 + 1) * 128], identf),
                             reads=[btmp[ti], bC], writes=[bps[tb0 + j // 4]])
                    xslice = xsb[:, :, m * 128:(m + 1) * 128]
                    P.op("dve", TT(xslice, ps[:, tb0:tb0 + 2, :].rearrange("p b (j f) -> p (b j) f", j=4), xslice, ALU.add),
                         reads=[bps[tb0], bps[tb0 + 1], bxsb], writes=[bxsb])
            pend()
            ntiles = []
            for j in range(8):
                col = j * 128
                ntiles.append((xsb[:, j, :], bxsb, A2, Sh2, (lambda kc, col=col: h2T[:, kc, col:col + 128]), bh2a, bh2d))
            norm_pipeline(ntiles, xn, bxn, junk, bjunk)
            P.dma("sp", out_v[:, 8 * TB:8 * TB + 8, :], xsb, reads=[bxsb], writes=[bout])
            P.barrier()
            for jf in range(NFF):
                wi = wcount[0] % 2
                wcount[0] += 1
                P.dma("pool", wgu[wi], wgu_d[jf], writes=[bwgu[wi]])
                for sb in range(2):
                    for gu in range(2):
                        ob = 2 * (sb % 2) + gu + (4 if jf % 2 else 0)
                        for kc in range(KC):
                            P.op("pe", MM(ps[:, ob, :], wgu[wi][:, kc, gu * 128:(gu + 1) * 128], h2T[:, kc, sb * 512:(sb + 1) * 512], kc == 0, kc == KC - 1),
                                 reads=[bwgu[wi], bh2a, bh2d], writes=[bps[ob]])
                    ob = 2 * (sb % 2) + (4 if jf % 2 else 0)
                    si = tcount[0] % 2
                    tcount[0] += 1
                    P.op("act", ACTF(sg[si], ps[:, ob, :], AF.Silu), reads=[bps[ob]], writes=[bsg[si]])
                    P.op("dve", TT(actT[:, jf, sb * 512:(sb + 1) * 512], sg[si], ps[:, ob + 1, :], ALU.mult), reads=[bsg[si], bps[ob + 1]], writes=[bact], relaxed=True)
            pend = None
            for m in range(16):
                wi = m % 2
                P.dma("pool", wdb[wi], wd_d[m], writes=[bwd[wi]])
                xi = m % 2
                P.dma("sp", xs[xi], out_v[:, 8 * TB:8 * TB + 8, m * 128:(m + 1) * 128], reads=[bout], writes=[bxs[xi]], sem_buf=bxs[xi])
                for sb in range(2):
                    ob = sb
                    for fc in range(NFF):
                        P.op("pe", MM(ps[:, ob, :], wdb[wi][:, fc, :], actT[:, fc, sb * 512:(sb + 1) * 512], fc == 0, fc == NFF - 1),
                             reads=[bwd[wi], bact], writes=[bps[ob]])
                    ti = tcount[0] % 2
                    tcount[0] += 1
                    P.op("act", ACTF(tmpf[ti][:, 0, :], ps[:, ob, :], AF.Copy, scale=gate2[:, m:m + 1]), reads=[bps[ob], bK], writes=[btmp[ti]])
                    if pend is not None:
                        pend()

                    def pend(m=m, sb=sb, ti=ti, xi=xi):
                        tb = 2 + ti
                        for j in range(4):
                            P.op("pe", TR(ps[:, tb, j * 128:(j + 1) * 128], tmpf[ti][:, 0, j * 128:(j + 1) * 128], identf), reads=[btmp[ti], bC], writes=[bps[tb]])
                        xslice = xs[xi][:, sb * 4:(sb + 1) * 4, :]
                        P.op("dve", TT(xslice, ps[:, tb, :].rearrange("p (j f) -> p j f", j=4), xslice, ALU.add), reads=[bps[tb], bxs[xi]], writes=[bxs[xi]])
                        if sb == 1:
                            bo2 = P.buf(f"o2_{TB}_{m}")
                            P.dma("sp", out_v[:, 8 * TB:8 * TB + 8, m * 128:(m + 1) * 128], xs[xi], reads=[bxs[xi]], writes=[bout, bo2], sem_buf=bo2)
            pend()
            P.barrier()
    P.barrier()
    P.emit()
    return nc, P


def _prep_shared(inp):
    f = np.float32
    w_in = np.asarray(inp["w_in"], f)[0]
    cols = []
    for hh in range(4):
        cols += list(range(hh * 128, hh * 128 + 128)) + list(range(512 + hh * 128, 512 + hh * 128 + 128))
        cols += list(range(1024 + hh * 256, 1024 + hh * 256 + 256)) + list(range(2048 + hh * 256, 2048 + hh * 256 + 256))
        cols += [3072 + hh, 3076 + hh]
    for hp in range(4):
        for hd in range(2):
            h = 2 * hp + hd
            cols += list(range(3080 + h * 128, 3080 + h * 128 + 128)) + list(range(4104 + h * 128, 4104 + h * 128 + 128))
            cols += list(range(5128 + h * 128, 5128 + h * 128 + 128))
    cols = np.asarray(cols)
    sh = {}
    sh["w_in_l"] = np.ascontiguousarray(w_in.reshape(KC, 128, N_IN).transpose(1, 0, 2)[:, :, cols])
    sh["w_ada_l"] = np.ascontiguousarray(np.asarray(inp["w_ada"], f)[0].reshape(KC, 128, 24, 512).transpose(2, 1, 0, 3))
    sh["wo_l"] = np.ascontiguousarray(np.asarray(inp["w_out"], f)[0].reshape(KC, 128, 16, 128).transpose(2, 1, 0, 3))
    wgu = np.asarray(inp["w_gate_up"], f)[0].reshape(KC, 128, 2, NFF, 128)
    sh["wgu_l"] = np.ascontiguousarray(wgu.transpose(3, 1, 0, 2, 4).reshape(NFF, 128, KC, 256))
    sh["wd_l"] = np.ascontiguousarray(np.asarray(inp["w_down"], f)[0].reshape(NFF, 128, 16, 128).transpose(2, 1, 0, 3))
    return sh


def _prep_cst(inp, b, half):
    f = np.float32
    c = np.zeros((128, NCST), f)

    def put(name, arr):
        o, n = CO[name]
        c[:, o:o + n] = np.asarray(arr, f).reshape(128, n)

    rep = lambda v: np.broadcast_to(np.asarray(v, f).reshape(1, -1), (128, np.asarray(v).size))
    put("c_l", np.asarray(inp["c"], f)[b].reshape(KC, 128).T)
    put("b_l", np.asarray(inp["b_ada"], f)[0].reshape(96, 128).T)
    put("n1w", np.asarray(inp["norm1_w"], f)[0].reshape(KC, 128).T)
    put("n2w", np.asarray(inp["norm2_w"], f)[0].reshape(KC, 128).T)
    cw = np.asarray(inp["mlstm_conv_w"], f)[0].reshape(4, 2, 4, 128)
    put("cw", cw.transpose(3, 2, 1, 0).reshape(128, 32))
    cb = np.asarray(inp["mlstm_conv_b"], f)[0].reshape(2, 4, 128)
    put("cb", cb.transpose(2, 1, 0).reshape(128, 8))
    gb = np.asarray(inp["mlstm_gate_b"], f)[0].reshape(2, 4)
    put("gb", rep(gb.T.reshape(-1)))
    put("nw", rep(np.asarray(inp["mlstm_norm_w"], f)[0].reshape(-1)))
    qw = np.asarray(inp["q_norm_w"], f)[0]
    kw = np.asarray(inp["k_norm_w"], f)[0]
    put("wqk", rep(np.concatenate([qw, qw, kw, kw, qw, qw, kw, kw])))
    put("lamv", rep(np.concatenate([np.asarray(inp[k], f)[0] for k in ("lambda_q1", "lambda_k1", "lambda_q2", "lambda_k2")])))
    put("subw", np.asarray(inp["subln_w"], f)[0].reshape(128, 1))
    put("flag", np.full((128, 1), float(half), f))
    put("pbias", np.full((128, 1), 0.0 if half else -30000.0, f))
    invf = (np.float32(500000.0) ** (-np.arange(0, 16, 2, dtype=np.float32) / np.float32(16))).astype(f)
    put("invf", rep(invf))
    put("ident", np.eye(128, dtype=f))
    put("tri", np.triu(np.ones((128, 128), f)))
    return c


def _in_maps(inp):
    sh = _prep_shared(inp)
    x = np.asarray(inp["x"], np.float32)
    pos = np.asarray(inp["positions"], np.int32)
    maps = []
    for core in range(8):
        b, half = core // 2, core % 2
        m = dict(sh)
        m["x"] = np.ascontiguousarray(np.concatenate([x[b, 0:TOK], x[b, half * TOK:(half + 1) * TOK]], axis=0))
        pp = np.concatenate([pos[b, 0:TOK], pos[b, half * TOK:(half + 1) * TOK]])
        m["pos"] = np.ascontiguousarray(pp.reshape(NTILE, 128).T)
        m["cst"] = _prep_cst(inp, b, half)
        maps.append(m)
    return maps


_NC = None


def kernel(**inputs):
    global _NC
    if _NC is None:
        _NC = build()[0]
    maps = _in_maps(inputs)
    res = run_bass_kernel_spmd(_NC, maps, core_ids=list(range(8)))
    out = np.empty((4, 2 * TOK, D), np.float32)
    for core in range(8):
        b, half = core // 2, core % 2
        out[b, half * TOK:(half + 1) * TOK] = res.results[core]["out"]
    return out
```
